# Optimizing a Trainium2 kernel written in Bass

```python
import math
import jax, jax.numpy as jnp
from jax import lax
import numpy as np

D_MODEL = 1024
BATCH = 8
SEQ = 4096
DEPTH = 2
DEC_BATCH = 1
DEC_SEQ = 16384
PAST_LEN = 128

GRID_W = 64
N_HEADS = 16
HEAD_DIM = D_MODEL // N_HEADS
WIN_H_MAX = 8
WIN_W = 16
Q_BLOCK_W = 16
K_BLOCK_W = 32
S5_GROUP = 16
S5_GROUPS = D_MODEL // S5_GROUP
S5_STATE = 64
DT_MIN = 1e-3
DT_MAX = 1e-1
D_FF = 4 * D_MODEL
N_MIXERS = 2
N_ATTN = (DEPTH + 1) // 2
N_SSM = DEPTH // 2
EPS = 1e-6
NEG_INF = -1e30

kernel_name = 'hybrid_natten_s5_encoder'


def rms_norm(x, g):
    xf = x.astype(jnp.float32)
    y = xf * lax.rsqrt(jnp.mean(xf * xf, axis=-1, keepdims=True) + EPS)
    return (y * g.astype(jnp.float32)).astype(x.dtype)


def neighbourhood_attention(h, w_qkv, w_o, rpb):
    bsz, L, _ = h.shape
    rows = L // GRID_W
    kh = min(WIN_H_MAX, rows)
    qkv = (h @ w_qkv).reshape(bsz, rows, GRID_W, 3, N_HEADS, HEAD_DIM)
    q = qkv[:, :, :, 0] * (HEAD_DIM ** -0.5)
    k = qkv[:, :, :, 1]
    v = qkv[:, :, :, 2]
    r = np.arange(rows)
    row_start = np.clip(r - kh // 2, 0, rows - kh)
    row_idx = row_start[:, None] + np.arange(kh)[None, :]
    dr = row_idx - r[:, None] + (WIN_H_MAX - 1)
    outs = []
    for j in range(GRID_W // Q_BLOCK_W):
        qc = j * Q_BLOCK_W + np.arange(Q_BLOCK_W)
        col_start = np.clip(qc - WIN_W // 2, 0, GRID_W - WIN_W)
        kb = int(np.clip(j * Q_BLOCK_W - WIN_W // 2, 0, GRID_W - K_BLOCK_W))
        kc = kb + np.arange(K_BLOCK_W)
        valid = (kc[None, :] >= col_start[:, None]) & (kc[None, :] < col_start[:, None] + WIN_W)
        dc = np.clip(kc[None, :] - qc[:, None], -(WIN_W - 1), WIN_W - 1) + (WIN_W - 1)
        q_blk = q[:, :, j * Q_BLOCK_W:(j + 1) * Q_BLOCK_W]
        k_blk = jnp.take(k[:, :, kb:kb + K_BLOCK_W], row_idx, axis=1)
        v_blk = jnp.take(v[:, :, kb:kb + K_BLOCK_W], row_idx, axis=1)
        s = jnp.einsum('brqhd,brkwhd->bhrqkw', q_blk, k_blk).astype(jnp.float32)
        bias = rpb[:, dr[:, None, :, None], dc[None, :, None, :]]
        s = s + bias[None].astype(jnp.float32)
        s = jnp.where(valid[:, None, :], s, NEG_INF)
        p = jax.nn.softmax(s.reshape(s.shape[:4] + (kh * K_BLOCK_W,)), axis=-1)
        p = p.reshape(s.shape).astype(v.dtype)
        outs.append(jnp.einsum('bhrqkw,brkwhd->brqhd', p, v_blk))
    o = jnp.concatenate(outs, axis=2).reshape(bsz, L, D_MODEL)
    return o @ w_o


def _complex_affine_combine(left, right):
    ar1, ai1, br1, bi1 = left
    ar2, ai2, br2, bi2 = right
    ar = ar2 * ar1 - ai2 * ai1
    ai = ar2 * ai1 + ai2 * ar1
    br = ar2 * br1 - ai2 * bi1 + br2
    bi = ar2 * bi1 + ai2 * br1 + bi2
    return ar, ai, br, bi


def s5_direction(u_g, a_re, a_im, log_dt, b_re, b_im, c_re, c_im, reverse):
    f32 = jnp.float32
    a_re = a_re.astype(f32)
    a_im = a_im.astype(f32)
    b_re = b_re.astype(f32)
    b_im = b_im.astype(f32)
    c_re = c_re.astype(f32)
    c_im = c_im.astype(f32)
    dt = jnp.exp(log_dt.astype(f32))[:, None]
    mag = jnp.exp(a_re * dt)
    lam_re = mag * jnp.cos(a_im * dt)
    lam_im = mag * jnp.sin(a_im * dt)
    den = a_re * a_re + a_im * a_im
    nr = lam_re - 1.0
    ni = lam_im
    zr = (nr * a_re + ni * a_im) / den
    zi = (ni * a_re - nr * a_im) / den
    bbar_re = zr[..., None] * b_re - zi[..., None] * b_im
    bbar_im = zr[..., None] * b_im + zi[..., None] * b_re
    bu_re = jnp.einsum('blgi,gpi->blgp', u_g, bbar_re)
    bu_im = jnp.einsum('blgi,gpi->blgp', u_g, bbar_im)
    lr = jnp.broadcast_to(lam_re, bu_re.shape)
    li = jnp.broadcast_to(lam_im, bu_im.shape)
    _, _, h_re, h_im = lax.associative_scan(_complex_affine_combine, (lr, li, bu_re, bu_im), reverse=reverse, axis=1)
    return jnp.einsum('blgp,gip->blgi', h_re, c_re) - jnp.einsum('blgp,gip->blgi', h_im, c_im)


def s5_mixer(h, a_re, a_im, log_dt, b_re, b_im, c_re, c_im, d_skip, w_glu):
    bsz, L, _ = h.shape
    u = h.astype(jnp.float32)
    u_g = u.reshape(bsz, L, S5_GROUPS, S5_GROUP)
    y = d_skip.astype(jnp.float32) * u
    for direction in range(2):
        y_dir = s5_direction(u_g, a_re[direction], a_im[direction], log_dt[direction],
                             b_re[direction], b_im[direction], c_re[direction], c_im[direction],
                             reverse=(direction == 1))
        y = y + y_dir.reshape(bsz, L, D_MODEL)
    g = jax.nn.gelu(y).astype(h.dtype)
    val, gate = jnp.split(g @ w_glu, 2, axis=-1)
    return val * jax.nn.sigmoid(gate)


def run_trunk(x, c, ada_w, ada_b, norm_gain, attn_w_qkv, attn_w_o, attn_rpb,
              s5_a_re, s5_a_im, s5_log_dt, s5_b_re, s5_b_im, s5_c_re, s5_c_im,
              s5_d, s5_w_glu, ffn_w1, ffn_w2):
    c_act = jax.nn.silu(c)
    for i in range(DEPTH):
        mod = c_act @ ada_w[i] + ada_b[i]
        sh_m, sc_m, g_m, sh_f, sc_f, g_f = [m[:, None, :] for m in jnp.split(mod, 6, axis=-1)]
        h = rms_norm(x, norm_gain[i, 0]) * (1.0 + sc_m) + sh_m
        j = i // N_MIXERS
        if i % N_MIXERS == 0:
            y = neighbourhood_attention(h, attn_w_qkv[j], attn_w_o[j], attn_rpb[j])
        else:
            y = s5_mixer(h, s5_a_re[j], s5_a_im[j], s5_log_dt[j], s5_b_re[j], s5_b_im[j],
                         s5_c_re[j], s5_c_im[j], s5_d[j], s5_w_glu[j])
        x = x + g_m * rms_norm(y, norm_gain[i, 1])
        h = rms_norm(x, norm_gain[i, 2]) * (1.0 + sc_f) + sh_f
        y = jnp.square(jax.nn.relu(h @ ffn_w1[i])) @ ffn_w2[i]
        x = x + g_f * rms_norm(y, norm_gain[i, 3])
    return x


def setup_inputs(seed: int = 0) -> dict:
    key = jax.random.key(seed)
    ks = jax.random.split(key, 24)
    f32 = jnp.float32
    D = D_MODEL
    n_idx = jnp.arange(S5_STATE, dtype=f32)
    return {
        'x_prompt': jax.random.normal(ks[0], (BATCH, SEQ, D), f32),
        'x_sample': jax.random.normal(ks[1], (DEC_BATCH, DEC_SEQ, D), f32),
        'c_prompt': jax.random.normal(ks[2], (BATCH, D), f32),
        'c_sample': jax.random.normal(ks[3], (DEC_BATCH, D), f32),
        'ada_w': jax.random.normal(ks[4], (DEPTH, D, 6 * D), f32) * (0.5 * D ** -0.5),
        'ada_b': jax.random.normal(ks[5], (DEPTH, 6 * D), f32) * 0.02,
        'norm_gain': 1.0 + 0.05 * jax.random.normal(ks[6], (DEPTH, 4, D), f32),
        'attn_w_qkv': jax.random.normal(ks[7], (N_ATTN, D, 3 * D), f32) * D ** -0.5,
        'attn_w_o': jax.random.normal(ks[8], (N_ATTN, D, D), f32) * D ** -0.5,
        'attn_rpb': jax.random.normal(ks[9], (N_ATTN, N_HEADS, 2 * WIN_H_MAX - 1, 2 * WIN_W - 1), f32) * 0.02,
        's5_a_re': -0.5 + 0.01 * jax.random.normal(ks[10], (N_SSM, 2, S5_GROUPS, S5_STATE), f32),
        's5_a_im': math.pi * n_idx + 0.01 * jax.random.normal(ks[11], (N_SSM, 2, S5_GROUPS, S5_STATE), f32),
        's5_log_dt': jax.random.uniform(ks[12], (N_SSM, 2, S5_GROUPS), f32, math.log(DT_MIN), math.log(DT_MAX)),
        's5_b_re': jax.random.normal(ks[13], (N_SSM, 2, S5_GROUPS, S5_STATE, S5_GROUP), f32) * (2 * S5_GROUP) ** -0.5,
        's5_b_im': jax.random.normal(ks[14], (N_SSM, 2, S5_GROUPS, S5_STATE, S5_GROUP), f32) * (2 * S5_GROUP) ** -0.5,
        's5_c_re': jax.random.normal(ks[15], (N_SSM, 2, S5_GROUPS, S5_GROUP, S5_STATE), f32) * (2 * S5_STATE) ** -0.5,
        's5_c_im': jax.random.normal(ks[16], (N_SSM, 2, S5_GROUPS, S5_GROUP, S5_STATE), f32) * (2 * S5_STATE) ** -0.5,
        's5_d': jax.random.normal(ks[17], (N_SSM, D), f32),
        's5_w_glu': jax.random.normal(ks[18], (N_SSM, D, 2 * D), f32) * D ** -0.5,
        'ffn_w1': jax.random.normal(ks[19], (DEPTH, D, D_FF), f32) * D ** -0.5,
        'ffn_w2': jax.random.normal(ks[20], (DEPTH, D_FF, D), f32) * D_FF ** -0.5,
    }


def reference(x_prompt, x_sample, c_prompt, c_sample, ada_w, ada_b, norm_gain,
              attn_w_qkv, attn_w_o, attn_rpb, s5_a_re, s5_a_im, s5_log_dt,
              s5_b_re, s5_b_im, s5_c_re, s5_c_im, s5_d, s5_w_glu, ffn_w1, ffn_w2):
    y_prompt = run_trunk(x_prompt, c_prompt, ada_w, ada_b, norm_gain, attn_w_qkv, attn_w_o, attn_rpb,
                         s5_a_re, s5_a_im, s5_log_dt, s5_b_re, s5_b_im, s5_c_re, s5_c_im,
                         s5_d, s5_w_glu, ffn_w1, ffn_w2)
    y_sample = run_trunk(x_sample, c_sample, ada_w, ada_b, norm_gain, attn_w_qkv, attn_w_o, attn_rpb,
                         s5_a_re, s5_a_im, s5_log_dt, s5_b_re, s5_b_im, s5_c_re, s5_c_im,
                         s5_d, s5_w_glu, ffn_w1, ffn_w2)
    return (y_prompt, y_sample)
```

```python
import numpy as np
from contextlib import ExitStack
import concourse.bass as bass
import concourse.mybir as mybir
from concourse.bass_utils import run_bass_kernel_spmd

F32, BF = mybir.dt.float32, mybir.dt.bfloat16
AF = mybir.ActivationFunctionType
ALU = mybir.AluOpType
D = 1024
KC = 8
W = 64
NH = 16
DFF = 4096
EPS = 1e-6
ENG = ("sp", "act", "dve", "pool", "pe")
NDS = 24
SB_BASE = 16640
SB_LIMIT = 229376


class T:
    def __init__(s, name):
        s.name = name
        s.w = None
        s.r = []


class Builder:
    def __init__(s, nc):
        s.nc = nc
        s.prog = {e: [] for e in ENG}
        s.cnt = {}
        s.waited = {e: {} for e in ENG}
        s.ndma = 0
        s.sb = SB_BASE
        s.nalloc = 0

    def alloc(s, shape, dt, name=None):
        nb = int(np.prod(shape[1:])) * (4 if dt == F32 else 2)
        nb = (nb + 63) // 64 * 64
        assert s.sb + nb <= SB_LIMIT, ("SBUF overflow", name, s.sb, nb)
        s.nalloc += 1
        t = s.nc.alloc_sbuf_tensor_at("%s_%d" % (name or "t", s.nalloc), list(shape), dt, offset=s.sb)
        s.sb += nb
        return t

    def _deps(s, eng, reads, writes):
        deps = []
        for t in reads:
            if t.w:
                deps.append((t.w, True))
        for t in writes:
            if t.w:
                deps.append((t.w, True))
            deps.extend((r, False) for r in t.r)
        out = {}
        for (k, v), isw in deps:
            if k == eng and (eng == "pe" or not isw):
                continue
            if s.waited[eng].get(k, 0) >= v:
                continue
            out[k] = max(out.get(k, 0), v)
        for k, v in out.items():
            s.waited[eng][k] = v
        return list(out.items())

    def _finish(s, tok, reads, writes):
        for t in reads:
            t.r.append(tok)
        for t in writes:
            t.w = tok
            t.r = []

    def op(s, eng, fn, reads=(), writes=()):
        waits = s._deps(eng, reads, writes)
        s.cnt[eng] = s.cnt.get(eng, 0) + 1
        tok = (eng, s.cnt[eng])
        s.prog[eng].append((waits, fn, (eng, 1)))
        s._finish(tok, reads, writes)
        return tok

    def dma(s, out, in_, reads=(), writes=(), q="sp", slow=False):
        k = "d%d" % (s.ndma % NDS)
        s.ndma += 1
        waits = s._deps(q, reads, writes)
        prev = s.cnt.get(k, 0)
        if prev and s.waited[q].get(k, 0) < prev:
            waits.append((k, prev))
            s.waited[q][k] = prev
        s.cnt[k] = prev + 16
        tok = (k, s.cnt[k])
        if slow:
            fn = lambda e: e.dma_start(out=out, in_=in_, allow_slow_non_contiguous=True)
        else:
            fn = lambda e: e.dma_start(out=out, in_=in_)
        s.prog[q].append((waits, fn, (k, 16)))
        s._finish(tok, reads, writes)
        return tok

    def barrier(s):
        for e in ENG:
            waits = []
            for k, v in s.cnt.items():
                if k == e and e == "pe":
                    continue
                if s.waited[e].get(k, 0) < v:
                    waits.append((k, v))
                    s.waited[e][k] = v
            if waits:
                s.prog[e].append((waits, None, None))

    def emit(s):
        nc = s.nc
        with ExitStack() as es:
            sems = {}
            for k in list(ENG) + ["d%d" % i for i in range(NDS)]:
                sems[k] = es.enter_context(nc.semaphore("s_" + k))
            block = es.enter_context(nc.Block())
            decos = {"sp": block.sync, "act": block.scalar, "dve": block.vector,
                     "pool": block.gpsimd, "pe": block.tensor}
            for eng in ENG:
                def body(e, eng=eng):
                    for waits, fn, inc in s.prog[eng]:
                        for k, v in waits:
                            e.wait_ge(sems[k], v)
                        if fn is not None:
                            ins = fn(e)
                            ins.then_inc(sems[inc[0]], inc[1])
                decos[eng](body)


def build(RSEG, debug=False, NSEG=11, upto="all"):
    nc = bass.Bass("TRN2", target_bir_lowering=False)
    b = Builder(nc)
    TS = RSEG * W
    TK = (RSEG + 8) * W
    NQB = RSEG // 4
    NTK = TK // 128
    NTS = TS // 128

    def din(name, shape, dt=F32):
        return nc.dram_tensor(name, list(shape), dt, kind="ExternalInput").ap()

    xp = din("xp", [(2 * RSEG + 8) * W, D])
    xs = din("xs", [(RSEG + 8) * W, D])
    xsa = din("xsa", [(8 * RSEG + 8) * W, D])
    cvec = din("cvec", [2, D])
    ada_w = din("ada_w", [2, D, 6 * D])
    ada_b = din("ada_b", [2, 6 * D])
    ngain = din("norm_gain", [2, 4, D])
    w_qkv = din("w_qkv", [D, 3 * D])
    w_o = din("w_o", [D, D])
    rpbY = din("rpbY", [NH, 128, 14 * 64])
    cvm = din("cvm", [128, 64])
    bandm = din("bandm", [128, 14])
    rvx = din("rvx", [11, 128, NQB * 6 * 4])
    ident_in = din("ident", [128, 128])
    ffn_w1 = din("ffn_w1", [2, D, DFF])
    ffn_w2 = din("ffn_w2", [2, DFF, D])
    yp = nc.dram_tensor("yp", [2 * TS, D], F32, kind="ExternalOutput").ap()
    ys = nc.dram_tensor("ys", [TS, D], F32, kind="ExternalOutput").ap()
    mv = nc.dram_tensor("mv", [2, 2, 6, D], F32, kind="ExternalOutput" if debug else "Internal").ap()
    aoT = nc.dram_tensor("aoT", [NSEG, KC, 128, TS], BF).ap()
    x1d = nc.dram_tensor("x1d", [NSEG, TS, D], F32).ap()
    x2d = nc.dram_tensor("x2d", [NSEG, TS, D], F32, kind="ExternalOutput" if debug else "Internal").ap()
    t_mv = T("mv")
    t_ao = [T("ao%d" % i) for i in range(NSEG)]
    t_x1 = [T("x1_%d" % i) for i in range(NSEG)]
    t_x2 = [T("x2_%d" % i) for i in range(NSEG)]

    def xsrc(seg):
        if seg == 0:
            return xp[0:TK, :]
        if seg == 1:
            return xp[TS:TS + TK, :]
        if seg == 2:
            return xs[0:TK, :]
        return xsa[(seg - 3) * TS:(seg - 3) * TS + TK, :]

    banks = [nc.alloc_psum_tensor("pb%d" % i, [128, 512], F32) for i in range(7)]
    t_bank = [T("bank%d" % i) for i in range(7)]

    ident_f = b.alloc([128, 128], F32, "identf")
    ident = b.alloc([128, 128], BF, "ident")
    t_ident = T("ident")
    b.dma(ident_f[:], ident_in, writes=[t_ident])
    b.op("dve", lambda e: e.tensor_copy(out=ident[:], in_=ident_f[:]), reads=[t_ident], writes=[t_ident])
    const_mark = b.sb

    ccol = b.alloc([128, 2, KC], F32, "ccol")
    crep = b.alloc([128, 2, KC, 128], F32, "crep")
    t_c = T("c")
    b.dma(ccol[:], cvec.rearrange("s (kc p) -> p s kc", p=128), writes=[t_c], slow=True)
    b.op("act", lambda e: e.activation(out=ccol[:], in_=ccol[:], func=AF.Silu), reads=[t_c], writes=[t_c])
    for q in range(2):
        b.op("dve", lambda e, q=q: e.tensor_copy(
            out=crep[:, q, :, :], in_=ccol[:, q, :].unsqueeze(2).to_broadcast([128, KC, 128])),
            reads=[t_c], writes=[t_c])
    stage = [b.alloc([128, 3072], F32, "adst%d" % i) for i in range(2)]
    t_stage = [T("adst0"), T("adst1")]
    adab = b.alloc([1, 6 * D], F32, "adab")
    gainb = b.alloc([1, 4, D], F32, "gainb")
    modrow = b.alloc([1, 6 * D], F32, "modrow")
    dv = b.alloc([1, 6, D], F32, "dv")
    t_adab, t_mod, t_dv = T("adab"), T("mod"), T("dv")
    si = 0
    for l in range(2):
        b.dma(adab[:], ada_b[l:l + 1, :], writes=[t_adab])
        b.dma(gainb[:], ngain[l:l + 1, :, :], writes=[t_adab])
        for q in range(2):
            for nh in range(2):
                for kc in range(KC):
                    st, tst = stage[si % 2], t_stage[si % 2]
                    si += 1
                    b.dma(st[:], ada_w[l, kc * 128:(kc + 1) * 128, nh * 3072:(nh + 1) * 3072], writes=[tst])

                    def mm(e, st=st, kc=kc, q=q):
                        for j in range(6):
                            ins = e.matmul(banks[j][0:32, :], lhsT=crep[:, q, kc, 0:32], rhs=st[:, j * 512:(j + 1) * 512],
                                           start=(kc == 0), stop=(kc == KC - 1))
                        return ins
                    b.op("pe", mm, reads=[tst, t_c], writes=t_bank[0:6])
                for j in range(6):
                    c0 = nh * 3072 + j * 512
                    b.op("dve", lambda e, j=j, c0=c0: e.tensor_tensor(
                        out=modrow[0:1, c0:c0 + 512], in0=banks[j][0:1, :], in1=adab[0:1, c0:c0 + 512], op=ALU.add),
                        reads=[t_bank[j], t_adab], writes=[t_mod])
            for (o, sc_i, g_i) in ((0, 1, 0), (3, 4, 2)):
                b.op("dve", lambda e, o=o, sc_i=sc_i, g_i=g_i: e.scalar_tensor_tensor(
                    out=dv[0:1, o, :], in0=modrow[0:1, sc_i * D:(sc_i + 1) * D], scalar=1.0, in1=gainb[0:1, g_i, :],
                    op0=ALU.add, op1=ALU.mult), reads=[t_mod, t_adab], writes=[t_dv])
            for (o, sh_i) in ((1, 0), (4, 3)):
                b.op("dve", lambda e, o=o, sh_i=sh_i: e.tensor_copy(
                    out=dv[0:1, o, :], in_=modrow[0:1, sh_i * D:(sh_i + 1) * D]), reads=[t_mod], writes=[t_dv])
            for (o, gt_i, g_i) in ((2, 2, 1), (5, 5, 3)):
                b.op("dve", lambda e, o=o, gt_i=gt_i, g_i=g_i: e.tensor_tensor(
                    out=dv[0:1, o, :], in0=modrow[0:1, gt_i * D:(gt_i + 1) * D], in1=gainb[0:1, g_i, :], op=ALU.mult),
                    reads=[t_mod, t_adab], writes=[t_dv])
            b.dma(mv[q:q + 1, l, :, :], dv[:], reads=[t_dv], writes=[t_mv])
    b.barrier()
    b.sb = const_mark

    def load_cols(q, l, which, name):
        t = b.alloc([128, KC], F32, name)
        tt = T(name)
        b.dma(t[:], mv[q, l, which, :].rearrange("(kc p) -> p kc", p=128), reads=[t_mv], writes=[tt], slow=True)
        return t, tt

    def load_row(q, l, which, name):
        t = b.alloc([128, D], F32, name)
        tt = T(name)
        b.dma(t[:], mv[q, l, which:which + 1, :].partition_broadcast(128), reads=[t_mv], writes=[tt])
        return t, tt

    junk = b.alloc([128, D], BF, "junk")
    t_junk = T("junk")
    tp_ps = nc.alloc_psum_tensor("tp_ps", [128, KC, 128], BF)
    t_tp = T("tp")
    common_mark = b.sb

    def rstd_of(src_aps, reads, ss, t_ss):
        for i, ap in enumerate(src_aps):
            n = ap.shape[-1]
            b.op("act", lambda e, ap=ap, i=i, n=n: e.activation(out=junk[:, 0:n], in_=ap, func=AF.Square,
                                                            accum_out=ss[:, 1 + i:2 + i]),
                 reads=reads, writes=[t_junk, t_ss])
        if len(src_aps) == 2:
            b.op("dve", lambda e: e.tensor_tensor(out=ss[:, 1:2], in0=ss[:, 1:2], in1=ss[:, 2:3], op=ALU.add),
                 reads=[t_ss], writes=[t_ss])
        b.op("dve", lambda e: e.tensor_scalar(out=ss[:, 0:1], in0=ss[:, 1:2], scalar1=1.0 / D, scalar2=EPS,
                                              op0=ALU.mult, op1=ALU.add), reads=[t_ss], writes=[t_ss])
        b.op("act", lambda e: e.activation(out=ss[:, 3:4], in_=ss[:, 0:1], func=AF.Sqrt), reads=[t_ss], writes=[t_ss])
        b.op("dve", lambda e: e.reciprocal(out=ss[:, 0:1], in_=ss[:, 3:4]), reads=[t_ss], writes=[t_ss])

    def norm_stats(x_t, t_x, ss, t_ss, xn, t_xn):
        rstd_of([x_t[:, :]], [t_x], ss, t_ss)
        b.op("act", lambda e: e.activation(out=xn[:], in_=x_t[:, :], func=AF.Copy, scale=ss[:, 0:1]),
             reads=[t_x, t_ss], writes=[t_xn])

    def norm_tr(xn, t_xn):
        def tr(e):
            for kc in range(KC):
                ins = e.transpose(tp_ps[:, kc, :], xn[:, kc * 128:(kc + 1) * 128], ident[:])
            return ins
        b.op("pe", tr, reads=[t_xn, t_ident], writes=[t_tp])

    def norm_evac(acol, bcol, t_ab, dst, t_dst, tok0):
        for kc in range(KC):
            b.op("act", lambda e, kc=kc: e.activation(out=dst[:, kc, tok0:tok0 + 128], in_=tp_ps[:, kc, :],
                                                      func=AF.Identity, scale=acol[:, kc:kc + 1], bias=bcol[:, kc:kc + 1]),
                 reads=[t_tp, t_ab], writes=[t_dst])

    def norm_T(x_t, t_x, ss, t_ss, xn, t_xn, acol, bcol, t_ab, dst, t_dst, tok0):
        norm_stats(x_t, t_x, ss, t_ss, xn, t_xn)
        norm_tr(xn, t_xn)
        norm_evac(acol, bcol, t_ab, dst, t_dst, tok0)

    for seg in range(NSEG):
        b.barrier()
        b.sb = common_mark
        q = 0 if seg < 2 else 1
        acol, t_acol = load_cols(q, 0, 0, "acol")
        bcol, t_bcol = load_cols(q, 0, 1, "bcol")
        t_ab = T("ab")
        b.op("dve", lambda e: e.tensor_copy(out=acol[:, 0:1], in_=acol[:, 0:1]), reads=[t_acol, t_bcol], writes=[t_ab])
        hT = b.alloc([128, KC, TK], BF, "hT")
        t_hT = T("hT")
        xt = [b.alloc([128, D], F32, "xt%d" % i) for i in range(2)]
        t_xt = [T("xt0"), T("xt1")]
        ssA = [b.alloc([128, 4], F32, "ssA%d" % i) for i in range(2)]
        t_ssA = [T("ssA0"), T("ssA1")]
        xnA = [b.alloc([128, D], BF, "xnA%d" % i) for i in range(2)]
        t_xnA = [T("xnA0"), T("xnA1")]
        xsr = xsrc(seg)

        def a1_stats(t):
            b.dma(xt[t % 2][:], xsr[t * 128:(t + 1) * 128, :], writes=[t_xt[t % 2]])
            norm_stats(xt[t % 2], t_xt[t % 2], ssA[t % 2], t_ssA[t % 2], xnA[t % 2], t_xnA[t % 2])
        a1_stats(0)
        for t in range(NTK):
            norm_tr(xnA[t % 2], t_xnA[t % 2])
            if t + 1 < NTK:
                a1_stats(t + 1)
            norm_evac(acol, bcol, t_ab, hT, t_hT, t * 128)
        wst = b.alloc([128, KC, 3, 128], F32, "wst")
        t_wst = T("wst")
        wq = [b.alloc([128, KC, 3, 128], BF, "wq%d" % i) for i in range(2)]
        KT = [b.alloc([128, TK], BF, "KT%d" % i) for i in range(2)]
        QT = [b.alloc([128, TS], BF, "QT%d" % i) for i in range(2)]
        Vaug = [b.alloc([128, NTK, 2, 128], BF, "Vaug%d" % i) for i in range(2)]
        t_wq = [T("wq0"), T("wq1")]
        t_KT = [T("KT0"), T("KT1")]
        t_QT = [T("QT0"), T("QT1")]
        t_V = [T("V0"), T("V1")]
        Yst = b.alloc([128, 14 * 64], F32, "Yst")
        t_Yst = T("Yst")
        Ytf = [[b.alloc([128, 14 * 64], BF, "Ytf") for hh in range(2)] for i in range(2)]
        Yti = [[b.alloc([128, 14 * 64], BF, "Yti") for hh in range(2)] for i in range(2)]
        t_Yt = [[T("Yt") for hh in range(2)] for i in range(2)]
        cv = b.alloc([128, 64], F32, "cv")
        bandt = b.alloc([128, 14], F32, "bandt")
        rv = b.alloc([128, NQB, 6, 4], BF, "rv")
        rvf = b.alloc([128, NQB * 24], F32, "rvf")
        t_cv, t_rv = T("cv"), T("rv")
        b.dma(cv[:], cvm, writes=[t_cv])
        b.dma(bandt[:], bandm, writes=[t_cv])
        b.dma(rvf[:], rvx[seg], writes=[t_rv])
        b.op("dve", lambda e: e.tensor_copy(out=rv[:].rearrange("p a b c -> p (a b c)"), in_=rvf[:]),
             reads=[t_rv], writes=[t_rv])
        expS = [b.alloc([128, 6, 256], F32, "expS%d" % i) for i in range(2)]
        Pm = [b.alloc([128, 6, 256], BF, "Pm%d" % i) for i in range(2)]
        tmpP = b.alloc([128, 6, 256], BF, "tmpP")
        t_expS, t_Pm, t_tmpP = [T("expS0"), T("expS1")], [T("Pm0"), T("Pm1")], T("tmpP")
        rec = [b.alloc([128, 256], F32, "rec%d" % i) for i in range(2)]
        t_rec = [T("rec0"), T("rec1")]
        AO = [b.alloc([128, TS], BF, "AO%d" % i) for i in range(2)]
        t_AO = [T("AO0"), T("AO1")]
        t_Oh = [t_bank[3], t_bank[2]]
        for i in range(2):
            b.op("pool", lambda e, i=i: e.memset(Vaug[i][:, :, 0, 64:128], 1.0), writes=[t_V[i]])
            b.op("pool", lambda e, i=i: e.memset(Vaug[i][:, :, 1, 0:64], 1.0), writes=[t_V[i]])
        wv = w_qkv.rearrange("(kc p) (w n) -> p kc w n", p=128, w=3)

        def proj_items(hp, sl):
            items = []

            def it_w():
                for wi in range(3):
                    b.dma(wst[:, :, wi, :], wv[:, :, wi, hp * 128:(hp + 1) * 128], writes=[t_wst])
                b.op("pool", lambda e: e.tensor_copy(out=wq[sl][:], in_=wst[:]), reads=[t_wst], writes=[t_wq[sl]])
            items.append(it_w)
            for hh in range(2):
                def it_y(hh=hh):
                    b.dma(Yst[:], rpbY[2 * hp + hh], writes=[t_Yst])
                    b.op("act", lambda e: e.activation(out=Yst[:], in_=Yst[:], func=AF.Exp), reads=[t_Yst], writes=[t_Yst])
                    b.op("dve", lambda e: e.tensor_tensor(
                        out=Ytf[sl][hh][:].rearrange("p (m c) -> p m c", c=64), in0=Yst[:].rearrange("p (m c) -> p m c", c=64),
                        in1=cv[:].unsqueeze(1).to_broadcast([128, 14, 64]), op=ALU.mult),
                        reads=[t_Yst, t_cv], writes=[t_Yt[sl][hh]])
                    b.op("dve", lambda e: e.tensor_tensor(
                        out=Yti[sl][hh][:].rearrange("p (m c) -> p m c", c=64), in0=Ytf[sl][hh][:].rearrange("p (m c) -> p m c", c=64),
                        in1=bandt[:].unsqueeze(2).to_broadcast([128, 14, 64]), op=ALU.mult),
                        reads=[t_cv], writes=[t_Yt[sl][hh]])
                items.append(it_y)
            for (dst, t_dst, wi, ntok, off, scl) in ((KT[sl], t_KT[sl], 1, TK, 0, 1.0), (QT[sl], t_QT[sl], 0, TS, 256, 0.125)):
                for c in range(ntok // 512):
                    def it_kq(dst=dst, t_dst=t_dst, wi=wi, c=c, off=off, scl=scl):
                        bk = c % 2

                        def mm(e):
                            for kc in range(KC):
                                ins = e.matmul(banks[bk][:, :], lhsT=wq[sl][:, kc, wi, :],
                                               rhs=hT[:, kc, off + c * 512:off + (c + 1) * 512],
                                               start=(kc == 0), stop=(kc == KC - 1))
                            return ins
                        b.op("pe", mm, reads=[t_wq[sl], t_hT], writes=[t_bank[bk]])
                        b.op("act", lambda e: e.activation(out=dst[:, c * 512:(c + 1) * 512], in_=banks[bk][:, :],
                                                           func=AF.Copy, scale=scl), reads=[t_bank[bk]], writes=[t_dst])
                    items.append(it_kq)
            for t4 in range(NTK // 4):
                def it_v(t4=t4):
                    bk = t4 % 2

                    def mmv(e):
                        for j in range(4):
                            t = t4 * 4 + j
                            for kc in range(KC):
                                ins = e.matmul(banks[bk][:, j * 128:(j + 1) * 128], lhsT=hT[:, kc, t * 128:(t + 1) * 128],
                                               rhs=wq[sl][:, kc, 2, :], start=(kc == 0), stop=(kc == KC - 1))
                        return ins
                    b.op("pe", mmv, reads=[t_wq[sl], t_hT], writes=[t_bank[bk]])
                    pv = banks[bk][:, :].rearrange("p (j n) -> p j n", n=128)
                    b.op("act", lambda e: e.activation(out=Vaug[sl][:, t4 * 4:t4 * 4 + 4, 0, 0:64], in_=pv[:, :, 0:64], func=AF.Copy),
                         reads=[t_bank[bk]], writes=[t_V[sl]])
                    b.op("dve", lambda e: e.tensor_copy(out=Vaug[sl][:, t4 * 4:t4 * 4 + 4, 1, 64:128], in_=pv[:, :, 64:128]),
                         reads=[t_bank[bk]], writes=[t_V[sl]])
                items.append(it_v)
            return items

        for it in proj_items(0, 0):
            it()
        def attn_hp(hp, sl):
            pending = proj_items(hp + 1, 1 - sl) if hp + 1 < KC else []
            iters = [(hh, qb) for hh in range(2) for qb in range(NQB)]
            per = (len(pending) + len(iters) - 1) // len(iters)

            def scores(n):
                hh, qb = iters[n]
                p0 = 64 * hh

                def mms(e):
                    for kk in range(6):
                        k = 5 - kk
                        kt0 = (4 * qb + 2 * k) * 64
                        ins = e.matmul(banks[4 + kk // 2][:, (kk % 2) * 256:(kk % 2) * 256 + 256],
                                       lhsT=KT[sl][p0:p0 + 64, kt0:kt0 + 128],
                                       rhs=QT[sl][p0:p0 + 64, qb * 256:(qb + 1) * 256], start=True, stop=True)
                    return ins
                b.op("pe", mms, reads=[t_KT[sl], t_QT[sl]], writes=[t_bank[4], t_bank[5], t_bank[6]])

            def softmax(n):
                hh, qb = iters[n]
                bf_ = n % 2
                for j in range(3):
                    b.op("act", lambda e, j=j: e.activation(
                        out=expS[bf_][:, 2 * j:2 * j + 2, :], in_=banks[4 + j][:, :].rearrange("p (a n) -> p a n", n=256),
                        func=AF.Exp), reads=[t_bank[4 + j]], writes=[t_expS[bf_]])
                boundary = qb in (0, NQB - 1)
                Y = Ytf[sl][hh] if boundary else Yti[sl][hh]
                ywin = bass.AP(Y, Y[:].offset, [list(Y[:].ap[0]), [128, 6], [1, 256]])
                if boundary:
                    b.op("dve", lambda e: e.tensor_tensor(out=tmpP[:], in0=expS[bf_][:], in1=ywin, op=ALU.mult),
                         reads=[t_expS[bf_], t_Yt[sl][hh]], writes=[t_tmpP])
                    b.op("pool", lambda e: e.tensor_tensor(
                        out=Pm[bf_][:].rearrange("p k (r c) -> p k r c", c=64),
                        in0=tmpP[:].rearrange("p k (r c) -> p k r c", c=64),
                        in1=rv[:, qb, :, :].unsqueeze(3).to_broadcast([128, 6, 4, 64]), op=ALU.mult),
                        reads=[t_tmpP, t_rv], writes=[t_Pm[bf_]])
                else:
                    b.op("dve", lambda e: e.tensor_tensor(out=Pm[bf_][:], in0=expS[bf_][:], in1=ywin, op=ALU.mult),
                         reads=[t_expS[bf_], t_Yt[sl][hh]], writes=[t_Pm[bf_]])

            def pvmm(n):
                hh, qb = iters[n]
                bf_ = n % 2

                def mmo(e):
                    for kk in range(6):
                        k = 5 - kk
                        tt_ = 2 * qb + k
                        ins = e.matmul(banks[3 - bf_][:, 0:256], lhsT=Vaug[sl][:, tt_, hh, :], rhs=Pm[bf_][:, kk, :],
                                       start=(kk == 0), stop=(kk == 5))
                    return ins
                b.op("pe", mmo, reads=[t_V[sl], t_Pm[bf_]], writes=[t_Oh[bf_]])

            def normo(n):
                hh, qb = iters[n]
                bf_ = n % 2
                dn, dd = (64, 0) if hh == 0 else (0, 64)
                ob = banks[3 - bf_][:, 0:256]
                b.op("dve", lambda e: e.reciprocal(out=rec[bf_][dd:dd + 64, :], in_=ob[dn:dn + 64, :]),
                     reads=[t_Oh[bf_]], writes=[t_rec[bf_]])
                b.op("dve", lambda e: e.tensor_tensor(
                    out=AO[sl][dd:dd + 64, qb * 256:(qb + 1) * 256], in0=ob[dd:dd + 64, :],
                    in1=rec[bf_][dd:dd + 64, :], op=ALU.mult), reads=[t_Oh[bf_], t_rec[bf_]], writes=[t_AO[sl]])

            import os
            MODE = os.environ.get("ATT_MODE", "pipe")
            if MODE == "seq":
                for n in range(len(iters)):
                    scores(n)
                    softmax(n)
                    pvmm(n)
                    normo(n)
                while pending:
                    pending.pop(0)()
            elif MODE == "noproj":
                scores(0)
                for n in range(len(iters)):
                    softmax(n)
                    if n + 1 < len(iters):
                        scores(n + 1)
                    pvmm(n)
                    if n > 0:
                        normo(n - 1)
                normo(len(iters) - 1)
                while pending:
                    pending.pop(0)()
            else:
                scores(0)
                for n in range(len(iters)):
                    softmax(n)
                    if n + 1 < len(iters):
                        scores(n + 1)
                    for _ in range(per):
                        if pending:
                            pending.pop(0)()
                    pvmm(n)
                    if n > 0:
                        normo(n - 1)
                normo(len(iters) - 1)
                while pending:
                    pending.pop(0)()
            b.dma(aoT[seg, hp], AO[sl][:], reads=[t_AO[sl]], writes=[t_ao[seg]])

        for hp in range(KC):
            attn_hp(hp, hp % 2)

        b.barrier()
        b.sb = common_mark
        wo_st = b.alloc([128, KC, D], F32, "wo_st")
        wo = b.alloc([128, KC, D], BF, "wo")
        t_wo = T("wo")
        b.dma(wo_st[:], w_o.rearrange("(kc p) n -> p kc n", p=128), writes=[t_wo])
        b.op("pool", lambda e: e.tensor_copy(out=wo[:], in_=wo_st[:]), reads=[t_wo], writes=[t_wo])
        grow, t_grow = load_row(q, 0, 2, "grow")
        aot = [b.alloc([128, KC, 128], BF, "aot%d" % i) for i in range(2)]
        t_aot = [T("aot0"), T("aot1")]
        xt = [b.alloc([128, D], F32, "xt%d" % i) for i in range(2)]
        t_xt = [T("xt0"), T("xt1")]
        tmpW = [b.alloc([128, D], F32, "tmpW%d" % i) for i in range(2)]
        t_tmpW = [T("tmpW0"), T("tmpW1")]
        x1t = [b.alloc([128, D], F32, "x1t%d" % i) for i in range(2)]
        t_x1t = [T("x1t0"), T("x1t1")]
        ssW = [b.alloc([128, 4], F32, "ssW%d" % i) for i in range(2)]
        t_ssW = [T("ssW0"), T("ssW1")]
        for tt in range(NTS):
            a, ta, x_, tx = aot[tt % 2], t_aot[tt % 2], xt[tt % 2], t_xt[tt % 2]
            b0 = 2 * (tt % 2)
            tmp, t_tmp, ss, t_ss = tmpW[tt % 2], t_tmpW[tt % 2], ssW[tt % 2], t_ssW[tt % 2]
            b.dma(a[:], aoT[seg, :, :, tt * 128:(tt + 1) * 128].rearrange("k p t -> p k t"), reads=[t_ao[seg]], writes=[ta])
            b.dma(x_[:], xsr[256 + tt * 128:256 + (tt + 1) * 128, :], writes=[tx])

            def mmw(e, a=a, b0=b0):
                for nh in range(2):
                    for hp in range(KC):
                        ins = e.matmul(banks[b0 + nh][:, :], lhsT=a[:, hp, :], rhs=wo[:, hp, nh * 512:(nh + 1) * 512],
                                       start=(hp == 0), stop=(hp == KC - 1))
                return ins
            b.op("pe", mmw, reads=[ta, t_wo], writes=[t_bank[b0], t_bank[b0 + 1]])
            rstd_of([banks[b0][:, :], banks[b0 + 1][:, :]], [t_bank[b0], t_bank[b0 + 1]], ss, t_ss)
            for nh in range(2):
                b.op("dve", lambda e, nh=nh, b0=b0, tmp=tmp, ss=ss: e.scalar_tensor_tensor(
                    out=tmp[:, nh * 512:(nh + 1) * 512], in0=banks[b0 + nh][:, :], scalar=ss[:, 0:1],
                    in1=grow[:, nh * 512:(nh + 1) * 512], op0=ALU.mult, op1=ALU.mult),
                    reads=[t_bank[b0 + nh], t_ss, t_grow], writes=[t_tmp])
            o, to = x1t[tt % 2], t_x1t[tt % 2]
            b.op("pool", lambda e, o=o, x_=x_, tmp=tmp: e.tensor_tensor(out=o[:], in0=x_[:], in1=tmp[:], op=ALU.add),
                 reads=[tx, t_tmp], writes=[to])
            b.dma(x1d[seg, tt * 128:(tt + 1) * 128, :], o[:], reads=[to], writes=[t_x1[seg]])

    def ffn_phase(l, srcs, t_srcs, dsts, t_dsts):
        b.barrier()
        b.sb = common_mark
        w1b = b.alloc([128, KC, DFF], BF, "w1b")
        w2b = b.alloc([128, 32, D], BF, "w2b")
        t_w1, t_w2 = T("w1"), T("w2")
        mark = b.sb
        stg = [b.alloc([128, DFF], F32, "stg%d" % i) for i in range(2)]
        t_stg = [T("stg0"), T("stg1")]
        w1v = ffn_w1[l].rearrange("(kc p) n -> p kc n", p=128)
        w2v = ffn_w2[l].rearrange("(k p) n -> p k n", p=128)
        for kc in range(KC):
            st, ts_ = stg[kc % 2], t_stg[kc % 2]
            b.dma(st[:], w1v[:, kc, :], writes=[ts_])
            b.op("pool" if kc % 2 else "dve", lambda e, st=st, kc=kc: e.tensor_copy(out=w1b[:, kc, :], in_=st[:]),
                 reads=[ts_], writes=[t_w1])
        for k4 in range(8):
            st, ts_ = stg[k4 % 2], t_stg[k4 % 2]
            b.dma(st[:].rearrange("p (k n) -> p k n", n=D), w2v[:, k4 * 4:(k4 + 1) * 4, :], writes=[ts_])
            b.op("pool" if k4 % 2 else "dve", lambda e, st=st, k4=k4: e.tensor_copy(
                out=w2b[:, k4 * 4:(k4 + 1) * 4, :], in_=st[:].rearrange("p (k n) -> p k n", n=D)),
                reads=[ts_], writes=[t_w2])
        b.barrier()
        b.sb = mark
        x1t = [b.alloc([128, D], F32, "fx%d" % i) for i in range(4)]
        t_x1t = [T("fx%d" % i) for i in range(4)]
        h2T = b.alloc([128, KC, 512], BF, "h2T")
        t_h2T = T("h2T")
        hid = b.alloc([128, 32, 512], BF, "hid")
        t_hid = T("hid")
        rl = [b.alloc([128, 512], BF, "rl%d" % i) for i in range(2)]
        t_rl = [T("rl0"), T("rl1")]
        tmpF = [b.alloc([128, D], F32, "ftmp%d" % i) for i in range(2)]
        t_tmpF = [T("ftmp0"), T("ftmp1")]
        ssF = [b.alloc([128, 4], F32, "fss%d" % i) for i in range(2)]
        t_ssF = [T("fss0"), T("fss1")]
        ss = b.alloc([128, 4], F32, "fss")
        t_ss = T("fss")
        xn = b.alloc([128, D], BF, "fxn")
        t_xn = T("fxn")
        mark2 = b.sb
        oi = 0
        for seg in range(len(srcs)):
            q = 0 if seg < 2 else 1
            if seg in (0, 2):
                if seg == 2:
                    b.barrier()
                b.sb = mark2
                acol, t_acol = load_cols(q, l, 3, "facol")
                bcol, t_bcol = load_cols(q, l, 4, "fbcol")
                t_ab = T("fab")
                b.op("dve", lambda e, acol=acol: e.tensor_copy(out=acol[:, 0:1], in_=acol[:, 0:1]),
                     reads=[t_acol, t_bcol], writes=[t_ab])
                grow, t_grow = load_row(q, l, 5, "fgrow")
            for blk in range(TS // 512):
                for tt in range(4):
                    r0 = blk * 512 + tt * 128
                    b.dma(x1t[tt][:], srcs[seg][r0:r0 + 128, :], reads=[t_srcs[seg]], writes=[t_x1t[tt]])
                    norm_T(x1t[tt], t_x1t[tt], ss, t_ss, xn, t_xn, acol, bcol, t_ab, h2T, t_h2T, tt * 128)
                for m in range(32):
                    bk = m % 2

                    def mmu(e, m=m, bk=bk):
                        for kc in range(KC):
                            ins = e.matmul(banks[bk][:, :], lhsT=w1b[:, kc, m * 128:(m + 1) * 128], rhs=h2T[:, kc, :],
                                           start=(kc == 0), stop=(kc == KC - 1))
                        return ins
                    b.op("pe", mmu, reads=[t_w1, t_h2T], writes=[t_bank[bk]])
                    b.op("act", lambda e, bk=bk: e.activation(out=rl[bk][:], in_=banks[bk][:, :], func=AF.Relu),
                         reads=[t_bank[bk]], writes=[t_rl[bk]])
                    b.op("pool", lambda e, m=m, bk=bk: e.tensor_tensor(out=hid[:, m, :], in0=rl[bk][:], in1=rl[bk][:], op=ALU.mult),
                         reads=[t_rl[bk]], writes=[t_hid])
                for tt in range(4):
                    b0 = 2 + 2 * (tt % 2)
                    tmp_, t_tmp_, ss_, t_ss_ = tmpF[tt % 2], t_tmpF[tt % 2], ssF[tt % 2], t_ssF[tt % 2]

                    def mmd(e, tt=tt, b0=b0):
                        for nh in range(2):
                            for k in range(32):
                                ins = e.matmul(banks[b0 + nh][:, :], lhsT=hid[:, k, tt * 128:(tt + 1) * 128],
                                               rhs=w2b[:, k, nh * 512:(nh + 1) * 512], start=(k == 0), stop=(k == 31))
                        return ins
                    b.op("pe", mmd, reads=[t_hid, t_w2], writes=[t_bank[b0], t_bank[b0 + 1]])
                    rstd_of([banks[b0][:, :], banks[b0 + 1][:, :]], [t_bank[b0], t_bank[b0 + 1]], ss_, t_ss_)
                    for nh in range(2):
                        b.op("dve", lambda e, nh=nh, b0=b0, tmp_=tmp_, ss_=ss_: e.scalar_tensor_tensor(
                            out=tmp_[:, nh * 512:(nh + 1) * 512], in0=banks[b0 + nh][:, :], scalar=ss_[:, 0:1],
                            in1=grow[:, nh * 512:(nh + 1) * 512], op0=ALU.mult, op1=ALU.mult),
                            reads=[t_bank[b0 + nh], t_ss_, t_grow], writes=[t_tmp_])
                    b.op("pool", lambda e, tt=tt, tmp_=tmp_: e.tensor_tensor(out=tmp_[:], in0=x1t[tt][:], in1=tmp_[:], op=ALU.add),
                         reads=[t_x1t[tt], t_tmp_], writes=[t_tmp_])
                    r0 = blk * 512 + tt * 128
                    b.dma(dsts[seg][r0:r0 + 128, :], tmp_[:], reads=[t_tmp_], writes=[t_dsts[seg]])

    ffn_phase(0, [x1d[i] for i in range(NSEG)], t_x1, [x2d[i] for i in range(NSEG)], t_x2)
    if upto == "l0":
        b.barrier()
        return nc, b

    import math
    I32 = mybir.dt.int32
    NCH = TS // 8
    NB = NCH // 16
    NCT = NCH // 128
    LOG2NC = int(round(math.log2(NCH)))
    a_re_in = din("s5_a_re", [2, 64, 64])
    a_im_in = din("s5_a_im", [2, 64, 64])
    ldt_in = din("s5_log_dt", [2, 64])
    b_in = [din("s5_b_re", [2, 64, 64, 16]), din("s5_b_im", [2, 64, 64, 16])]
    c_in = [din("s5_c_re", [2, 64, 16, 64]), din("s5_c_im", [2, 64, 16, 64])]
    d_in = din("s5_d", [D])
    wglu_in = din("s5_w_glu", [D, 2 * D])
    mfb_in = din("mfb", [2, 128, 128])
    sel_in = din("sel", [128, 8])
    winD = nc.dram_tensor("winD", [8, 128, 8 * 2 * 2 * 64], BF).ap()
    woutD = nc.dram_tensor("woutD", [8, 128, 2 * 2 * 4 * 128], BF).ap()
    wtoepD = nc.dram_tensor("wtoepD", [8, 128, 8 * 128], BF).ap()
    ytd = nc.dram_tensor("ytd", [3, TS, D], BF).ap()
    x3d = nc.dram_tensor("x3d", [3, TS, D], F32, kind="ExternalOutput" if debug else "Internal").ap()
    t_wD = T("wD")
    t_yt = [T("yt%d" % i) for i in range(3)]
    t_x3 = [T("x3_%d" % i) for i in range(3)]

    def tt(eng, out, in0, in1, op, reads, writes):
        return b.op(eng, lambda e: e.tensor_tensor(out=out, in0=in0, in1=in1, op=op), reads=reads, writes=writes)

    def tsc(eng, out, in0, s1, s2, op0, op1, reads, writes):
        if op1 is None:
            return b.op(eng, lambda e: e.tensor_scalar(out=out, in0=in0, scalar1=s1, scalar2=None, op0=op0), reads=reads, writes=writes)
        return b.op(eng, lambda e: e.tensor_scalar(out=out, in0=in0, scalar1=s1, scalar2=s2, op0=op0, op1=op1), reads=reads, writes=writes)

    def cp(eng, out, in_, reads, writes):
        if eng == "act":
            return b.op(eng, lambda e: e.activation(out=out, in_=in_, func=AF.Copy), reads=reads, writes=writes)
        return b.op(eng, lambda e: e.tensor_copy(out=out, in_=in_), reads=reads, writes=writes)

    def cmul(eng, o_r, o_i, ar, ai, br, bi, t1, t2, tk):
        tt(eng, t1, ar, br, ALU.mult, tk, tk)
        tt(eng, t2, ai, bi, ALU.mult, tk, tk)
        tt(eng, o_r, t1, t2, ALU.subtract, tk, tk)
        tt(eng, t1, ar, bi, ALU.mult, tk, tk)
        tt(eng, t2, ai, br, ALU.mult, tk, tk)
        tt(eng, o_i, t1, t2, ALU.add, tk, tk)

    def cmad(eng, d_r, d_i, cr, ci, s_r, s_i, t1, reads, writes):
        rw = list(reads) + list(writes)
        tt(eng, t1, cr, s_r, ALU.mult, rw, writes)
        tt(eng, d_r, d_r, t1, ALU.add, rw, writes)
        tt(eng, t1, ci, s_i, ALU.mult, rw, writes)
        tt(eng, d_r, d_r, t1, ALU.subtract, rw, writes)
        tt(eng, t1, cr, s_i, ALU.mult, rw, writes)
        tt(eng, d_i, d_i, t1, ALU.add, rw, writes)
        tt(eng, t1, ci, s_r, ALU.mult, rw, writes)
        tt(eng, d_i, d_i, t1, ALU.add, rw, writes)

    b.barrier()
    b.sb = common_mark
    L8 = b.alloc([128, 2, 2, 32], F32, "L8")
    L128 = b.alloc([128, 2, 2, 32], F32, "L128")
    LSEG = b.alloc([128, 2, 2, 32], F32, "LSEG")
    selt = b.alloc([128, 8], F32, "selt")
    CA8 = b.alloc([128, 32, 2, 2, 2], F32, "CA8")
    CA128 = b.alloc([128, 32, 2, 2, 2], F32, "CA128")
    HIN = b.alloc([128, 3, 32, 2, 2], F32, "HIN")
    t_Lt, t_sel, t_HIN = T("Lt"), T("sel"), T("HIN")
    b.dma(selt[:], sel_in, writes=[t_sel])
    s5_mark = b.sb

    tP = T("prep")
    P = [tP]
    AR = b.alloc([128, 2, 32], F32, "AR")
    AI = b.alloc([128, 2, 32], F32, "AI")
    DT = b.alloc([128, 2, 32], F32, "DT")
    for d in range(2):
        for g2 in range(2):
            b.dma(AR[64 * g2:64 * g2 + 64, d, :], a_re_in[d].rearrange("(gq g2) p -> g2 p gq", g2=2)[g2], writes=P, slow=True)
            b.dma(AI[64 * g2:64 * g2 + 64, d, :], a_im_in[d].rearrange("(gq g2) p -> g2 p gq", g2=2)[g2], writes=P, slow=True)
            b.dma(DT[64 * g2:64 * g2 + 64, d:d + 1, :],
                  ldt_in[d:d + 1, :].rearrange("o (gq g2) -> o g2 gq", g2=2)[:, g2, :].partition_broadcast(64), writes=P, slow=True)
    b.op("act", lambda e: e.activation(out=DT[:], in_=DT[:], func=AF.Exp), reads=P, writes=P)
    XR = b.alloc([128, 2, 32], F32, "XR")
    TH = b.alloc([128, 2, 32], F32, "TH")
    tt("dve", XR[:], AR[:], DT[:], ALU.mult, P, P)
    tt("dve", TH[:], AI[:], DT[:], ALU.mult, P, P)
    PW = b.alloc([128, 9, 2, 2, 32], F32, "PW")
    mg = b.alloc([128, 2, 32], F32, "mg")
    arg = b.alloc([128, 2, 32], F32, "arg")
    uu = b.alloc([128, 2, 32], F32, "uu")
    ni = b.alloc([128, 2, 32], I32, "ni")
    nf = b.alloc([128, 2, 32], F32, "nf")
    for k in range(9):
        b.op("act", lambda e, k=k: e.activation(out=mg[:], in_=XR[:], func=AF.Exp, scale=float(k)), reads=P, writes=P)
        for ri_idx, ph in ((1, 0.0), (0, math.pi / 2)):
            tsc("dve", arg[:], TH[:], float(k), ph, ALU.mult, ALU.add, P, P)
            tsc("dve", uu[:], arg[:], 1.0 / (2 * math.pi), None, ALU.mult, None, P, P)
            cp("dve", ni[:], uu[:], P, P)
            cp("dve", nf[:], ni[:], P, P)
            b.op("dve", lambda e: e.scalar_tensor_tensor(out=uu[:], in0=nf[:], scalar=-2 * math.pi, in1=arg[:],
                                                         op0=ALU.mult, op1=ALU.add), reads=P, writes=P)
            b.op("act", lambda e: e.activation(out=nf[:], in_=uu[:], func=AF.Sin), reads=P, writes=P)
            tt("dve", PW[:, k, ri_idx, :, :], mg[:], nf[:], ALU.mult, P, P)
    EIN = b.alloc([128, 2, 2, 32, 8], F32, "EIN")
    EOUT = b.alloc([128, 2, 2, 32, 8], F32, "EOUT")
    for k in range(9):
        if k <= 7:
            cp("pool", EIN[:, :, 0, :, 7 - k], PW[:, k, :, 0, :], P, P)
            cp("pool", EIN[:, :, 1, :, k], PW[:, k, :, 1, :], P, P)
        if k >= 1:
            cp("pool", EOUT[:, :, 0, :, k - 1], PW[:, k, :, 0, :], P, P)
            cp("pool", EOUT[:, :, 1, :, 8 - k], PW[:, k, :, 1, :], P, P)
    zr = b.alloc([128, 2, 32], F32, "zr")
    zi = b.alloc([128, 2, 32], F32, "zi")
    q1 = b.alloc([128, 2, 32], F32, "q1")
    q2 = b.alloc([128, 2, 32], F32, "q2")
    q3 = b.alloc([128, 2, 32], F32, "q3")
    nr = b.alloc([128, 2, 32], F32, "nr")
    tsc("dve", nr[:], PW[:, 1, 0, :, :], -1.0, None, ALU.add, None, P, P)
    nim = PW[:, 1, 1, :, :]
    tt("dve", q1[:], AR[:], AR[:], ALU.mult, P, P)
    tt("dve", q2[:], AI[:], AI[:], ALU.mult, P, P)
    tt("dve", q1[:], q1[:], q2[:], ALU.add, P, P)
    b.op("dve", lambda e: e.reciprocal(out=q3[:], in_=q1[:]), reads=P, writes=P)
    tt("dve", q1[:], nr[:], AR[:], ALU.mult, P, P)
    tt("dve", q2[:], nim, AI[:], ALU.mult, P, P)
    tt("dve", q1[:], q1[:], q2[:], ALU.add, P, P)
    tt("dve", zr[:], q1[:], q3[:], ALU.mult, P, P)
    tt("dve", q1[:], nim, AR[:], ALU.mult, P, P)
    tt("dve", q2[:], nr[:], AI[:], ALU.mult, P, P)
    tt("dve", q1[:], q1[:], q2[:], ALU.subtract, P, P)
    tt("dve", zi[:], q1[:], q3[:], ALU.mult, P, P)
    PL = [tP, t_Lt]
    cp("dve", L8[:], PW[:, 8, :, :, :], PL, PL)
    I8 = b.alloc([128, 2, 2, 32], F32, "I8")
    tt("dve", q1[:], L8[:, 0], L8[:, 0], ALU.mult, PL, P)
    tt("dve", q2[:], L8[:, 1], L8[:, 1], ALU.mult, PL, P)
    tt("dve", q1[:], q1[:], q2[:], ALU.add, P, P)
    b.op("dve", lambda e: e.reciprocal(out=q3[:], in_=q1[:]), reads=P, writes=P)
    tt("dve", I8[:, 0], L8[:, 0], q3[:], ALU.mult, PL, P)
    tt("dve", q1[:], L8[:, 1], q3[:], ALU.mult, PL, P)
    tsc("dve", I8[:, 1], q1[:], -1.0, None, ALU.mult, None, P, P)
    sqa = b.alloc([128, 2, 2, 32], F32, "sqa")
    sqb = b.alloc([128, 2, 2, 32], F32, "sqb")
    cur = L8
    for it in range(LOG2NC):
        dst = L128 if it == 3 else (LSEG if it == LOG2NC - 1 else (sqa if cur is not sqa else sqb))
        tt("dve", q1[:], cur[:, 0], cur[:, 0], ALU.mult, PL, P)
        tt("dve", q2[:], cur[:, 1], cur[:, 1], ALU.mult, PL, P)
        tt("dve", dst[:, 0], q1[:], q2[:], ALU.subtract, PL, PL)
        tt("dve", q1[:], cur[:, 0], cur[:, 1], ALU.mult, PL, P)
        tsc("dve", dst[:, 1], q1[:], 2.0, None, ALU.mult, None, PL, PL)
        cur = dst
    if upto == "prepA":
        b.barrier()
        return nc, b
    Braw = b.alloc([128, 2, 2, 32, 16], F32, "Braw")
    Bb = b.alloc([128, 2, 2, 32, 16], F32, "Bb")
    for ri in range(2):
        for d in range(2):
            for g2 in range(2):
                b.dma(Braw[64 * g2:64 * g2 + 64, ri, d, :, :],
                      b_in[ri][d].rearrange("(gq g2) p j -> g2 p gq j", g2=2)[g2], writes=P)
    bt1 = b.alloc([128, 32, 16], F32, "bt1")
    bt2 = b.alloc([128, 32, 16], F32, "bt2")
    for d in range(2):
        zrb = zr[:, d, :].unsqueeze(2).to_broadcast([128, 32, 16])
        zib = zi[:, d, :].unsqueeze(2).to_broadcast([128, 32, 16])
        cmul("dve", Bb[:, 0, d], Bb[:, 1, d], zrb, zib, Braw[:, 0, d], Braw[:, 1, d], bt1[:], bt2[:], P)
    Cn = b.alloc([128, 8, 2, 2, 64], F32, "Cn")
    Ct = b.alloc([128, 2, 2, 32, 16], F32, "Ct")
    for ri in range(2):
        for d in range(2):
            b.dma(Cn[:, :, ri, d, :], c_in[ri][d].rearrange("(gb gl) i p -> (gl i) gb p", gl=8), writes=P)
    for gb in range(8):
        bk = gb % 2

        def trc(e, gb=gb, bk=bk):
            for ri in range(2):
                for d in range(2):
                    c4 = ri * 2 + d
                    ins = e.transpose(banks[bk][0:64, c4 * 128:(c4 + 1) * 128], Cn[:, gb, ri, d, :], ident_f[:, :])
            return ins
        b.op("pe", trc, reads=P + [t_ident], writes=[t_bank[bk]])
        for g2 in range(2):
            src = banks[bk][0:64, :].rearrange("p (c q g i) -> p c q g i", c=4, q=4, g=2)[:, :, :, g2, :]
            dst = Ct[64 * g2:64 * g2 + 64, :, :, 4 * gb:4 * gb + 4, :].rearrange("p r d q i -> p (r d) q i")
            cp("act", dst, src, [t_bank[bk]], P)
    if upto == "prepB":
        b.barrier()
        return nc, b
    dcol = b.alloc([128, 64], F32, "dcol")
    for s_ in range(8):
        b.dma(dcol[16 * s_:16 * s_ + 16, :], d_in.rearrange("(g j) -> j g", j=16), writes=P, slow=True)
    MF = b.alloc([128, 128], F32, "MF")
    MB = b.alloc([128, 128], F32, "MB")
    b.dma(MF[:], mfb_in[0], writes=P)
    b.dma(MB[:], mfb_in[1], writes=P)
    XW = b.alloc([128, 2, 2, 4, 8, 16], F32, "XW")
    XI = b.alloc([128, 2, 2, 4, 8, 16], F32, "XI")
    WoF = b.alloc([128, 2, 2, 4, 8, 16], F32, "WoF")
    XIz = [b.alloc([128, 2, 2, 4, 8, 16], F32, "XIz%d" % i) for i in range(2)]
    w1_ = b.alloc([128, 4, 8, 16], F32, "w1_")
    w2_ = b.alloc([128, 4, 8, 16], F32, "w2_")
    wob = b.alloc([128, 2 * 2 * 4 * 128], BF, "wob")
    wib = b.alloc([128, 8 * 2 * 2 * 64], BF, "wib")
    wtb = b.alloc([128, 8, 128], BF, "wtb")
    tq1 = b.alloc([128, 128], F32, "tq1")
    tq2 = b.alloc([128, 128], F32, "tq2")
    t_wo_, t_wi_, t_wt_ = T("wob"), T("wib"), T("wtb")
    SH = [128, 4, 8, 16]
    for gb in range(8):
        gqs = slice(4 * gb, 4 * gb + 4)
        for d in range(2):
            er = EIN[:, 0, d, gqs, :].unsqueeze(3).to_broadcast(SH)
            ei = EIN[:, 1, d, gqs, :].unsqueeze(3).to_broadcast(SH)
            br_ = Bb[:, 0, d, gqs, :].unsqueeze(2).to_broadcast(SH)
            bi_ = Bb[:, 1, d, gqs, :].unsqueeze(2).to_broadcast(SH)
            cmul("dve", XW[:, 0, d], XW[:, 1, d], er, ei, br_, bi_, w1_[:], w2_[:], P)
            ir = I8[:, 0, d, gqs].unsqueeze(2).unsqueeze(3).to_broadcast(SH)
            ii = I8[:, 1, d, gqs].unsqueeze(2).unsqueeze(3).to_broadcast(SH)
            cmul("pool", XI[:, 0, d], XI[:, 1, d], XW[:, 0, d], XW[:, 1, d], ir, ii, w1_[:], w2_[:], P)
            cr = Ct[:, 0, d, gqs, :].unsqueeze(2).to_broadcast(SH)
            ci = Ct[:, 1, d, gqs, :].unsqueeze(2).to_broadcast(SH)
            eor = EOUT[:, 0, d, gqs, :].unsqueeze(3).to_broadcast(SH)
            eoi = EOUT[:, 1, d, gqs, :].unsqueeze(3).to_broadcast(SH)
            cmul("dve", WoF[:, 0, d], w2_[:], cr, ci, eor, eoi, w1_[:], WoF[:, 1, d], P)
            tsc("dve", WoF[:, 1, d], w2_[:], -1.0, None, ALU.mult, None, P, P)
        if upto == "prepC":
            break
        cp("act", wob[:], WoF[:].rearrange("p r d q s i -> p (r d q s i)"), P, [t_wo_])
        b.dma(woutD[gb], wob[:], reads=[t_wo_], writes=[t_wD])
        if upto == "prepD":
            continue
        for hb_ in range(4):
            def trw(e, hb_=hb_):
                for ri in range(2):
                    for d in range(2):
                        col = (ri * 2 + d) * 128
                        ins = e.transpose(banks[hb_][:, col:col + 128],
                                          XW[:, ri, d, hb_, :, :].rearrange("p s j -> p (s j)"), ident_f[:, :])
                return ins
            b.op("pe", trw, reads=P + [t_ident], writes=[t_bank[hb_]])
            cp("act", wib[:, hb_ * 512:(hb_ + 1) * 512].rearrange("p (g c n) -> p c g n", g=2, c=4),
               banks[hb_][:, :].rearrange("p (c g n) -> p c g n", c=4, g=2), [t_bank[hb_]], [t_wi_])
        b.dma(winD[gb], wib[:], reads=[t_wi_], writes=[t_wD])
        if upto == "prepE":
            continue
        for g2 in range(2):
            b.op("pool", lambda e, g2=g2: e.memset(XIz[g2][:], 0.0), reads=P, writes=P)
            cp("pool", XIz[g2][64 * g2:64 * g2 + 64], XI[64 * g2:64 * g2 + 64], P, P)
        for gl in range(8):
            g2, glq = gl % 2, gl // 2
            g = 8 * gb + gl
            bk = 4 + gl % 2

            def mmt(e, g2=g2, glq=glq, bk=bk):
                for d in range(2):
                    for ri in range(2):
                        ins = e.matmul(banks[bk][:, d * 128:(d + 1) * 128],
                                       lhsT=XIz[g2][:, ri, d, glq, :, :].rearrange("p s j -> p (s j)"),
                                       rhs=WoF[:, ri, d, glq, :, :].rearrange("p s i -> p (s i)"),
                                       start=(ri == 0), stop=(ri == 1))
                return ins
            b.op("pe", mmt, reads=P, writes=[t_bank[bk]])
            tt("dve", tq1[:], banks[bk][:, 0:128], MF[:], ALU.mult, P + [t_bank[bk]], P)
            tt("dve", tq2[:], banks[bk][:, 128:256], MB[:], ALU.mult, P + [t_bank[bk]], P)
            tt("dve", tq1[:], tq1[:], tq2[:], ALU.add, P, P)
            b.op("dve", lambda e, gl=gl, g=g: e.scalar_tensor_tensor(
                out=wtb[:, gl, :], in0=ident_f[:, :], scalar=dcol[:, g:g + 1], in1=tq1[:], op0=ALU.mult, op1=ALU.add),
                reads=P + [t_ident], writes=[t_wt_])
        b.dma(wtoepD[gb], wtb[:].rearrange("p g n -> p (g n)"), reads=[t_wt_], writes=[t_wD])

    if upto == "prep":
        b.barrier()
        return nc, b
    b.barrier()
    b.sb = s5_mark
    arow1b = b.alloc([128, D], F32, "arow1")
    brow1b = b.alloc([128, D], F32, "brow1")
    t_ab1b = T("ab1")
    arow1 = [arow1b, arow1b]
    brow1 = [brow1b, brow1b]
    t_ab1 = [t_ab1b, t_ab1b]
    curq = [None]

    def set_q(q):
        if curq[0] == q:
            return
        curq[0] = q
        b.dma(arow1b[:], mv[q, 1, 0:1, :].partition_broadcast(128), reads=[t_mv], writes=[t_ab1b])
        b.dma(brow1b[:], mv[q, 1, 1:2, :].partition_broadcast(128), reads=[t_mv], writes=[t_ab1b])
    EA = b.alloc([128, 8, 32, 2, 2], F32, "EA")
    t_EA = T("EA")
    HBs = [b.alloc([128, 32, 2, 2, NB], F32, "HB%d" % i) for i in range(3)]
    t_HB = [T("HB%d" % i) for i in range(3)]
    s5_mark2 = b.sb

    U8 = b.alloc([128, 64, NCH], BF, "U8")
    t_U8 = T("U8")
    S_off = b.sb
    S = b.alloc([128, 16, 2, 2, max(NCH, 192)], F32, "S")
    S_end = b.sb
    SCW = max(NCH, 192)
    t_S = [T("S_f"), T("S_b")]
    tmpA = [b.alloc([128, 16, 2, NB], F32, "tmpA%d" % i) for i in range(2)]
    tmpA2 = [b.alloc([128, 16, 2, NB], F32, "tmpA2%d" % i) for i in range(2)]
    tmpB = [b.alloc([128, 16, NB], F32, "tmpB%d" % i) for i in range(2)]
    t_tm = [T("tm0"), T("tm1")]
    winb = [b.alloc([128, 8, 2, 2, 64], BF, "winb0")] * 2
    t_winb = [T("winb0")] * 2
    HBtmp = b.alloc([128, 32, 2, 2, NB], F32, "HBtmp")
    t_HBtmp = T("HBtmp")
    sm_t = b.alloc([128, 32, 2], F32, "sm_t")
    s5_mark3 = b.sb
    SENG = ("dve", "dve")
    wcount = [0]

    def build_U8(src, t_src, q):
        set_q(q)
        mk = b.sb
        b.sb = S_off
        xc = b.alloc([128, 8, D], F32, "xc")
        hb = b.alloc([128, 64, 8, 16], BF, "hb")
        assert b.sb <= S_end
        b.sb = mk
        ss8 = b.alloc([128, 24], F32, "ss8")
        t_xc, t_hb, t_s8 = t_S[0], t_S[1], T("ss8")
        for ct in range(NCT):
            b.dma(xc[:], src[ct * 1024:(ct + 1) * 1024, :].rearrange("(c s) d -> c s d", s=8), reads=[t_src], writes=[t_xc])
            for s_ in range(8):
                b.op("act", lambda e, s_=s_: e.activation(out=junk[:], in_=xc[:, s_, :], func=AF.Square,
                                                          accum_out=ss8[:, s_:s_ + 1]), reads=[t_xc], writes=[t_junk, t_s8])
            tsc("dve", ss8[:, 8:16], ss8[:, 0:8], 1.0 / D, EPS, ALU.mult, ALU.add, [t_s8], [t_s8])
            b.op("act", lambda e: e.activation(out=ss8[:, 16:24], in_=ss8[:, 8:16], func=AF.Sqrt), reads=[t_s8], writes=[t_s8])
            b.op("dve", lambda e: e.reciprocal(out=ss8[:, 8:16], in_=ss8[:, 16:24]), reads=[t_s8], writes=[t_s8])
            for s_ in range(8):
                b.op("dve", lambda e, s_=s_: e.scalar_tensor_tensor(
                    out=xc[:, s_, :], in0=xc[:, s_, :], scalar=ss8[:, 8 + s_:9 + s_], in1=arow1[q][:],
                    op0=ALU.mult, op1=ALU.mult), reads=[t_xc, t_s8, t_ab1[q]], writes=[t_xc])
            for s_ in range(8):
                b.op("pool", lambda e, s_=s_: e.tensor_tensor(
                    out=hb[:, :, s_, :], in0=xc[:, s_, :].rearrange("c (g j) -> c g j", j=16),
                    in1=brow1[q][:].rearrange("c (g j) -> c g j", j=16), op=ALU.add),
                    reads=[t_xc, t_ab1[q]], writes=[t_hb])
            for g8 in range(8):
                def tr(e, g8=g8):
                    for gi in range(8):
                        ins = e.transpose(tp_ps[:, gi, :], hb[:, g8 * 8 + gi, :, :].rearrange("c s j -> c (s j)"), ident[:])
                    return ins
                b.op("pe", tr, reads=[t_hb, t_ident], writes=[t_tp])
                cp("act", U8[:, g8 * 8:(g8 + 1) * 8, ct * 128:(ct + 1) * 128], tp_ps[:, :, :], [t_tp], [t_U8])
        b.sb = mk

    def state_mm(half):
        for sb_ in range(4):
            gb = half * 4 + sb_
            wb, twb = winb[wcount[0] % 2], t_winb[wcount[0] % 2]
            wcount[0] += 1
            b.dma(wb[:].rearrange("p g r d n -> p (g r d n)"), winD[gb], reads=[t_wD], writes=[twb])
            for glq in range(4):
                gq = sb_ * 4 + glq
                nbk = (4 * NCH + 511) // 512
                bks = [(glq % 2) * nbk + i for i in range(nbk)]

                def mm(e, glq=glq, gb=gb, wb=wb, bks=bks):
                    for g2 in range(2):
                        gl = 2 * glq + g2
                        g = 8 * gb + gl
                        for d in range(2):
                            for ri in range(2):
                                c4 = d * 2 + ri
                                bk = bks[(c4 * NCH) // 512]
                                off = (c4 * NCH) % 512
                                ins = e.matmul(banks[bk][64 * g2:64 * g2 + 64, off:off + NCH], lhsT=wb[:, gl, ri, d, :],
                                               rhs=U8[:, g, :], start=True, stop=True, tile_position=(0, 64 * g2))
                    return ins
                b.op("pe", mm, reads=[twb, t_U8], writes=[t_bank[x] for x in bks])
                for i, bk in enumerate(bks):
                    n = min(512, 4 * NCH - i * 512)
                    ncmb = n // NCH
                    c40 = (i * 512) // NCH
                    for cc in range(ncmb):
                        c4 = c40 + cc
                        cp("act", S[:, gq, c4 // 2, c4 % 2, 0:NCH], banks[bk][:, cc * NCH:(cc + 1) * NCH], [t_bank[bk]], t_S)

    def c0_scan(half):
        S6 = S[:, :, :, :, 0:NCH].rearrange("p g d r (a c) -> p g d r a c", c=16)
        gqs = slice(half * 16, half * 16 + 16)
        for k in range(1, 16):
            for d in range(2):
                eng = SENG[d]
                c0, pv = (k, k - 1) if d == 0 else (15 - k, 16 - k)
                lr = L8[:, 0, d, gqs]
                li = L8[:, 1, d, gqs]
                tk = [t_S[d], t_tm[d]]
                tt(eng, tmpA[d][:], S6[:, :, d, :, :, pv], lr.unsqueeze(2).unsqueeze(3).to_broadcast([128, 16, 2, NB]),
                   ALU.mult, tk + [t_Lt], tk)
                tt(eng, S6[:, :, d, :, :, c0], S6[:, :, d, :, :, c0], tmpA[d][:], ALU.add, tk, tk)
                lib = li.unsqueeze(2).to_broadcast([128, 16, NB])
                tt(eng, tmpB[d][:], S6[:, :, d, 1, :, pv], lib, ALU.mult, tk + [t_Lt], tk)
                tt(eng, S6[:, :, d, 0, :, c0], S6[:, :, d, 0, :, c0], tmpB[d][:], ALU.subtract, tk, tk)
                tt(eng, tmpB[d][:], S6[:, :, d, 0, :, pv], lib, ALU.mult, tk + [t_Lt], tk)
                tt(eng, S6[:, :, d, 1, :, c0], S6[:, :, d, 1, :, c0], tmpB[d][:], ALU.add, tk, tk)

    def pass1(HB, t_hb_):
        S6 = S[:, :, :, :, 0:NCH].rearrange("p g d r (a c) -> p g d r a c", c=16)
        for half in range(2):
            state_mm(half)
            c0_scan(half)
            gqs = slice(half * 16, half * 16 + 16)
            cp("dve", HB[:, gqs, 0, :, :], S6[:, :, 0, :, :, 15], [t_S[0]], [t_hb_])
            cp("pool", HB[:, gqs, 1, :, :], S6[:, :, 1, :, :, 0], [t_S[1]], [t_hb_])

    def c1_scan(HB, t_hb_, hins, dirs=(0, 1)):
        tk = [t_hb_, t_HBtmp]
        for d in dirs:
            lr = L128[:, 0, d, :]
            li = L128[:, 1, d, :]
            order = list(range(NB)) if d == 0 else list(range(NB - 1, -1, -1))
            prev = None
            for c1 in order:
                if prev is None:
                    prev = c1
                    if d not in hins:
                        continue
                    sr, si, tks = hins[d]
                    rd = tk + list(tks) + [t_Lt]
                else:
                    sr, si = HB[:, :, d, 0, prev], HB[:, :, d, 1, prev]
                    rd = tk + [t_Lt]
                    prev = c1
                cmad("dve", HB[:, :, d, 0, c1], HB[:, :, d, 1, c1], lr, li, sr, si, sm_t[:, :, 0], rd, tk)

    def seg_source(seg):
        return x2d[seg], t_x2[seg]

    if NSEG > 3:
        for j in range(8):
            seg = 3 + j
            src, tsrc = seg_source(seg)
            build_U8(src, tsrc, 1)
            pass1(HBtmp, t_HBtmp)
            c1_scan(HBtmp, t_HBtmp, {})
            cp("dve", EA[:, j, :, 0, :], HBtmp[:, :, 0, :, NB - 1], [t_HBtmp], [t_EA])
            cp("dve", EA[:, j, :, 1, :], HBtmp[:, :, 1, :, 0], [t_HBtmp], [t_EA])
        acc = b.alloc([128, 32, 2, 2], F32, "acc")
        acc2 = b.alloc([128, 32, 2, 2], F32, "acc2")
        t_acc = T("acc")
        b.op("dve", lambda e: e.memset(HIN[:, 2], 0.0), writes=[t_HIN])
        for d in range(2):
            b.op("dve", lambda e: e.memset(acc[:], 0.0), writes=[t_acc])
            order = list(range(8)) if d == 0 else list(range(7, -1, -1))
            for j in order:
                b.op("dve", lambda e, j=j, d=d: e.scalar_tensor_tensor(
                    out=HIN[:, 2, :, d, :], in0=acc[:, :, d, :], scalar=selt[:, j:j + 1], in1=HIN[:, 2, :, d, :],
                    op0=ALU.mult, op1=ALU.add), reads=[t_acc, t_sel], writes=[t_HIN])
                cp("dve", acc2[:, :, d, :], EA[:, j, :, d, :], [t_EA, t_acc], [t_acc])
                cmad("dve", acc2[:, :, d, 0], acc2[:, :, d, 1], LSEG[:, 0, d, :], LSEG[:, 1, d, :],
                     acc[:, :, d, 0], acc[:, :, d, 1], sm_t[:, :, 0], [t_Lt], [t_acc])
                cp("dve", acc[:, :, d, :], acc2[:, :, d, :], [t_acc], [t_acc])

    if upto == "ea":
        b.barrier()
        return nc, b
    own = [0, 1, 2] if NSEG > 3 else [0, 1]
    for seg in own:
        src, tsrc = seg_source(seg)
        build_U8(src, tsrc, 0 if seg < 2 else 1)
        pass1(HBs[seg], t_HB[seg])
    hin_of = {seg: {} for seg in own}
    c1_scan(HBs[1], t_HB[1], {}, dirs=(1,))
    hin_of[0][1] = (HBs[1][:, :, 1, 0, 0], HBs[1][:, :, 1, 1, 0], [t_HB[1]])
    c1_scan(HBs[0], t_HB[0], hin_of[0], dirs=(0, 1))
    hin_of[1][0] = (HBs[0][:, :, 0, 0, NB - 1], HBs[0][:, :, 0, 1, NB - 1], [t_HB[0]])
    c1_scan(HBs[1], t_HB[1], hin_of[1], dirs=(0,))
    if 2 in own:
        for d in range(2):
            hin_of[2][d] = (HIN[:, 2, :, d, 0], HIN[:, 2, :, d, 1], [t_HIN])
        c1_scan(HBs[2], t_HB[2], hin_of[2])
    CXb = HBtmp
    t_CXb = t_HBtmp

    def make_CX(seg):
        HB = HBs[seg]
        b.op("pool", lambda e: e.memset(CXb[:], 0.0), writes=[t_CXb])
        cp("pool", CXb[:, :, 0, :, 1:NB], HB[:, :, 0, :, 0:NB - 1], [t_HB[seg]], [t_CXb])
        cp("pool", CXb[:, :, 1, :, 0:NB - 1], HB[:, :, 1, :, 1:NB], [t_HB[seg]], [t_CXb])
        for d, c1 in ((0, 0), (1, NB - 1)):
            if d in hin_of[seg]:
                sr, si, tks = hin_of[seg][d]
                cp("pool", CXb[:, :, d, 0, c1], sr, list(tks), [t_CXb])
                cp("pool", CXb[:, :, d, 1, c1], si, list(tks), [t_CXb])
    CXs = {seg: CXb for seg in own}
    t_CX = {seg: t_CXb for seg in own}

    if upto == "c1":
        b.barrier()
        return nc, b
    Hx = b.alloc([128, 16, 2, 2, NCH], BF, "Hx")
    t_Hx = T("Hx")
    wo2 = [b.alloc([128, 2, 2, 4, 128], BF, "wo2_%d" % i) for i in range(2)]
    wt2 = [b.alloc([128, 8, 128], BF, "wt2_0")] * 2
    t_wo2 = [T("wo2_0"), T("wo2_1")]
    t_wt2 = [T("wt2_0")] * 2
    ytt = [b.alloc([128, 8, 128], BF, "ytt0")] * 2
    t_ytt = [T("ytt0")] * 2
    yc = 0
    oc = 0
    S6 = S[:, :, :, :, 0:NCH].rearrange("p g d r (a c) -> p g d r a c", c=16)
    Hx6 = Hx[:].rearrange("p g d r (a c) -> p g d r a c", c=16)
    for seg in own:
        src, tsrc = seg_source(seg)
        build_U8(src, tsrc, 0 if seg < 2 else 1)
        make_CX(seg)
        CX = CXs[seg]
        for half in range(2):
            gqs = slice(half * 16, half * 16 + 16)
            state_mm(half)
            for d, c0 in ((0, 0), (1, 15)):
                eng = SENG[d]
                lrb = L8[:, 0, d, gqs].unsqueeze(2).to_broadcast([128, 16, NB])
                lib = L8[:, 1, d, gqs].unsqueeze(2).to_broadcast([128, 16, NB])
                cmad(eng, S6[:, :, d, 0, :, c0], S6[:, :, d, 1, :, c0], lrb, lib, CX[:, gqs, d, 0, :], CX[:, gqs, d, 1, :],
                     tmpB[d][:], [t_CX[seg], t_Lt, t_tm[d]], [t_S[d], t_tm[d]])
            c0_scan(half)
            for ri in range(2):
                cp("act", Hx6[:, :, 0, ri, :, 1:16], S6[:, :, 0, ri, :, 0:15], [t_S[0]], [t_Hx])
                cp("act", Hx6[:, :, 1, ri, :, 0:15], S6[:, :, 1, ri, :, 1:16], [t_S[1]], [t_Hx])
            cp("act", Hx6[:, :, 0, :, :, 0], CX[:, gqs, 0, :, :], [t_CX[seg]], [t_Hx])
            cp("act", Hx6[:, :, 1, :, :, 15], CX[:, gqs, 1, :, :], [t_CX[seg]], [t_Hx])
            for sb_ in range(4):
                gb = half * 4 + sb_
                wo_, two_, wt_, twt_ = wo2[oc % 2], t_wo2[oc % 2], wt2[oc % 2], t_wt2[oc % 2]
                oc += 1
                b.dma(wo_[:].rearrange("p r d q n -> p (r d q n)"), woutD[gb], reads=[t_wD], writes=[two_])
                b.dma(wt_[:].rearrange("p g n -> p (g n)"), wtoepD[gb], reads=[t_wD], writes=[twt_])
                for ct in range(NCT):
                    cs = slice(ct * 128, (ct + 1) * 128)
                    yt_, tyt_ = ytt[yc % 2], t_ytt[yc % 2]
                    yc += 1
                    for hb_ in range(2):
                        bk = 4 + hb_

                        def mmy(e, hb_=hb_, bk=bk, gb=gb, sb_=sb_, wo_=wo_, wt_=wt_, cs=cs):
                            for gg in range(4):
                                gl = hb_ * 4 + gg
                                g2, glq = gl % 2, gl // 2
                                g = 8 * gb + gl
                                gq = sb_ * 4 + glq
                                o = banks[bk][:, gg * 128:(gg + 1) * 128]
                                e.matmul(o, lhsT=U8[:, g, cs], rhs=wt_[:, gl, :], start=True, stop=False)
                                for d in range(2):
                                    for ri in range(2):
                                        ins = e.matmul(o, lhsT=Hx[64 * g2:64 * g2 + 64, gq, d, ri, cs],
                                                       rhs=wo_[64 * g2:64 * g2 + 64, ri, d, glq, :],
                                                       start=False, stop=(d == 1 and ri == 1))
                            return ins
                        b.op("pe", mmy, reads=[t_U8, t_Hx, two_, twt_], writes=[t_bank[bk]])
                        b.op("act", lambda e, hb_=hb_, bk=bk, yt_=yt_: e.activation(
                            out=yt_[:, :, hb_ * 64:(hb_ + 1) * 64].rearrange("c s (g i) -> c g s i", i=16),
                            in_=banks[bk][:, :].rearrange("c (g s i) -> c g s i", g=4, s=8), func=AF.Gelu),
                            reads=[t_bank[bk]], writes=[tyt_])
                    b.dma(ytd[seg][ct * 1024:(ct + 1) * 1024, gb * 128:(gb + 1) * 128].rearrange("(c s) n -> c s n", s=8),
                          yt_[:], reads=[tyt_], writes=[t_yt[seg]])

    if upto == "p2":
        b.barrier()
        return nc, b
    b.barrier()
    b.sb = s5_mark
    wg = b.alloc([128, KC, 2 * D], BF, "wg")
    t_wg = T("wg")
    mkg = b.sb
    wgs = [b.alloc([128, 2 * D], F32, "wgs%d" % i) for i in range(2)]
    t_wgs = [T("wgs0"), T("wgs1")]
    wgv = wglu_in.rearrange("(kc p) n -> p kc n", p=128)
    for kc in range(KC):
        b.dma(wgs[kc % 2][:], wgv[:, kc, :], writes=[t_wgs[kc % 2]])
        cp("pool" if kc % 2 else "dve", wg[:, kc, :], wgs[kc % 2][:], [t_wgs[kc % 2]], [t_wg])
    b.barrier()
    b.sb = mkg
    ytc = b.alloc([128, 8, D], BF, "ytc")
    x2c = b.alloc([128, 8, D], F32, "x2c")
    gT = b.alloc([128, KC, 128], BF, "gT")
    sg = b.alloc([128, D], F32, "sg")
    yv = b.alloc([128, D], F32, "yv")
    tmpg = b.alloc([128, D], F32, "tmpg")
    og = [b.alloc([128, D], F32, "og%d" % i) for i in range(2)]
    ssg = b.alloc([128, 4], F32, "ssg")
    t_ytc, t_x2c, t_gT, t_sg, t_yv, t_tmpg, t_ssg = T("ytc"), T("x2c"), T("gT"), T("sg"), T("yv"), T("tmpg"), T("ssg")
    t_og = [T("og0"), T("og1")]
    mkg2 = b.sb
    ogc = 0
    for seg in own:
        q = 0 if seg < 2 else 1
        if seg in (0, 2):
            if seg == 2:
                b.barrier()
            b.sb = mkg2
            growm, t_growm = load_row(q, 1, 2, "growm")
        for ct in range(NCT):
            rows = slice(ct * 1024, (ct + 1) * 1024)
            b.dma(ytc[:], ytd[seg][rows, :].rearrange("(c s) n -> c s n", s=8), reads=[t_yt[seg]], writes=[t_ytc])
            b.dma(x2c[:], x2d[seg][rows, :].rearrange("(c s) n -> c s n", s=8), reads=[t_x2[seg]], writes=[t_x2c])
            for s_ in range(8):
                def trg(e, s_=s_):
                    for kc in range(KC):
                        ins = e.transpose(tp_ps[:, kc, :], ytc[:, s_, kc * 128:(kc + 1) * 128], ident[:])
                    return ins
                b.op("pe", trg, reads=[t_ytc, t_ident], writes=[t_tp])
                cp("act", gT[:], tp_ps[:, :, :], [t_tp], [t_gT])

                def mmg(e):
                    for n4 in range(4):
                        for kc in range(KC):
                            ins = e.matmul(banks[n4][:, :], lhsT=gT[:, kc, :], rhs=wg[:, kc, n4 * 512:(n4 + 1) * 512],
                                           start=(kc == 0), stop=(kc == KC - 1))
                    return ins
                b.op("pe", mmg, reads=[t_gT, t_wg], writes=t_bank[0:4])
                for h2 in range(2):
                    b.op("act", lambda e, h2=h2: e.activation(out=sg[:, h2 * 512:(h2 + 1) * 512], in_=banks[2 + h2][:, :],
                                                              func=AF.Sigmoid), reads=[t_bank[2 + h2]], writes=[t_sg])
                    tt("dve", yv[:, h2 * 512:(h2 + 1) * 512], banks[h2][:, :], sg[:, h2 * 512:(h2 + 1) * 512], ALU.mult,
                       [t_bank[h2], t_sg], [t_yv])
                rstd_of([yv[:, :]], [t_yv], ssg, t_ssg)
                b.op("dve", lambda e: e.scalar_tensor_tensor(out=tmpg[:], in0=yv[:], scalar=ssg[:, 0:1], in1=growm[:],
                                                             op0=ALU.mult, op1=ALU.mult),
                     reads=[t_yv, t_ssg, t_growm], writes=[t_tmpg])
                o, to = og[ogc % 2], t_og[ogc % 2]
                ogc += 1
                tt("pool", o[:], x2c[:, s_, :], tmpg[:], ALU.add, [t_x2c, t_tmpg], [to])
                b.dma(x3d[seg][rows, :].rearrange("(c s) d -> c s d", s=8)[:, s_, :], o[:], reads=[to], writes=[t_x3[seg]])

    if debug:
        b.barrier()
        return nc, b
    t_out = [T("o0"), T("o1"), T("o2")]
    ffn_phase(1, [x3d[i] for i in range(3)], t_x3, [yp[0:TS, :], yp[TS:2 * TS, :], ys[:, :]], t_out)
    b.barrier()
    return nc, b


def make_inputs(inp, core, RSEG):
    TS = RSEG * W
    f = np.float32
    xp = np.zeros(((2 * RSEG + 8) * W, D), f)
    xp[256:256 + 2 * TS] = inp["x_prompt"][core]
    xs = np.zeros(((RSEG + 8) * W, D), f)
    Rs = 8 * RSEG
    g0s = core * RSEG
    lo, hi = max(0, g0s - 4), min(Rs, g0s + RSEG + 4)
    xs[(lo - (g0s - 4)) * W:(hi - (g0s - 4)) * W] = inp["x_sample"][0][lo * W:hi * W]
    cvec = np.stack([inp["c_prompt"][core], inp["c_sample"][0]]).astype(f)
    p = np.arange(128)
    kr2, kc = p // 64, p % 64
    mm = np.arange(14)
    qc = np.arange(64)
    dr = kr2[:, None, None] + 13 - mm[None, :, None] + 0 * qc[None, None, :]
    dc = np.clip(kc[:, None, None] - qc[None, None, :], -15, 15) + 15 + 0 * mm[None, :, None]
    rpbY = inp["attn_rpb"][0][:, dr, dc].reshape(NH, 128, 14 * 64).astype(f)
    cs = np.clip(qc - 8, 0, 48)
    cvm = ((kc[:, None] >= cs[None, :]) & (kc[:, None] < cs[None, :] + 16)).astype(f)
    mmi = np.arange(14)
    bandm = ((mmi[None, :] >= kr2[:, None] + 3) & (mmi[None, :] <= kr2[:, None] + 10)).astype(f)
    NQB = RSEG // 4
    xsa = np.zeros(((8 * RSEG + 8) * W, D), f)
    xsa[256:256 + 8 * TS] = inp["x_sample"][0]
    segs = [(0, 2 * RSEG), (RSEG, 2 * RSEG), (g0s, Rs)] + [(j * RSEG, Rs) for j in range(8)]
    rvx = np.zeros((11, 128, NQB, 6, 4), f)
    for seg, (g0, R) in enumerate(segs):
        for qb in range(NQB):
            for kk in range(6):
                k = 5 - kk
                for qr in range(4):
                    qg = g0 + 4 * qb + qr
                    ws = min(max(qg - 4, 0), R - 8)
                    for k2 in range(2):
                        kg = g0 - 4 + 4 * qb + 2 * k + k2
                        ok = (0 <= kg < R) and (ws <= kg < ws + 8)
                        rvx[seg, k2 * 64:(k2 + 1) * 64, qb, kk, qr] = 1.0 if ok else 0.0
    sidx = np.arange(128) // 16
    mfb = np.stack([(sidx[:, None] <= sidx[None, :]), (sidx[:, None] >= sidx[None, :])]).astype(f)
    sel = np.zeros((128, 8), f)
    sel[:, core] = 1.0
    return dict(xp=xp, xs=xs, xsa=xsa, cvec=cvec, ada_w=inp["ada_w"], ada_b=inp["ada_b"], norm_gain=inp["norm_gain"],
                w_qkv=inp["attn_w_qkv"][0], w_o=inp["attn_w_o"][0], rpbY=rpbY, cvm=cvm, bandm=bandm,
                rvx=rvx.reshape(11, 128, NQB * 24), ident=np.eye(128, dtype=f),
                ffn_w1=inp["ffn_w1"], ffn_w2=inp["ffn_w2"],
                s5_a_re=inp["s5_a_re"][0], s5_a_im=inp["s5_a_im"][0], s5_log_dt=inp["s5_log_dt"][0],
                s5_b_re=inp["s5_b_re"][0], s5_b_im=inp["s5_b_im"][0], s5_c_re=inp["s5_c_re"][0], s5_c_im=inp["s5_c_im"][0],
                s5_d=inp["s5_d"][0], s5_w_glu=inp["s5_w_glu"][0], mfb=mfb, sel=sel)


_CACHE = {}


def kernel(**inputs):
    RSEG = 32
    inp = {k: np.asarray(v) for k, v in inputs.items()}
    if "nc" not in _CACHE:
        nc, b = build(RSEG)
        b.emit()
        _CACHE["nc"] = nc
    nc = _CACHE["nc"]
    maps = [make_inputs(inp, core, RSEG) for core in range(8)]
    res = run_bass_kernel_spmd(nc, maps, core_ids=list(range(8)))
    TS = RSEG * W
    y_prompt = np.stack([res.results[c]["yp"] for c in range(8)]).astype(np.float32)
    y_sample = np.concatenate([res.results[c]["ys"] for c in range(8)], axis=0)[None].astype(np.float32)
    return (y_prompt, y_sample)
```

```python
import numpy as np
from contextlib import ExitStack
import concourse.bass as bass
import concourse.mybir as mybir
from concourse.bass_utils import run_bass_kernel_spmd

F32, BF = mybir.dt.float32, mybir.dt.bfloat16
AF = mybir.ActivationFunctionType
ALU = mybir.AluOpType
D = 1024
KC = 8
W = 64
NH = 16
DFF = 4096
EPS = 1e-6
ENG = ("sp", "act", "dve", "pool", "pe")
NDS = 24
SB_BASE = 16640
SB_LIMIT = 229376


class T:
    def __init__(s, name):
        s.name = name
        s.w = None
        s.r = []


class Builder:
    def __init__(s, nc):
        s.nc = nc
        s.prog = {e: [] for e in ENG}
        s.cnt = {}
        s.waited = {e: {} for e in ENG}
        s.ndma = 0
        s.sb = SB_BASE
        s.nalloc = 0

    def alloc(s, shape, dt, name=None):
        nb = int(np.prod(shape[1:])) * (4 if dt == F32 else 2)
        nb = (nb + 63) // 64 * 64
        assert s.sb + nb <= SB_LIMIT, ("SBUF overflow", name, s.sb, nb)
        s.nalloc += 1
        t = s.nc.alloc_sbuf_tensor_at("%s_%d" % (name or "t", s.nalloc), list(shape), dt, offset=s.sb)
        s.sb += nb
        return t

    def _deps(s, eng, reads, writes):
        deps = []
        for t in reads:
            if t.w:
                deps.append((t.w, True))
        for t in writes:
            if t.w:
                deps.append((t.w, True))
            deps.extend((r, False) for r in t.r)
        out = {}
        for (k, v), isw in deps:
            if k == eng and (eng == "pe" or not isw):
                continue
            if s.waited[eng].get(k, 0) >= v:
                continue
            out[k] = max(out.get(k, 0), v)
        for k, v in out.items():
            s.waited[eng][k] = v
        return list(out.items())

    def _finish(s, tok, reads, writes):
        for t in reads:
            t.r.append(tok)
        for t in writes:
            t.w = tok
            t.r = []

    def op(s, eng, fn, reads=(), writes=()):
        waits = s._deps(eng, reads, writes)
        s.cnt[eng] = s.cnt.get(eng, 0) + 1
        tok = (eng, s.cnt[eng])
        s.prog[eng].append((waits, fn, (eng, 1)))
        s._finish(tok, reads, writes)
        return tok

    def dma(s, out, in_, reads=(), writes=(), q="sp", slow=False):
        k = "d%d" % (s.ndma % NDS)
        s.ndma += 1
        waits = s._deps(q, reads, writes)
        prev = s.cnt.get(k, 0)
        if prev and s.waited[q].get(k, 0) < prev:
            waits.append((k, prev))
            s.waited[q][k] = prev
        s.cnt[k] = prev + 16
        tok = (k, s.cnt[k])
        if slow:
            fn = lambda e: e.dma_start(out=out, in_=in_, allow_slow_non_contiguous=True)
        else:
            fn = lambda e: e.dma_start(out=out, in_=in_)
        s.prog[q].append((waits, fn, (k, 16)))
        s._finish(tok, reads, writes)
        return tok

    def barrier(s):
        for e in ENG:
            waits = []
            for k, v in s.cnt.items():
                if k == e and e == "pe":
                    continue
                if s.waited[e].get(k, 0) < v:
                    waits.append((k, v))
                    s.waited[e][k] = v
            if waits:
                s.prog[e].append((waits, None, None))

    def emit(s):
        nc = s.nc
        with ExitStack() as es:
            sems = {}
            for k in list(ENG) + ["d%d" % i for i in range(NDS)]:
                sems[k] = es.enter_context(nc.semaphore("s_" + k))
            block = es.enter_context(nc.Block())
            decos = {"sp": block.sync, "act": block.scalar, "dve": block.vector,
                     "pool": block.gpsimd, "pe": block.tensor}
            for eng in ENG:
                def body(e, eng=eng):
                    for waits, fn, inc in s.prog[eng]:
                        for k, v in waits:
                            e.wait_ge(sems[k], v)
                        if fn is not None:
                            ins = fn(e)
                            ins.then_inc(sems[inc[0]], inc[1])
                decos[eng](body)


def build(RSEG, debug=False, NSEG=11, upto="all"):
    nc = bass.Bass("TRN2", target_bir_lowering=False)
    b = Builder(nc)
    TS = RSEG * W
    TK = (RSEG + 8) * W
    NQB = RSEG // 4
    NTK = TK // 128
    NTS = TS // 128

    def din(name, shape, dt=F32):
        return nc.dram_tensor(name, list(shape), dt, kind="ExternalInput").ap()

    xp = din("xp", [(2 * RSEG + 8) * W, D])
    xs = din("xs", [(RSEG + 8) * W, D])
    xsa = din("xsa", [(8 * RSEG + 8) * W, D])
    cvec = din("cvec", [2, D])
    ada_w = din("ada_w", [2, D, 6 * D])
    ada_b = din("ada_b", [2, 6 * D])
    ngain = din("norm_gain", [2, 4, D])
    w_qkv = din("w_qkv", [D, 3 * D])
    w_o = din("w_o", [D, D])
    rpbY = din("rpbY", [NH, 128, 14 * 64])
    cvm = din("cvm", [128, 64])
    bandm = din("bandm", [128, 14])
    rvx = din("rvx", [11, 128, NQB * 6 * 4])
    ident_in = din("ident", [128, 128])
    ffn_w1 = din("ffn_w1", [2, D, DFF])
    ffn_w2 = din("ffn_w2", [2, DFF, D])
    yp = nc.dram_tensor("yp", [2 * TS, D], F32, kind="ExternalOutput").ap()
    ys = nc.dram_tensor("ys", [TS, D], F32, kind="ExternalOutput").ap()
    mv = nc.dram_tensor("mv", [2, 2, 6, D], F32, kind="ExternalOutput" if debug else "Internal").ap()
    aoT = nc.dram_tensor("aoT", [NSEG, KC, 128, TS], BF).ap()
    x1d = nc.dram_tensor("x1d", [NSEG, TS, D], F32).ap()
    x2d = nc.dram_tensor("x2d", [NSEG, TS, D], F32, kind="ExternalOutput" if debug else "Internal").ap()
    t_mv = T("mv")
    t_ao = [T("ao%d" % i) for i in range(NSEG)]
    t_x1 = [T("x1_%d" % i) for i in range(NSEG)]
    t_x2 = [T("x2_%d" % i) for i in range(NSEG)]

    def xsrc(seg):
        if seg == 0:
            return xp[0:TK, :]
        if seg == 1:
            return xp[TS:TS + TK, :]
        if seg == 2:
            return xs[0:TK, :]
        return xsa[(seg - 3) * TS:(seg - 3) * TS + TK, :]

    banks = [nc.alloc_psum_tensor("pb%d" % i, [128, 512], F32) for i in range(7)]
    t_bank = [T("bank%d" % i) for i in range(7)]

    ident_f = b.alloc([128, 128], F32, "identf")
    ident = b.alloc([128, 128], BF, "ident")
    t_ident = T("ident")
    b.dma(ident_f[:], ident_in, writes=[t_ident])
    b.op("dve", lambda e: e.tensor_copy(out=ident[:], in_=ident_f[:]), reads=[t_ident], writes=[t_ident])
    const_mark = b.sb

    ccol = b.alloc([128, 2, KC], F32, "ccol")
    crep = b.alloc([128, 2, KC, 128], F32, "crep")
    t_c = T("c")
    b.dma(ccol[:], cvec.rearrange("s (kc p) -> p s kc", p=128), writes=[t_c], slow=True)
    b.op("act", lambda e: e.activation(out=ccol[:], in_=ccol[:], func=AF.Silu), reads=[t_c], writes=[t_c])
    for q in range(2):
        b.op("dve", lambda e, q=q: e.tensor_copy(
            out=crep[:, q, :, :], in_=ccol[:, q, :].unsqueeze(2).to_broadcast([128, KC, 128])),
            reads=[t_c], writes=[t_c])
    stage = [b.alloc([128, 3072], F32, "adst%d" % i) for i in range(2)]
    t_stage = [T("adst0"), T("adst1")]
    adab = b.alloc([1, 6 * D], F32, "adab")
    gainb = b.alloc([1, 4, D], F32, "gainb")
    modrow = b.alloc([1, 6 * D], F32, "modrow")
    dv = b.alloc([1, 6, D], F32, "dv")
    t_adab, t_mod, t_dv = T("adab"), T("mod"), T("dv")
    si = 0
    for l in range(2):
        b.dma(adab[:], ada_b[l:l + 1, :], writes=[t_adab])
        b.dma(gainb[:], ngain[l:l + 1, :, :], writes=[t_adab])
        for q in range(2):
            for nh in range(2):
                for kc in range(KC):
                    st, tst = stage[si % 2], t_stage[si % 2]
                    si += 1
                    b.dma(st[:], ada_w[l, kc * 128:(kc + 1) * 128, nh * 3072:(nh + 1) * 3072], writes=[tst])

                    def mm(e, st=st, kc=kc, q=q):
                        for j in range(6):
                            ins = e.matmul(banks[j][0:32, :], lhsT=crep[:, q, kc, 0:32], rhs=st[:, j * 512:(j + 1) * 512],
                                           start=(kc == 0), stop=(kc == KC - 1))
                        return ins
                    b.op("pe", mm, reads=[tst, t_c], writes=t_bank[0:6])
                for j in range(6):
                    c0 = nh * 3072 + j * 512
                    b.op("dve", lambda e, j=j, c0=c0: e.tensor_tensor(
                        out=modrow[0:1, c0:c0 + 512], in0=banks[j][0:1, :], in1=adab[0:1, c0:c0 + 512], op=ALU.add),
                        reads=[t_bank[j], t_adab], writes=[t_mod])
            for (o, sc_i, g_i) in ((0, 1, 0), (3, 4, 2)):
                b.op("dve", lambda e, o=o, sc_i=sc_i, g_i=g_i: e.scalar_tensor_tensor(
                    out=dv[0:1, o, :], in0=modrow[0:1, sc_i * D:(sc_i + 1) * D], scalar=1.0, in1=gainb[0:1, g_i, :],
                    op0=ALU.add, op1=ALU.mult), reads=[t_mod, t_adab], writes=[t_dv])
            for (o, sh_i) in ((1, 0), (4, 3)):
                b.op("dve", lambda e, o=o, sh_i=sh_i: e.tensor_copy(
                    out=dv[0:1, o, :], in_=modrow[0:1, sh_i * D:(sh_i + 1) * D]), reads=[t_mod], writes=[t_dv])
            for (o, gt_i, g_i) in ((2, 2, 1), (5, 5, 3)):
                b.op("dve", lambda e, o=o, gt_i=gt_i, g_i=g_i: e.tensor_tensor(
                    out=dv[0:1, o, :], in0=modrow[0:1, gt_i * D:(gt_i + 1) * D], in1=gainb[0:1, g_i, :], op=ALU.mult),
                    reads=[t_mod, t_adab], writes=[t_dv])
            b.dma(mv[q:q + 1, l, :, :], dv[:], reads=[t_dv], writes=[t_mv])
    b.barrier()
    b.sb = const_mark

    def load_cols(q, l, which, name):
        t = b.alloc([128, KC], F32, name)
        tt = T(name)
        b.dma(t[:], mv[q, l, which, :].rearrange("(kc p) -> p kc", p=128), reads=[t_mv], writes=[tt], slow=True)
        return t, tt

    def load_row(q, l, which, name):
        t = b.alloc([128, D], F32, name)
        tt = T(name)
        b.dma(t[:], mv[q, l, which:which + 1, :].partition_broadcast(128), reads=[t_mv], writes=[tt])
        return t, tt

    junk = b.alloc([128, D], BF, "junk")
    t_junk = T("junk")
    tp_ps = nc.alloc_psum_tensor("tp_ps", [128, KC, 128], BF)
    t_tp = T("tp")
    common_mark = b.sb

    def rstd_of(src_aps, reads, ss, t_ss):
        for i, ap in enumerate(src_aps):
            n = ap.shape[-1]
            b.op("act", lambda e, ap=ap, i=i, n=n: e.activation(out=junk[:, 0:n], in_=ap, func=AF.Square,
                                                            accum_out=ss[:, 1 + i:2 + i]),
                 reads=reads, writes=[t_junk, t_ss])
        if len(src_aps) == 2:
            b.op("dve", lambda e: e.tensor_tensor(out=ss[:, 1:2], in0=ss[:, 1:2], in1=ss[:, 2:3], op=ALU.add),
                 reads=[t_ss], writes=[t_ss])
        b.op("dve", lambda e: e.tensor_scalar(out=ss[:, 0:1], in0=ss[:, 1:2], scalar1=1.0 / D, scalar2=EPS,
                                              op0=ALU.mult, op1=ALU.add), reads=[t_ss], writes=[t_ss])
        b.op("act", lambda e: e.activation(out=ss[:, 3:4], in_=ss[:, 0:1], func=AF.Sqrt), reads=[t_ss], writes=[t_ss])
        b.op("dve", lambda e: e.reciprocal(out=ss[:, 0:1], in_=ss[:, 3:4]), reads=[t_ss], writes=[t_ss])

    def norm_stats(x_t, t_x, ss, t_ss, xn, t_xn):
        rstd_of([x_t[:, :]], [t_x], ss, t_ss)
        b.op("act", lambda e: e.activation(out=xn[:], in_=x_t[:, :], func=AF.Copy, scale=ss[:, 0:1]),
             reads=[t_x, t_ss], writes=[t_xn])

    def norm_tr(xn, t_xn):
        def tr(e):
            for kc in range(KC):
                ins = e.transpose(tp_ps[:, kc, :], xn[:, kc * 128:(kc + 1) * 128], ident[:])
            return ins
        b.op("pe", tr, reads=[t_xn, t_ident], writes=[t_tp])

    def norm_evac(acol, bcol, t_ab, dst, t_dst, tok0):
        for kc in range(KC):
            b.op("act", lambda e, kc=kc: e.activation(out=dst[:, kc, tok0:tok0 + 128], in_=tp_ps[:, kc, :],
                                                      func=AF.Identity, scale=acol[:, kc:kc + 1], bias=bcol[:, kc:kc + 1]),
                 reads=[t_tp, t_ab], writes=[t_dst])

    def norm_T(x_t, t_x, ss, t_ss, xn, t_xn, acol, bcol, t_ab, dst, t_dst, tok0):
        norm_stats(x_t, t_x, ss, t_ss, xn, t_xn)
        norm_tr(xn, t_xn)
        norm_evac(acol, bcol, t_ab, dst, t_dst, tok0)

    for seg in range(NSEG):
        b.barrier()
        b.sb = common_mark
        q = 0 if seg < 2 else 1
        acol, t_acol = load_cols(q, 0, 0, "acol")
        bcol, t_bcol = load_cols(q, 0, 1, "bcol")
        t_ab = T("ab")
        b.op("dve", lambda e: e.tensor_copy(out=acol[:, 0:1], in_=acol[:, 0:1]), reads=[t_acol, t_bcol], writes=[t_ab])
        hT = b.alloc([128, KC, TK], BF, "hT")
        t_hT = T("hT")
        xt = [b.alloc([128, D], F32, "xt%d" % i) for i in range(2)]
        t_xt = [T("xt0"), T("xt1")]
        ssA = [b.alloc([128, 4], F32, "ssA%d" % i) for i in range(2)]
        t_ssA = [T("ssA0"), T("ssA1")]
        xnA = [b.alloc([128, D], BF, "xnA%d" % i) for i in range(2)]
        t_xnA = [T("xnA0"), T("xnA1")]
        xsr = xsrc(seg)

        def a1_stats(t):
            b.dma(xt[t % 2][:], xsr[t * 128:(t + 1) * 128, :], writes=[t_xt[t % 2]])
            norm_stats(xt[t % 2], t_xt[t % 2], ssA[t % 2], t_ssA[t % 2], xnA[t % 2], t_xnA[t % 2])
        a1_stats(0)
        for t in range(NTK):
            norm_tr(xnA[t % 2], t_xnA[t % 2])
            if t + 1 < NTK:
                a1_stats(t + 1)
            norm_evac(acol, bcol, t_ab, hT, t_hT, t * 128)
        wst = b.alloc([128, KC, 3, 128], F32, "wst")
        t_wst = T("wst")
        wq = [b.alloc([128, KC, 3, 128], BF, "wq%d" % i) for i in range(2)]
        KT = [b.alloc([128, TK], BF, "KT%d" % i) for i in range(2)]
        QT = [b.alloc([128, TS], BF, "QT%d" % i) for i in range(2)]
        Vaug = [b.alloc([128, NTK, 2, 128], BF, "Vaug%d" % i) for i in range(2)]
        t_wq = [T("wq0"), T("wq1")]
        t_KT = [T("KT0"), T("KT1")]
        t_QT = [T("QT0"), T("QT1")]
        t_V = [T("V0"), T("V1")]
        Yst = b.alloc([128, 14 * 64], F32, "Yst")
        t_Yst = T("Yst")
        Ytf = [[b.alloc([128, 14 * 64], BF, "Ytf") for hh in range(2)] for i in range(2)]
        Yti = [[b.alloc([128, 14 * 64], BF, "Yti") for hh in range(2)] for i in range(2)]
        t_Yt = [[T("Yt") for hh in range(2)] for i in range(2)]
        cv = b.alloc([128, 64], F32, "cv")
        bandt = b.alloc([128, 14], F32, "bandt")
        rv = b.alloc([128, NQB, 6, 4], BF, "rv")
        rvf = b.alloc([128, NQB * 24], F32, "rvf")
        t_cv, t_rv = T("cv"), T("rv")
        b.dma(cv[:], cvm, writes=[t_cv])
        b.dma(bandt[:], bandm, writes=[t_cv])
        b.dma(rvf[:], rvx[seg], writes=[t_rv])
        b.op("dve", lambda e: e.tensor_copy(out=rv[:].rearrange("p a b c -> p (a b c)"), in_=rvf[:]),
             reads=[t_rv], writes=[t_rv])
        expS = [b.alloc([128, 6, 256], F32, "expS%d" % i) for i in range(2)]
        Pm = [b.alloc([128, 6, 256], BF, "Pm%d" % i) for i in range(2)]
        tmpP = b.alloc([128, 6, 256], BF, "tmpP")
        t_expS, t_Pm, t_tmpP = [T("expS0"), T("expS1")], [T("Pm0"), T("Pm1")], T("tmpP")
        rec = [b.alloc([128, 256], F32, "rec%d" % i) for i in range(2)]
        t_rec = [T("rec0"), T("rec1")]
        AO = [b.alloc([128, TS], BF, "AO%d" % i) for i in range(2)]
        t_AO = [T("AO0"), T("AO1")]
        t_Oh = [t_bank[3], t_bank[2]]
        for i in range(2):
            b.op("pool", lambda e, i=i: e.memset(Vaug[i][:, :, 0, 64:128], 1.0), writes=[t_V[i]])
            b.op("pool", lambda e, i=i: e.memset(Vaug[i][:, :, 1, 0:64], 1.0), writes=[t_V[i]])
        wv = w_qkv.rearrange("(kc p) (w n) -> p kc w n", p=128, w=3)

        def proj_items(hp, sl):
            items = []

            def it_w():
                for wi in range(3):
                    b.dma(wst[:, :, wi, :], wv[:, :, wi, hp * 128:(hp + 1) * 128], writes=[t_wst])
                b.op("pool", lambda e: e.tensor_copy(out=wq[sl][:], in_=wst[:]), reads=[t_wst], writes=[t_wq[sl]])
            items.append(it_w)
            for hh in range(2):
                def it_y(hh=hh):
                    b.dma(Yst[:], rpbY[2 * hp + hh], writes=[t_Yst])
                    b.op("act", lambda e: e.activation(out=Yst[:], in_=Yst[:], func=AF.Exp), reads=[t_Yst], writes=[t_Yst])
                    b.op("dve", lambda e: e.tensor_tensor(
                        out=Ytf[sl][hh][:].rearrange("p (m c) -> p m c", c=64), in0=Yst[:].rearrange("p (m c) -> p m c", c=64),
                        in1=cv[:].unsqueeze(1).to_broadcast([128, 14, 64]), op=ALU.mult),
                        reads=[t_Yst, t_cv], writes=[t_Yt[sl][hh]])
                    b.op("dve", lambda e: e.tensor_tensor(
                        out=Yti[sl][hh][:].rearrange("p (m c) -> p m c", c=64), in0=Ytf[sl][hh][:].rearrange("p (m c) -> p m c", c=64),
                        in1=bandt[:].unsqueeze(2).to_broadcast([128, 14, 64]), op=ALU.mult),
                        reads=[t_cv], writes=[t_Yt[sl][hh]])
                items.append(it_y)
            for (dst, t_dst, wi, ntok, off, scl) in ((KT[sl], t_KT[sl], 1, TK, 0, 1.0), (QT[sl], t_QT[sl], 0, TS, 256, 0.125)):
                for c in range(ntok // 512):
                    def it_kq(dst=dst, t_dst=t_dst, wi=wi, c=c, off=off, scl=scl):
                        bk = c % 2

                        def mm(e):
                            for kc in range(KC):
                                ins = e.matmul(banks[bk][:, :], lhsT=wq[sl][:, kc, wi, :],
                                               rhs=hT[:, kc, off + c * 512:off + (c + 1) * 512],
                                               start=(kc == 0), stop=(kc == KC - 1))
                            return ins
                        b.op("pe", mm, reads=[t_wq[sl], t_hT], writes=[t_bank[bk]])
                        b.op("act", lambda e: e.activation(out=dst[:, c * 512:(c + 1) * 512], in_=banks[bk][:, :],
                                                           func=AF.Copy, scale=scl), reads=[t_bank[bk]], writes=[t_dst])
                    items.append(it_kq)
            for t4 in range(NTK // 4):
                def it_v(t4=t4):
                    bk = t4 % 2

                    def mmv(e):
                        for j in range(4):
                            t = t4 * 4 + j
                            for kc in range(KC):
                                ins = e.matmul(banks[bk][:, j * 128:(j + 1) * 128], lhsT=hT[:, kc, t * 128:(t + 1) * 128],
                                               rhs=wq[sl][:, kc, 2, :], start=(kc == 0), stop=(kc == KC - 1))
                        return ins
                    b.op("pe", mmv, reads=[t_wq[sl], t_hT], writes=[t_bank[bk]])
                    pv = banks[bk][:, :].rearrange("p (j n) -> p j n", n=128)
                    b.op("act", lambda e: e.activation(out=Vaug[sl][:, t4 * 4:t4 * 4 + 4, 0, 0:64], in_=pv[:, :, 0:64], func=AF.Copy),
                         reads=[t_bank[bk]], writes=[t_V[sl]])
                    b.op("dve", lambda e: e.tensor_copy(out=Vaug[sl][:, t4 * 4:t4 * 4 + 4, 1, 64:128], in_=pv[:, :, 64:128]),
                         reads=[t_bank[bk]], writes=[t_V[sl]])
                items.append(it_v)
            return items

        for it in proj_items(0, 0):
            it()
        def attn_hp(hp, sl):
            pending = proj_items(hp + 1, 1 - sl) if hp + 1 < KC else []
            iters = [(hh, qb) for hh in range(2) for qb in range(NQB)]
            per = (len(pending) + len(iters) - 1) // len(iters)

            def scores(n):
                hh, qb = iters[n]
                p0 = 64 * hh

                def mms(e):
                    for kk in range(6):
                        k = 5 - kk
                        kt0 = (4 * qb + 2 * k) * 64
                        ins = e.matmul(banks[4 + kk // 2][:, (kk % 2) * 256:(kk % 2) * 256 + 256],
                                       lhsT=KT[sl][p0:p0 + 64, kt0:kt0 + 128],
                                       rhs=QT[sl][p0:p0 + 64, qb * 256:(qb + 1) * 256], start=True, stop=True)
                    return ins
                b.op("pe", mms, reads=[t_KT[sl], t_QT[sl]], writes=[t_bank[4], t_bank[5], t_bank[6]])

            def softmax(n):
                hh, qb = iters[n]
                bf_ = n % 2
                for j in range(3):
                    b.op("act", lambda e, j=j: e.activation(
                        out=expS[bf_][:, 2 * j:2 * j + 2, :], in_=banks[4 + j][:, :].rearrange("p (a n) -> p a n", n=256),
                        func=AF.Exp), reads=[t_bank[4 + j]], writes=[t_expS[bf_]])
                boundary = qb in (0, NQB - 1)
                Y = Ytf[sl][hh] if boundary else Yti[sl][hh]
                ywin = bass.AP(Y, Y[:].offset, [list(Y[:].ap[0]), [128, 6], [1, 256]])
                if boundary:
                    b.op("dve", lambda e: e.tensor_tensor(out=tmpP[:], in0=expS[bf_][:], in1=ywin, op=ALU.mult),
                         reads=[t_expS[bf_], t_Yt[sl][hh]], writes=[t_tmpP])
                    b.op("pool", lambda e: e.tensor_tensor(
                        out=Pm[bf_][:].rearrange("p k (r c) -> p k r c", c=64),
                        in0=tmpP[:].rearrange("p k (r c) -> p k r c", c=64),
                        in1=rv[:, qb, :, :].unsqueeze(3).to_broadcast([128, 6, 4, 64]), op=ALU.mult),
                        reads=[t_tmpP, t_rv], writes=[t_Pm[bf_]])
                else:
                    b.op("dve", lambda e: e.tensor_tensor(out=Pm[bf_][:], in0=expS[bf_][:], in1=ywin, op=ALU.mult),
                         reads=[t_expS[bf_], t_Yt[sl][hh]], writes=[t_Pm[bf_]])

            def pvmm(n):
                hh, qb = iters[n]
                bf_ = n % 2

                def mmo(e):
                    for kk in range(6):
                        k = 5 - kk
                        tt_ = 2 * qb + k
                        ins = e.matmul(banks[3 - bf_][:, 0:256], lhsT=Vaug[sl][:, tt_, hh, :], rhs=Pm[bf_][:, kk, :],
                                       start=(kk == 0), stop=(kk == 5))
                    return ins
                b.op("pe", mmo, reads=[t_V[sl], t_Pm[bf_]], writes=[t_Oh[bf_]])

            def normo(n):
                hh, qb = iters[n]
                bf_ = n % 2
                dn, dd = (64, 0) if hh == 0 else (0, 64)
                ob = banks[3 - bf_][:, 0:256]
                b.op("dve", lambda e: e.reciprocal(out=rec[bf_][dd:dd + 64, :], in_=ob[dn:dn + 64, :]),
                     reads=[t_Oh[bf_]], writes=[t_rec[bf_]])
                b.op("dve", lambda e: e.tensor_tensor(
                    out=AO[sl][dd:dd + 64, qb * 256:(qb + 1) * 256], in0=ob[dd:dd + 64, :],
                    in1=rec[bf_][dd:dd + 64, :], op=ALU.mult), reads=[t_Oh[bf_], t_rec[bf_]], writes=[t_AO[sl]])

            import os
            MODE = os.environ.get("ATT_MODE", "pipe")
            if MODE == "seq":
                for n in range(len(iters)):
                    scores(n)
                    softmax(n)
                    pvmm(n)
                    normo(n)
                while pending:
                    pending.pop(0)()
            elif MODE == "noproj":
                scores(0)
                for n in range(len(iters)):
                    softmax(n)
                    if n + 1 < len(iters):
                        scores(n + 1)
                    pvmm(n)
                    if n > 0:
                        normo(n - 1)
                normo(len(iters) - 1)
                while pending:
                    pending.pop(0)()
            else:
                scores(0)
                for n in range(len(iters)):
                    softmax(n)
                    if n + 1 < len(iters):
                        scores(n + 1)
                    for _ in range(per):
                        if pending:
                            pending.pop(0)()
                    pvmm(n)
                    if n > 0:
                        normo(n - 1)
                normo(len(iters) - 1)
                while pending:
                    pending.pop(0)()
            b.dma(aoT[seg, hp], AO[sl][:], reads=[t_AO[sl]], writes=[t_ao[seg]])

        for hp in range(KC):
            attn_hp(hp, hp % 2)

        b.barrier()
        b.sb = common_mark
        wo_st = b.alloc([128, KC, D], F32, "wo_st")
        wo = b.alloc([128, KC, D], BF, "wo")
        t_wo = T("wo")
        b.dma(wo_st[:], w_o.rearrange("(kc p) n -> p kc n", p=128), writes=[t_wo])
        b.op("pool", lambda e: e.tensor_copy(out=wo[:], in_=wo_st[:]), reads=[t_wo], writes=[t_wo])
        grow, t_grow = load_row(q, 0, 2, "grow")
        aot = [b.alloc([128, KC, 128], BF, "aot%d" % i) for i in range(2)]
        t_aot = [T("aot0"), T("aot1")]
        xt = [b.alloc([128, D], F32, "xt%d" % i) for i in range(2)]
        t_xt = [T("xt0"), T("xt1")]
        tmpW = [b.alloc([128, D], F32, "tmpW%d" % i) for i in range(2)]
        t_tmpW = [T("tmpW0"), T("tmpW1")]
        x1t = [b.alloc([128, D], F32, "x1t%d" % i) for i in range(2)]
        t_x1t = [T("x1t0"), T("x1t1")]
        ssW = [b.alloc([128, 4], F32, "ssW%d" % i) for i in range(2)]
        t_ssW = [T("ssW0"), T("ssW1")]
        for tt in range(NTS):
            a, ta, x_, tx = aot[tt % 2], t_aot[tt % 2], xt[tt % 2], t_xt[tt % 2]
            b0 = 2 * (tt % 2)
            tmp, t_tmp, ss, t_ss = tmpW[tt % 2], t_tmpW[tt % 2], ssW[tt % 2], t_ssW[tt % 2]
            b.dma(a[:], aoT[seg, :, :, tt * 128:(tt + 1) * 128].rearrange("k p t -> p k t"), reads=[t_ao[seg]], writes=[ta])
            b.dma(x_[:], xsr[256 + tt * 128:256 + (tt + 1) * 128, :], writes=[tx])

            def mmw(e, a=a, b0=b0):
                for nh in range(2):
                    for hp in range(KC):
                        ins = e.matmul(banks[b0 + nh][:, :], lhsT=a[:, hp, :], rhs=wo[:, hp, nh * 512:(nh + 1) * 512],
                                       start=(hp == 0), stop=(hp == KC - 1))
                return ins
            b.op("pe", mmw, reads=[ta, t_wo], writes=[t_bank[b0], t_bank[b0 + 1]])
            rstd_of([banks[b0][:, :], banks[b0 + 1][:, :]], [t_bank[b0], t_bank[b0 + 1]], ss, t_ss)
            for nh in range(2):
                b.op("dve", lambda e, nh=nh, b0=b0, tmp=tmp, ss=ss: e.scalar_tensor_tensor(
                    out=tmp[:, nh * 512:(nh + 1) * 512], in0=banks[b0 + nh][:, :], scalar=ss[:, 0:1],
                    in1=grow[:, nh * 512:(nh + 1) * 512], op0=ALU.mult, op1=ALU.mult),
                    reads=[t_bank[b0 + nh], t_ss, t_grow], writes=[t_tmp])
            o, to = x1t[tt % 2], t_x1t[tt % 2]
            b.op("pool", lambda e, o=o, x_=x_, tmp=tmp: e.tensor_tensor(out=o[:], in0=x_[:], in1=tmp[:], op=ALU.add),
                 reads=[tx, t_tmp], writes=[to])
            b.dma(x1d[seg, tt * 128:(tt + 1) * 128, :], o[:], reads=[to], writes=[t_x1[seg]])

    def ffn_phase(l, srcs, t_srcs, dsts, t_dsts):
        b.barrier()
        b.sb = common_mark
        w1b = b.alloc([128, KC, DFF], BF, "w1b")
        w2b = b.alloc([128, 32, D], BF, "w2b")
        t_w1, t_w2 = T("w1"), T("w2")
        mark = b.sb
        stg = [b.alloc([128, DFF], F32, "stg%d" % i) for i in range(2)]
        t_stg = [T("stg0"), T("stg1")]
        w1v = ffn_w1[l].rearrange("(kc p) n -> p kc n", p=128)
        w2v = ffn_w2[l].rearrange("(k p) n -> p k n", p=128)
        for kc in range(KC):
            st, ts_ = stg[kc % 2], t_stg[kc % 2]
            b.dma(st[:], w1v[:, kc, :], writes=[ts_])
            b.op("pool" if kc % 2 else "dve", lambda e, st=st, kc=kc: e.tensor_copy(out=w1b[:, kc, :], in_=st[:]),
                 reads=[ts_], writes=[t_w1])
        for k4 in range(8):
            st, ts_ = stg[k4 % 2], t_stg[k4 % 2]
            b.dma(st[:].rearrange("p (k n) -> p k n", n=D), w2v[:, k4 * 4:(k4 + 1) * 4, :], writes=[ts_])
            b.op("pool" if k4 % 2 else "dve", lambda e, st=st, k4=k4: e.tensor_copy(
                out=w2b[:, k4 * 4:(k4 + 1) * 4, :], in_=st[:].rearrange("p (k n) -> p k n", n=D)),
                reads=[ts_], writes=[t_w2])
        b.barrier()
        b.sb = mark
        x1t = [b.alloc([128, D], F32, "fx%d" % i) for i in range(4)]
        t_x1t = [T("fx%d" % i) for i in range(4)]
        h2T = b.alloc([128, KC, 512], BF, "h2T")
        t_h2T = T("h2T")
        hid = b.alloc([128, 32, 512], BF, "hid")
        t_hid = T("hid")
        rl = [b.alloc([128, 512], BF, "rl%d" % i) for i in range(2)]
        t_rl = [T("rl0"), T("rl1")]
        tmpF = [b.alloc([128, D], F32, "ftmp%d" % i) for i in range(2)]
        t_tmpF = [T("ftmp0"), T("ftmp1")]
        ssF = [b.alloc([128, 4], F32, "fss%d" % i) for i in range(2)]
        t_ssF = [T("fss0"), T("fss1")]
        ss = b.alloc([128, 4], F32, "fss")
        t_ss = T("fss")
        xn = b.alloc([128, D], BF, "fxn")
        t_xn = T("fxn")
        xn2 = b.alloc([128, D], BF, "fxn2")
        ss2 = b.alloc([128, 4], F32, "fss2")
        xnN, t_xnN = [xn, xn2], [t_xn, T("fxn2")]
        ssN, t_ssN = [ss, ss2], [t_ss, T("fss2")]
        mark2 = b.sb
        oi = 0
        for seg in range(len(srcs)):
            q = 0 if seg < 2 else 1
            if seg in (0, 2):
                if seg == 2:
                    b.barrier()
                b.sb = mark2
                acol, t_acol = load_cols(q, l, 3, "facol")
                bcol, t_bcol = load_cols(q, l, 4, "fbcol")
                t_ab = T("fab")
                b.op("dve", lambda e, acol=acol: e.tensor_copy(out=acol[:, 0:1], in_=acol[:, 0:1]),
                     reads=[t_acol, t_bcol], writes=[t_ab])
                grow, t_grow = load_row(q, l, 5, "fgrow")
            for blk in range(TS // 512):
                def f_stats(tt, blk=blk, seg=seg):
                    r0 = blk * 512 + tt * 128
                    b.dma(x1t[tt][:], srcs[seg][r0:r0 + 128, :], reads=[t_srcs[seg]], writes=[t_x1t[tt]])
                    norm_stats(x1t[tt], t_x1t[tt], ssN[tt % 2], t_ssN[tt % 2], xnN[tt % 2], t_xnN[tt % 2])
                f_stats(0)
                for tt in range(4):
                    norm_tr(xnN[tt % 2], t_xnN[tt % 2])
                    if tt + 1 < 4:
                        f_stats(tt + 1)
                    norm_evac(acol, bcol, t_ab, h2T, t_h2T, tt * 128)
                for m in range(32):
                    bk = m % 2

                    def mmu(e, m=m, bk=bk):
                        for kc in range(KC):
                            ins = e.matmul(banks[bk][:, :], lhsT=w1b[:, kc, m * 128:(m + 1) * 128], rhs=h2T[:, kc, :],
                                           start=(kc == 0), stop=(kc == KC - 1))
                        return ins
                    b.op("pe", mmu, reads=[t_w1, t_h2T], writes=[t_bank[bk]])
                    b.op("act", lambda e, bk=bk: e.activation(out=rl[bk][:], in_=banks[bk][:, :], func=AF.Relu),
                         reads=[t_bank[bk]], writes=[t_rl[bk]])
                    b.op("pool", lambda e, m=m, bk=bk: e.tensor_tensor(out=hid[:, m, :], in0=rl[bk][:], in1=rl[bk][:], op=ALU.mult),
                         reads=[t_rl[bk]], writes=[t_hid])
                for tt in range(4):
                    b0 = 2 + 2 * (tt % 2)
                    tmp_, t_tmp_, ss_, t_ss_ = tmpF[tt % 2], t_tmpF[tt % 2], ssF[tt % 2], t_ssF[tt % 2]

                    def mmd(e, tt=tt, b0=b0):
                        for nh in range(2):
                            for k in range(32):
                                ins = e.matmul(banks[b0 + nh][:, :], lhsT=hid[:, k, tt * 128:(tt + 1) * 128],
                                               rhs=w2b[:, k, nh * 512:(nh + 1) * 512], start=(k == 0), stop=(k == 31))
                        return ins
                    b.op("pe", mmd, reads=[t_hid, t_w2], writes=[t_bank[b0], t_bank[b0 + 1]])
                    rstd_of([banks[b0][:, :], banks[b0 + 1][:, :]], [t_bank[b0], t_bank[b0 + 1]], ss_, t_ss_)
                    for nh in range(2):
                        b.op("dve", lambda e, nh=nh, b0=b0, tmp_=tmp_, ss_=ss_: e.scalar_tensor_tensor(
                            out=tmp_[:, nh * 512:(nh + 1) * 512], in0=banks[b0 + nh][:, :], scalar=ss_[:, 0:1],
                            in1=grow[:, nh * 512:(nh + 1) * 512], op0=ALU.mult, op1=ALU.mult),
                            reads=[t_bank[b0 + nh], t_ss_, t_grow], writes=[t_tmp_])
                    b.op("pool", lambda e, tt=tt, tmp_=tmp_: e.tensor_tensor(out=tmp_[:], in0=x1t[tt][:], in1=tmp_[:], op=ALU.add),
                         reads=[t_x1t[tt], t_tmp_], writes=[t_tmp_])
                    r0 = blk * 512 + tt * 128
                    b.dma(dsts[seg][r0:r0 + 128, :], tmp_[:], reads=[t_tmp_], writes=[t_dsts[seg]])

    ffn_phase(0, [x1d[i] for i in range(NSEG)], t_x1, [x2d[i] for i in range(NSEG)], t_x2)
    if upto == "l0":
        b.barrier()
        return nc, b

    import math
    I32 = mybir.dt.int32
    NCH = TS // 8
    NB = NCH // 16
    NCT = NCH // 128
    LOG2NC = int(round(math.log2(NCH)))
    a_re_in = din("s5_a_re", [2, 64, 64])
    a_im_in = din("s5_a_im", [2, 64, 64])
    ldt_in = din("s5_log_dt", [2, 64])
    b_in = [din("s5_b_re", [2, 64, 64, 16]), din("s5_b_im", [2, 64, 64, 16])]
    c_in = [din("s5_c_re", [2, 64, 16, 64]), din("s5_c_im", [2, 64, 16, 64])]
    d_in = din("s5_d", [D])
    wglu_in = din("s5_w_glu", [D, 2 * D])
    mfb_in = din("mfb", [2, 128, 128])
    sel_in = din("sel", [128, 8])
    winD = nc.dram_tensor("winD", [8, 128, 8 * 2 * 2 * 64], BF).ap()
    woutD = nc.dram_tensor("woutD", [8, 128, 2 * 2 * 4 * 128], BF).ap()
    wtoepD = nc.dram_tensor("wtoepD", [8, 128, 8 * 128], BF).ap()
    ytd = nc.dram_tensor("ytd", [3, TS, D], BF).ap()
    x3d = nc.dram_tensor("x3d", [3, TS, D], F32, kind="ExternalOutput" if debug else "Internal").ap()
    t_wD = T("wD")
    t_yt = [T("yt%d" % i) for i in range(3)]
    t_x3 = [T("x3_%d" % i) for i in range(3)]

    def tt(eng, out, in0, in1, op, reads, writes):
        return b.op(eng, lambda e: e.tensor_tensor(out=out, in0=in0, in1=in1, op=op), reads=reads, writes=writes)

    def tsc(eng, out, in0, s1, s2, op0, op1, reads, writes):
        if op1 is None:
            return b.op(eng, lambda e: e.tensor_scalar(out=out, in0=in0, scalar1=s1, scalar2=None, op0=op0), reads=reads, writes=writes)
        return b.op(eng, lambda e: e.tensor_scalar(out=out, in0=in0, scalar1=s1, scalar2=s2, op0=op0, op1=op1), reads=reads, writes=writes)

    def cp(eng, out, in_, reads, writes):
        if eng == "act":
            return b.op(eng, lambda e: e.activation(out=out, in_=in_, func=AF.Copy), reads=reads, writes=writes)
        return b.op(eng, lambda e: e.tensor_copy(out=out, in_=in_), reads=reads, writes=writes)

    def cmul(eng, o_r, o_i, ar, ai, br, bi, t1, t2, tk):
        tt(eng, t1, ar, br, ALU.mult, tk, tk)
        tt(eng, t2, ai, bi, ALU.mult, tk, tk)
        tt(eng, o_r, t1, t2, ALU.subtract, tk, tk)
        tt(eng, t1, ar, bi, ALU.mult, tk, tk)
        tt(eng, t2, ai, br, ALU.mult, tk, tk)
        tt(eng, o_i, t1, t2, ALU.add, tk, tk)

    def cmad(eng, d_r, d_i, cr, ci, s_r, s_i, t1, reads, writes):
        rw = list(reads) + list(writes)
        tt(eng, t1, cr, s_r, ALU.mult, rw, writes)
        tt(eng, d_r, d_r, t1, ALU.add, rw, writes)
        tt(eng, t1, ci, s_i, ALU.mult, rw, writes)
        tt(eng, d_r, d_r, t1, ALU.subtract, rw, writes)
        tt(eng, t1, cr, s_i, ALU.mult, rw, writes)
        tt(eng, d_i, d_i, t1, ALU.add, rw, writes)
        tt(eng, t1, ci, s_r, ALU.mult, rw, writes)
        tt(eng, d_i, d_i, t1, ALU.add, rw, writes)

    b.barrier()
    b.sb = common_mark
    L8 = b.alloc([128, 2, 2, 32], F32, "L8")
    L128 = b.alloc([128, 2, 2, 32], F32, "L128")
    LSEG = b.alloc([128, 2, 2, 32], F32, "LSEG")
    selt = b.alloc([128, 8], F32, "selt")
    CA8 = b.alloc([128, 32, 2, 2, 2], F32, "CA8")
    CA128 = b.alloc([128, 32, 2, 2, 2], F32, "CA128")
    HIN = b.alloc([128, 3, 32, 2, 2], F32, "HIN")
    t_Lt, t_sel, t_HIN = T("Lt"), T("sel"), T("HIN")
    b.dma(selt[:], sel_in, writes=[t_sel])
    s5_mark = b.sb

    tP = T("prep")
    P = [tP]
    AR = b.alloc([128, 2, 32], F32, "AR")
    AI = b.alloc([128, 2, 32], F32, "AI")
    DT = b.alloc([128, 2, 32], F32, "DT")
    for d in range(2):
        for g2 in range(2):
            b.dma(AR[64 * g2:64 * g2 + 64, d, :], a_re_in[d].rearrange("(gq g2) p -> g2 p gq", g2=2)[g2], writes=P, slow=True)
            b.dma(AI[64 * g2:64 * g2 + 64, d, :], a_im_in[d].rearrange("(gq g2) p -> g2 p gq", g2=2)[g2], writes=P, slow=True)
            b.dma(DT[64 * g2:64 * g2 + 64, d:d + 1, :],
                  ldt_in[d:d + 1, :].rearrange("o (gq g2) -> o g2 gq", g2=2)[:, g2, :].partition_broadcast(64), writes=P, slow=True)
    b.op("act", lambda e: e.activation(out=DT[:], in_=DT[:], func=AF.Exp), reads=P, writes=P)
    XR = b.alloc([128, 2, 32], F32, "XR")
    TH = b.alloc([128, 2, 32], F32, "TH")
    tt("dve", XR[:], AR[:], DT[:], ALU.mult, P, P)
    tt("dve", TH[:], AI[:], DT[:], ALU.mult, P, P)
    PW = b.alloc([128, 9, 2, 2, 32], F32, "PW")
    mg = b.alloc([128, 2, 32], F32, "mg")
    arg = b.alloc([128, 2, 32], F32, "arg")
    uu = b.alloc([128, 2, 32], F32, "uu")
    ni = b.alloc([128, 2, 32], I32, "ni")
    nf = b.alloc([128, 2, 32], F32, "nf")
    for k in range(9):
        b.op("act", lambda e, k=k: e.activation(out=mg[:], in_=XR[:], func=AF.Exp, scale=float(k)), reads=P, writes=P)
        for ri_idx, ph in ((1, 0.0), (0, math.pi / 2)):
            tsc("dve", arg[:], TH[:], float(k), ph, ALU.mult, ALU.add, P, P)
            tsc("dve", uu[:], arg[:], 1.0 / (2 * math.pi), None, ALU.mult, None, P, P)
            cp("dve", ni[:], uu[:], P, P)
            cp("dve", nf[:], ni[:], P, P)
            b.op("dve", lambda e: e.scalar_tensor_tensor(out=uu[:], in0=nf[:], scalar=-2 * math.pi, in1=arg[:],
                                                         op0=ALU.mult, op1=ALU.add), reads=P, writes=P)
            b.op("act", lambda e: e.activation(out=nf[:], in_=uu[:], func=AF.Sin), reads=P, writes=P)
            tt("dve", PW[:, k, ri_idx, :, :], mg[:], nf[:], ALU.mult, P, P)
    EIN = b.alloc([128, 2, 2, 32, 8], F32, "EIN")
    EOUT = b.alloc([128, 2, 2, 32, 8], F32, "EOUT")
    for k in range(9):
        if k <= 7:
            cp("pool", EIN[:, :, 0, :, 7 - k], PW[:, k, :, 0, :], P, P)
            cp("pool", EIN[:, :, 1, :, k], PW[:, k, :, 1, :], P, P)
        if k >= 1:
            cp("pool", EOUT[:, :, 0, :, k - 1], PW[:, k, :, 0, :], P, P)
            cp("pool", EOUT[:, :, 1, :, 8 - k], PW[:, k, :, 1, :], P, P)
    zr = b.alloc([128, 2, 32], F32, "zr")
    zi = b.alloc([128, 2, 32], F32, "zi")
    q1 = b.alloc([128, 2, 32], F32, "q1")
    q2 = b.alloc([128, 2, 32], F32, "q2")
    q3 = b.alloc([128, 2, 32], F32, "q3")
    nr = b.alloc([128, 2, 32], F32, "nr")
    tsc("dve", nr[:], PW[:, 1, 0, :, :], -1.0, None, ALU.add, None, P, P)
    nim = PW[:, 1, 1, :, :]
    tt("dve", q1[:], AR[:], AR[:], ALU.mult, P, P)
    tt("dve", q2[:], AI[:], AI[:], ALU.mult, P, P)
    tt("dve", q1[:], q1[:], q2[:], ALU.add, P, P)
    b.op("dve", lambda e: e.reciprocal(out=q3[:], in_=q1[:]), reads=P, writes=P)
    tt("dve", q1[:], nr[:], AR[:], ALU.mult, P, P)
    tt("dve", q2[:], nim, AI[:], ALU.mult, P, P)
    tt("dve", q1[:], q1[:], q2[:], ALU.add, P, P)
    tt("dve", zr[:], q1[:], q3[:], ALU.mult, P, P)
    tt("dve", q1[:], nim, AR[:], ALU.mult, P, P)
    tt("dve", q2[:], nr[:], AI[:], ALU.mult, P, P)
    tt("dve", q1[:], q1[:], q2[:], ALU.subtract, P, P)
    tt("dve", zi[:], q1[:], q3[:], ALU.mult, P, P)
    PL = [tP, t_Lt]
    cp("dve", L8[:], PW[:, 8, :, :, :], PL, PL)
    I8 = b.alloc([128, 2, 2, 32], F32, "I8")
    tt("dve", q1[:], L8[:, 0], L8[:, 0], ALU.mult, PL, P)
    tt("dve", q2[:], L8[:, 1], L8[:, 1], ALU.mult, PL, P)
    tt("dve", q1[:], q1[:], q2[:], ALU.add, P, P)
    b.op("dve", lambda e: e.reciprocal(out=q3[:], in_=q1[:]), reads=P, writes=P)
    tt("dve", I8[:, 0], L8[:, 0], q3[:], ALU.mult, PL, P)
    tt("dve", q1[:], L8[:, 1], q3[:], ALU.mult, PL, P)
    tsc("dve", I8[:, 1], q1[:], -1.0, None, ALU.mult, None, P, P)
    sqa = b.alloc([128, 2, 2, 32], F32, "sqa")
    sqb = b.alloc([128, 2, 2, 32], F32, "sqb")
    cur = L8
    for it in range(LOG2NC):
        dst = L128 if it == 3 else (LSEG if it == LOG2NC - 1 else (sqa if cur is not sqa else sqb))
        tt("dve", q1[:], cur[:, 0], cur[:, 0], ALU.mult, PL, P)
        tt("dve", q2[:], cur[:, 1], cur[:, 1], ALU.mult, PL, P)
        tt("dve", dst[:, 0], q1[:], q2[:], ALU.subtract, PL, PL)
        tt("dve", q1[:], cur[:, 0], cur[:, 1], ALU.mult, PL, P)
        tsc("dve", dst[:, 1], q1[:], 2.0, None, ALU.mult, None, PL, PL)
        cur = dst
    if upto == "prepA":
        b.barrier()
        return nc, b
    Braw = b.alloc([128, 2, 2, 32, 16], F32, "Braw")
    Bb = b.alloc([128, 2, 2, 32, 16], F32, "Bb")
    for ri in range(2):
        for d in range(2):
            for g2 in range(2):
                b.dma(Braw[64 * g2:64 * g2 + 64, ri, d, :, :],
                      b_in[ri][d].rearrange("(gq g2) p j -> g2 p gq j", g2=2)[g2], writes=P)
    bt1 = b.alloc([128, 32, 16], F32, "bt1")
    bt2 = b.alloc([128, 32, 16], F32, "bt2")
    for d in range(2):
        zrb = zr[:, d, :].unsqueeze(2).to_broadcast([128, 32, 16])
        zib = zi[:, d, :].unsqueeze(2).to_broadcast([128, 32, 16])
        cmul("dve", Bb[:, 0, d], Bb[:, 1, d], zrb, zib, Braw[:, 0, d], Braw[:, 1, d], bt1[:], bt2[:], P)
    Cn = b.alloc([128, 8, 2, 2, 64], F32, "Cn")
    Ct = b.alloc([128, 2, 2, 32, 16], F32, "Ct")
    for ri in range(2):
        for d in range(2):
            b.dma(Cn[:, :, ri, d, :], c_in[ri][d].rearrange("(gb gl) i p -> (gl i) gb p", gl=8), writes=P)
    for gb in range(8):
        bk = gb % 2

        def trc(e, gb=gb, bk=bk):
            for ri in range(2):
                for d in range(2):
                    c4 = ri * 2 + d
                    ins = e.transpose(banks[bk][0:64, c4 * 128:(c4 + 1) * 128], Cn[:, gb, ri, d, :], ident_f[:, :])
            return ins
        b.op("pe", trc, reads=P + [t_ident], writes=[t_bank[bk]])
        for g2 in range(2):
            src = banks[bk][0:64, :].rearrange("p (c q g i) -> p c q g i", c=4, q=4, g=2)[:, :, :, g2, :]
            dst = Ct[64 * g2:64 * g2 + 64, :, :, 4 * gb:4 * gb + 4, :].rearrange("p r d q i -> p (r d) q i")
            cp("act", dst, src, [t_bank[bk]], P)
    if upto == "prepB":
        b.barrier()
        return nc, b
    dcol = b.alloc([128, 64], F32, "dcol")
    for s_ in range(8):
        b.dma(dcol[16 * s_:16 * s_ + 16, :], d_in.rearrange("(g j) -> j g", j=16), writes=P, slow=True)
    MF = b.alloc([128, 128], F32, "MF")
    MB = b.alloc([128, 128], F32, "MB")
    b.dma(MF[:], mfb_in[0], writes=P)
    b.dma(MB[:], mfb_in[1], writes=P)
    XW = b.alloc([128, 2, 2, 4, 8, 16], F32, "XW")
    XI = b.alloc([128, 2, 2, 4, 8, 16], F32, "XI")
    WoF = b.alloc([128, 2, 2, 4, 8, 16], F32, "WoF")
    XIz = [b.alloc([128, 2, 2, 4, 8, 16], F32, "XIz%d" % i) for i in range(2)]
    w1_ = b.alloc([128, 4, 8, 16], F32, "w1_")
    w2_ = b.alloc([128, 4, 8, 16], F32, "w2_")
    wob = b.alloc([128, 2 * 2 * 4 * 128], BF, "wob")
    wib = b.alloc([128, 8 * 2 * 2 * 64], BF, "wib")
    wtb = b.alloc([128, 8, 128], BF, "wtb")
    tq1 = b.alloc([128, 128], F32, "tq1")
    tq2 = b.alloc([128, 128], F32, "tq2")
    t_wo_, t_wi_, t_wt_ = T("wob"), T("wib"), T("wtb")
    SH = [128, 4, 8, 16]
    for gb in range(8):
        gqs = slice(4 * gb, 4 * gb + 4)
        for d in range(2):
            er = EIN[:, 0, d, gqs, :].unsqueeze(3).to_broadcast(SH)
            ei = EIN[:, 1, d, gqs, :].unsqueeze(3).to_broadcast(SH)
            br_ = Bb[:, 0, d, gqs, :].unsqueeze(2).to_broadcast(SH)
            bi_ = Bb[:, 1, d, gqs, :].unsqueeze(2).to_broadcast(SH)
            cmul("dve", XW[:, 0, d], XW[:, 1, d], er, ei, br_, bi_, w1_[:], w2_[:], P)
            ir = I8[:, 0, d, gqs].unsqueeze(2).unsqueeze(3).to_broadcast(SH)
            ii = I8[:, 1, d, gqs].unsqueeze(2).unsqueeze(3).to_broadcast(SH)
            cmul("pool", XI[:, 0, d], XI[:, 1, d], XW[:, 0, d], XW[:, 1, d], ir, ii, w1_[:], w2_[:], P)
            cr = Ct[:, 0, d, gqs, :].unsqueeze(2).to_broadcast(SH)
            ci = Ct[:, 1, d, gqs, :].unsqueeze(2).to_broadcast(SH)
            eor = EOUT[:, 0, d, gqs, :].unsqueeze(3).to_broadcast(SH)
            eoi = EOUT[:, 1, d, gqs, :].unsqueeze(3).to_broadcast(SH)
            cmul("dve", WoF[:, 0, d], w2_[:], cr, ci, eor, eoi, w1_[:], WoF[:, 1, d], P)
            tsc("dve", WoF[:, 1, d], w2_[:], -1.0, None, ALU.mult, None, P, P)
        if upto == "prepC":
            break
        cp("act", wob[:], WoF[:].rearrange("p r d q s i -> p (r d q s i)"), P, [t_wo_])
        b.dma(woutD[gb], wob[:], reads=[t_wo_], writes=[t_wD])
        if upto == "prepD":
            continue
        for hb_ in range(4):
            def trw(e, hb_=hb_):
                for ri in range(2):
                    for d in range(2):
                        col = (ri * 2 + d) * 128
                        ins = e.transpose(banks[hb_][:, col:col + 128],
                                          XW[:, ri, d, hb_, :, :].rearrange("p s j -> p (s j)"), ident_f[:, :])
                return ins
            b.op("pe", trw, reads=P + [t_ident], writes=[t_bank[hb_]])
            cp("act", wib[:, hb_ * 512:(hb_ + 1) * 512].rearrange("p (g c n) -> p c g n", g=2, c=4),
               banks[hb_][:, :].rearrange("p (c g n) -> p c g n", c=4, g=2), [t_bank[hb_]], [t_wi_])
        b.dma(winD[gb], wib[:], reads=[t_wi_], writes=[t_wD])
        if upto == "prepE":
            continue
        for g2 in range(2):
            b.op("pool", lambda e, g2=g2: e.memset(XIz[g2][:], 0.0), reads=P, writes=P)
            cp("pool", XIz[g2][64 * g2:64 * g2 + 64], XI[64 * g2:64 * g2 + 64], P, P)
        for gl in range(8):
            g2, glq = gl % 2, gl // 2
            g = 8 * gb + gl
            bk = 4 + gl % 2

            def mmt(e, g2=g2, glq=glq, bk=bk):
                for d in range(2):
                    for ri in range(2):
                        ins = e.matmul(banks[bk][:, d * 128:(d + 1) * 128],
                                       lhsT=XIz[g2][:, ri, d, glq, :, :].rearrange("p s j -> p (s j)"),
                                       rhs=WoF[:, ri, d, glq, :, :].rearrange("p s i -> p (s i)"),
                                       start=(ri == 0), stop=(ri == 1))
                return ins
            b.op("pe", mmt, reads=P, writes=[t_bank[bk]])
            tt("dve", tq1[:], banks[bk][:, 0:128], MF[:], ALU.mult, P + [t_bank[bk]], P)
            tt("dve", tq2[:], banks[bk][:, 128:256], MB[:], ALU.mult, P + [t_bank[bk]], P)
            tt("dve", tq1[:], tq1[:], tq2[:], ALU.add, P, P)
            b.op("dve", lambda e, gl=gl, g=g: e.scalar_tensor_tensor(
                out=wtb[:, gl, :], in0=ident_f[:, :], scalar=dcol[:, g:g + 1], in1=tq1[:], op0=ALU.mult, op1=ALU.add),
                reads=P + [t_ident], writes=[t_wt_])
        b.dma(wtoepD[gb], wtb[:].rearrange("p g n -> p (g n)"), reads=[t_wt_], writes=[t_wD])

    if upto == "prep":
        b.barrier()
        return nc, b
    b.barrier()
    b.sb = s5_mark
    arow1b = b.alloc([128, D], F32, "arow1")
    brow1b = b.alloc([128, D], F32, "brow1")
    t_ab1b = T("ab1")
    arow1 = [arow1b, arow1b]
    brow1 = [brow1b, brow1b]
    t_ab1 = [t_ab1b, t_ab1b]
    curq = [None]

    def set_q(q):
        if curq[0] == q:
            return
        curq[0] = q
        b.dma(arow1b[:], mv[q, 1, 0:1, :].partition_broadcast(128), reads=[t_mv], writes=[t_ab1b])
        b.dma(brow1b[:], mv[q, 1, 1:2, :].partition_broadcast(128), reads=[t_mv], writes=[t_ab1b])
    EA = b.alloc([128, 8, 32, 2, 2], F32, "EA")
    t_EA = T("EA")
    HBs = [b.alloc([128, 32, 2, 2, NB], F32, "HB%d" % i) for i in range(3)]
    t_HB = [T("HB%d" % i) for i in range(3)]
    s5_mark2 = b.sb

    U8 = b.alloc([128, 64, NCH], BF, "U8")
    t_U8 = T("U8")
    S_off = b.sb
    S = b.alloc([128, 16, 2, 2, max(NCH, 192)], F32, "S")
    S_end = b.sb
    SCW = max(NCH, 192)
    t_S = [T("S_f"), T("S_b")]
    tmpA = [b.alloc([128, 16, 2, NB], F32, "tmpA%d" % i) for i in range(2)]
    tmpA2 = [b.alloc([128, 16, 2, NB], F32, "tmpA2%d" % i) for i in range(2)]
    tmpB = [b.alloc([128, 16, NB], F32, "tmpB%d" % i) for i in range(2)]
    t_tm = [T("tm0"), T("tm1")]
    winb = [b.alloc([128, 8, 2, 2, 64], BF, "winb0")] * 2
    t_winb = [T("winb0")] * 2
    HBtmp = b.alloc([128, 32, 2, 2, NB], F32, "HBtmp")
    t_HBtmp = T("HBtmp")
    sm_t = b.alloc([128, 32, 2], F32, "sm_t")
    s5_mark3 = b.sb
    SENG = ("dve", "pool")
    wcount = [0]

    def build_U8(src, t_src, q):
        set_q(q)
        mk = b.sb
        b.sb = S_off
        xc = b.alloc([128, 8, D], F32, "xc")
        hb = b.alloc([128, 64, 8, 16], BF, "hb")
        assert b.sb <= S_end
        b.sb = mk
        ss8 = b.alloc([128, 24], F32, "ss8")
        t_xc, t_hb, t_s8 = t_S[0], t_S[1], T("ss8")
        for ct in range(NCT):
            b.dma(xc[:], src[ct * 1024:(ct + 1) * 1024, :].rearrange("(c s) d -> c s d", s=8), reads=[t_src], writes=[t_xc])
            for s_ in range(8):
                b.op("act", lambda e, s_=s_: e.activation(out=junk[:], in_=xc[:, s_, :], func=AF.Square,
                                                          accum_out=ss8[:, s_:s_ + 1]), reads=[t_xc], writes=[t_junk, t_s8])
            tsc("dve", ss8[:, 8:16], ss8[:, 0:8], 1.0 / D, EPS, ALU.mult, ALU.add, [t_s8], [t_s8])
            b.op("act", lambda e: e.activation(out=ss8[:, 16:24], in_=ss8[:, 8:16], func=AF.Sqrt), reads=[t_s8], writes=[t_s8])
            b.op("dve", lambda e: e.reciprocal(out=ss8[:, 8:16], in_=ss8[:, 16:24]), reads=[t_s8], writes=[t_s8])
            for s_ in range(8):
                b.op("dve", lambda e, s_=s_: e.scalar_tensor_tensor(
                    out=xc[:, s_, :], in0=xc[:, s_, :], scalar=ss8[:, 8 + s_:9 + s_], in1=arow1[q][:],
                    op0=ALU.mult, op1=ALU.mult), reads=[t_xc, t_s8, t_ab1[q]], writes=[t_xc])
            for s_ in range(8):
                b.op("pool", lambda e, s_=s_: e.tensor_tensor(
                    out=hb[:, :, s_, :], in0=xc[:, s_, :].rearrange("c (g j) -> c g j", j=16),
                    in1=brow1[q][:].rearrange("c (g j) -> c g j", j=16), op=ALU.add),
                    reads=[t_xc, t_ab1[q]], writes=[t_hb])
            for g8 in range(8):
                def tr(e, g8=g8):
                    for gi in range(8):
                        ins = e.transpose(tp_ps[:, gi, :], hb[:, g8 * 8 + gi, :, :].rearrange("c s j -> c (s j)"), ident[:])
                    return ins
                b.op("pe", tr, reads=[t_hb, t_ident], writes=[t_tp])
                cp("act", U8[:, g8 * 8:(g8 + 1) * 8, ct * 128:(ct + 1) * 128], tp_ps[:, :, :], [t_tp], [t_U8])
        b.sb = mk

    def state_mm(half):
        for sb_ in range(4):
            gb = half * 4 + sb_
            wb, twb = winb[wcount[0] % 2], t_winb[wcount[0] % 2]
            wcount[0] += 1
            b.dma(wb[:].rearrange("p g r d n -> p (g r d n)"), winD[gb], reads=[t_wD], writes=[twb])
            for glq in range(4):
                gq = sb_ * 4 + glq
                nbk = (4 * NCH + 511) // 512
                bks = [(glq % 2) * nbk + i for i in range(nbk)]

                def mm(e, glq=glq, gb=gb, wb=wb, bks=bks):
                    for g2 in range(2):
                        gl = 2 * glq + g2
                        g = 8 * gb + gl
                        for d in range(2):
                            for ri in range(2):
                                c4 = d * 2 + ri
                                bk = bks[(c4 * NCH) // 512]
                                off = (c4 * NCH) % 512
                                ins = e.matmul(banks[bk][64 * g2:64 * g2 + 64, off:off + NCH], lhsT=wb[:, gl, ri, d, :],
                                               rhs=U8[:, g, :], start=True, stop=True, tile_position=(0, 64 * g2))
                    return ins
                b.op("pe", mm, reads=[twb, t_U8], writes=[t_bank[x] for x in bks])
                for i, bk in enumerate(bks):
                    n = min(512, 4 * NCH - i * 512)
                    ncmb = n // NCH
                    c40 = (i * 512) // NCH
                    for cc in range(ncmb):
                        c4 = c40 + cc
                        cp("act", S[:, gq, c4 // 2, c4 % 2, 0:NCH], banks[bk][:, cc * NCH:(cc + 1) * NCH], [t_bank[bk]], t_S)

    def c0_scan(half):
        S6 = S[:, :, :, :, 0:NCH].rearrange("p g d r (a c) -> p g d r a c", c=16)
        gqs = slice(half * 16, half * 16 + 16)
        for k in range(1, 16):
            for d in range(2):
                eng = SENG[d]
                c0, pv = (k, k - 1) if d == 0 else (15 - k, 16 - k)
                lr = L8[:, 0, d, gqs]
                li = L8[:, 1, d, gqs]
                tk = [t_S[d], t_tm[d]]
                tt(eng, tmpA[d][:], S6[:, :, d, :, :, pv], lr.unsqueeze(2).unsqueeze(3).to_broadcast([128, 16, 2, NB]),
                   ALU.mult, tk + [t_Lt], tk)
                tt(eng, S6[:, :, d, :, :, c0], S6[:, :, d, :, :, c0], tmpA[d][:], ALU.add, tk, tk)
                lib = li.unsqueeze(2).to_broadcast([128, 16, NB])
                tt(eng, tmpB[d][:], S6[:, :, d, 1, :, pv], lib, ALU.mult, tk + [t_Lt], tk)
                tt(eng, S6[:, :, d, 0, :, c0], S6[:, :, d, 0, :, c0], tmpB[d][:], ALU.subtract, tk, tk)
                tt(eng, tmpB[d][:], S6[:, :, d, 0, :, pv], lib, ALU.mult, tk + [t_Lt], tk)
                tt(eng, S6[:, :, d, 1, :, c0], S6[:, :, d, 1, :, c0], tmpB[d][:], ALU.add, tk, tk)

    def pass1(HB, t_hb_):
        S6 = S[:, :, :, :, 0:NCH].rearrange("p g d r (a c) -> p g d r a c", c=16)
        for half in range(2):
            state_mm(half)
            c0_scan(half)
            gqs = slice(half * 16, half * 16 + 16)
            cp("dve", HB[:, gqs, 0, :, :], S6[:, :, 0, :, :, 15], [t_S[0]], [t_hb_])
            cp("pool", HB[:, gqs, 1, :, :], S6[:, :, 1, :, :, 0], [t_S[1]], [t_hb_])

    def c1_scan(HB, t_hb_, hins, dirs=(0, 1)):
        tk = [t_hb_, t_HBtmp]
        for d in dirs:
            lr = L128[:, 0, d, :]
            li = L128[:, 1, d, :]
            order = list(range(NB)) if d == 0 else list(range(NB - 1, -1, -1))
            prev = None
            for c1 in order:
                if prev is None:
                    prev = c1
                    if d not in hins:
                        continue
                    sr, si, tks = hins[d]
                    rd = tk + list(tks) + [t_Lt]
                else:
                    sr, si = HB[:, :, d, 0, prev], HB[:, :, d, 1, prev]
                    rd = tk + [t_Lt]
                    prev = c1
                cmad("dve", HB[:, :, d, 0, c1], HB[:, :, d, 1, c1], lr, li, sr, si, sm_t[:, :, 0], rd, tk)

    def seg_source(seg):
        return x2d[seg], t_x2[seg]

    if NSEG > 3:
        for j in range(8):
            seg = 3 + j
            src, tsrc = seg_source(seg)
            build_U8(src, tsrc, 1)
            pass1(HBtmp, t_HBtmp)
            c1_scan(HBtmp, t_HBtmp, {})
            cp("dve", EA[:, j, :, 0, :], HBtmp[:, :, 0, :, NB - 1], [t_HBtmp], [t_EA])
            cp("dve", EA[:, j, :, 1, :], HBtmp[:, :, 1, :, 0], [t_HBtmp], [t_EA])
        acc = b.alloc([128, 32, 2, 2], F32, "acc")
        acc2 = b.alloc([128, 32, 2, 2], F32, "acc2")
        t_acc = T("acc")
        b.op("dve", lambda e: e.memset(HIN[:, 2], 0.0), writes=[t_HIN])
        for d in range(2):
            b.op("dve", lambda e: e.memset(acc[:], 0.0), writes=[t_acc])
            order = list(range(8)) if d == 0 else list(range(7, -1, -1))
            for j in order:
                b.op("dve", lambda e, j=j, d=d: e.scalar_tensor_tensor(
                    out=HIN[:, 2, :, d, :], in0=acc[:, :, d, :], scalar=selt[:, j:j + 1], in1=HIN[:, 2, :, d, :],
                    op0=ALU.mult, op1=ALU.add), reads=[t_acc, t_sel], writes=[t_HIN])
                cp("dve", acc2[:, :, d, :], EA[:, j, :, d, :], [t_EA, t_acc], [t_acc])
                cmad("dve", acc2[:, :, d, 0], acc2[:, :, d, 1], LSEG[:, 0, d, :], LSEG[:, 1, d, :],
                     acc[:, :, d, 0], acc[:, :, d, 1], sm_t[:, :, 0], [t_Lt], [t_acc])
                cp("dve", acc[:, :, d, :], acc2[:, :, d, :], [t_acc], [t_acc])

    if upto == "ea":
        b.barrier()
        return nc, b
    own = [0, 1, 2] if NSEG > 3 else [0, 1]
    for seg in own:
        src, tsrc = seg_source(seg)
        build_U8(src, tsrc, 0 if seg < 2 else 1)
        pass1(HBs[seg], t_HB[seg])
    hin_of = {seg: {} for seg in own}
    c1_scan(HBs[1], t_HB[1], {}, dirs=(1,))
    hin_of[0][1] = (HBs[1][:, :, 1, 0, 0], HBs[1][:, :, 1, 1, 0], [t_HB[1]])
    c1_scan(HBs[0], t_HB[0], hin_of[0], dirs=(0, 1))
    hin_of[1][0] = (HBs[0][:, :, 0, 0, NB - 1], HBs[0][:, :, 0, 1, NB - 1], [t_HB[0]])
    c1_scan(HBs[1], t_HB[1], hin_of[1], dirs=(0,))
    if 2 in own:
        for d in range(2):
            hin_of[2][d] = (HIN[:, 2, :, d, 0], HIN[:, 2, :, d, 1], [t_HIN])
        c1_scan(HBs[2], t_HB[2], hin_of[2])
    CXb = HBtmp
    t_CXb = t_HBtmp

    def make_CX(seg):
        HB = HBs[seg]
        b.op("pool", lambda e: e.memset(CXb[:], 0.0), writes=[t_CXb])
        cp("pool", CXb[:, :, 0, :, 1:NB], HB[:, :, 0, :, 0:NB - 1], [t_HB[seg]], [t_CXb])
        cp("pool", CXb[:, :, 1, :, 0:NB - 1], HB[:, :, 1, :, 1:NB], [t_HB[seg]], [t_CXb])
        for d, c1 in ((0, 0), (1, NB - 1)):
            if d in hin_of[seg]:
                sr, si, tks = hin_of[seg][d]
                cp("pool", CXb[:, :, d, 0, c1], sr, list(tks), [t_CXb])
                cp("pool", CXb[:, :, d, 1, c1], si, list(tks), [t_CXb])
    CXs = {seg: CXb for seg in own}
    t_CX = {seg: t_CXb for seg in own}

    if upto == "c1":
        b.barrier()
        return nc, b
    Hx = b.alloc([128, 16, 2, 2, NCH], BF, "Hx")
    t_Hx = T("Hx")
    wo2 = [b.alloc([128, 2, 2, 4, 128], BF, "wo2_%d" % i) for i in range(2)]
    wt2 = [b.alloc([128, 8, 128], BF, "wt2_0")] * 2
    t_wo2 = [T("wo2_0"), T("wo2_1")]
    t_wt2 = [T("wt2_0")] * 2
    ytt = [b.alloc([128, 8, 128], BF, "ytt0")] * 2
    t_ytt = [T("ytt0")] * 2
    yc = 0
    oc = 0
    S6 = S[:, :, :, :, 0:NCH].rearrange("p g d r (a c) -> p g d r a c", c=16)
    Hx6 = Hx[:].rearrange("p g d r (a c) -> p g d r a c", c=16)
    for seg in own:
        src, tsrc = seg_source(seg)
        build_U8(src, tsrc, 0 if seg < 2 else 1)
        make_CX(seg)
        CX = CXs[seg]
        for half in range(2):
            gqs = slice(half * 16, half * 16 + 16)
            state_mm(half)
            for d, c0 in ((0, 0), (1, 15)):
                eng = SENG[d]
                lrb = L8[:, 0, d, gqs].unsqueeze(2).to_broadcast([128, 16, NB])
                lib = L8[:, 1, d, gqs].unsqueeze(2).to_broadcast([128, 16, NB])
                cmad(eng, S6[:, :, d, 0, :, c0], S6[:, :, d, 1, :, c0], lrb, lib, CX[:, gqs, d, 0, :], CX[:, gqs, d, 1, :],
                     tmpB[d][:], [t_CX[seg], t_Lt, t_tm[d]], [t_S[d], t_tm[d]])
            c0_scan(half)
            for ri in range(2):
                cp("act", Hx6[:, :, 0, ri, :, 1:16], S6[:, :, 0, ri, :, 0:15], [t_S[0]], [t_Hx])
                cp("act", Hx6[:, :, 1, ri, :, 0:15], S6[:, :, 1, ri, :, 1:16], [t_S[1]], [t_Hx])
            cp("act", Hx6[:, :, 0, :, :, 0], CX[:, gqs, 0, :, :], [t_CX[seg]], [t_Hx])
            cp("act", Hx6[:, :, 1, :, :, 15], CX[:, gqs, 1, :, :], [t_CX[seg]], [t_Hx])
            for sb_ in range(4):
                gb = half * 4 + sb_
                wo_, two_, wt_, twt_ = wo2[oc % 2], t_wo2[oc % 2], wt2[oc % 2], t_wt2[oc % 2]
                oc += 1
                b.dma(wo_[:].rearrange("p r d q n -> p (r d q n)"), woutD[gb], reads=[t_wD], writes=[two_])
                b.dma(wt_[:].rearrange("p g n -> p (g n)"), wtoepD[gb], reads=[t_wD], writes=[twt_])
                for ct in range(NCT):
                    cs = slice(ct * 128, (ct + 1) * 128)
                    yt_, tyt_ = ytt[yc % 2], t_ytt[yc % 2]
                    yc += 1
                    for hb_ in range(2):
                        bk = 4 + hb_

                        def mmy(e, hb_=hb_, bk=bk, gb=gb, sb_=sb_, wo_=wo_, wt_=wt_, cs=cs):
                            for gg in range(4):
                                gl = hb_ * 4 + gg
                                g2, glq = gl % 2, gl // 2
                                g = 8 * gb + gl
                                gq = sb_ * 4 + glq
                                o = banks[bk][:, gg * 128:(gg + 1) * 128]
                                e.matmul(o, lhsT=U8[:, g, cs], rhs=wt_[:, gl, :], start=True, stop=False)
                                for d in range(2):
                                    for ri in range(2):
                                        ins = e.matmul(o, lhsT=Hx[64 * g2:64 * g2 + 64, gq, d, ri, cs],
                                                       rhs=wo_[64 * g2:64 * g2 + 64, ri, d, glq, :],
                                                       start=False, stop=(d == 1 and ri == 1))
                            return ins
                        b.op("pe", mmy, reads=[t_U8, t_Hx, two_, twt_], writes=[t_bank[bk]])
                        b.op("act", lambda e, hb_=hb_, bk=bk, yt_=yt_: e.activation(
                            out=yt_[:, :, hb_ * 64:(hb_ + 1) * 64].rearrange("c s (g i) -> c g s i", i=16),
                            in_=banks[bk][:, :].rearrange("c (g s i) -> c g s i", g=4, s=8), func=AF.Gelu),
                            reads=[t_bank[bk]], writes=[tyt_])
                    b.dma(ytd[seg][ct * 1024:(ct + 1) * 1024, gb * 128:(gb + 1) * 128].rearrange("(c s) n -> c s n", s=8),
                          yt_[:], reads=[tyt_], writes=[t_yt[seg]])

    if upto == "p2":
        b.barrier()
        return nc, b
    b.barrier()
    b.sb = s5_mark
    wg = b.alloc([128, KC, 2 * D], BF, "wg")
    t_wg = T("wg")
    mkg = b.sb
    wgs = [b.alloc([128, 2 * D], F32, "wgs%d" % i) for i in range(2)]
    t_wgs = [T("wgs0"), T("wgs1")]
    wgv = wglu_in.rearrange("(kc p) n -> p kc n", p=128)
    for kc in range(KC):
        b.dma(wgs[kc % 2][:], wgv[:, kc, :], writes=[t_wgs[kc % 2]])
        cp("pool" if kc % 2 else "dve", wg[:, kc, :], wgs[kc % 2][:], [t_wgs[kc % 2]], [t_wg])
    b.barrier()
    b.sb = mkg
    ytc = b.alloc([128, 8, D], BF, "ytc")
    x2c = b.alloc([128, 8, D], F32, "x2c")
    gT = b.alloc([128, KC, 128], BF, "gT")
    sg = b.alloc([128, D], F32, "sg")
    yv = b.alloc([128, D], F32, "yv")
    tmpg = b.alloc([128, D], F32, "tmpg")
    og = [b.alloc([128, D], F32, "og%d" % i) for i in range(2)]
    ssg = b.alloc([128, 4], F32, "ssg")
    t_ytc, t_x2c, t_gT, t_sg, t_yv, t_tmpg, t_ssg = T("ytc"), T("x2c"), T("gT"), T("sg"), T("yv"), T("tmpg"), T("ssg")
    t_og = [T("og0"), T("og1")]
    mkg2 = b.sb
    ogc = 0
    for seg in own:
        q = 0 if seg < 2 else 1
        if seg in (0, 2):
            if seg == 2:
                b.barrier()
            b.sb = mkg2
            growm, t_growm = load_row(q, 1, 2, "growm")
        for ct in range(NCT):
            rows = slice(ct * 1024, (ct + 1) * 1024)
            b.dma(ytc[:], ytd[seg][rows, :].rearrange("(c s) n -> c s n", s=8), reads=[t_yt[seg]], writes=[t_ytc])
            b.dma(x2c[:], x2d[seg][rows, :].rearrange("(c s) n -> c s n", s=8), reads=[t_x2[seg]], writes=[t_x2c])
            for s_ in range(8):
                def trg(e, s_=s_):
                    for kc in range(KC):
                        ins = e.transpose(tp_ps[:, kc, :], ytc[:, s_, kc * 128:(kc + 1) * 128], ident[:])
                    return ins
                b.op("pe", trg, reads=[t_ytc, t_ident], writes=[t_tp])
                cp("act", gT[:], tp_ps[:, :, :], [t_tp], [t_gT])

                def mmg(e):
                    for n4 in range(4):
                        for kc in range(KC):
                            ins = e.matmul(banks[n4][:, :], lhsT=gT[:, kc, :], rhs=wg[:, kc, n4 * 512:(n4 + 1) * 512],
                                           start=(kc == 0), stop=(kc == KC - 1))
                    return ins
                b.op("pe", mmg, reads=[t_gT, t_wg], writes=t_bank[0:4])
                for h2 in range(2):
                    b.op("act", lambda e, h2=h2: e.activation(out=sg[:, h2 * 512:(h2 + 1) * 512], in_=banks[2 + h2][:, :],
                                                              func=AF.Sigmoid), reads=[t_bank[2 + h2]], writes=[t_sg])
                    tt("dve", yv[:, h2 * 512:(h2 + 1) * 512], banks[h2][:, :], sg[:, h2 * 512:(h2 + 1) * 512], ALU.mult,
                       [t_bank[h2], t_sg], [t_yv])
                rstd_of([yv[:, :]], [t_yv], ssg, t_ssg)
                b.op("dve", lambda e: e.scalar_tensor_tensor(out=tmpg[:], in0=yv[:], scalar=ssg[:, 0:1], in1=growm[:],
                                                             op0=ALU.mult, op1=ALU.mult),
                     reads=[t_yv, t_ssg, t_growm], writes=[t_tmpg])
                o, to = og[ogc % 2], t_og[ogc % 2]
                ogc += 1
                tt("pool", o[:], x2c[:, s_, :], tmpg[:], ALU.add, [t_x2c, t_tmpg], [to])
                b.dma(x3d[seg][rows, :].rearrange("(c s) d -> c s d", s=8)[:, s_, :], o[:], reads=[to], writes=[t_x3[seg]])

    if debug:
        b.barrier()
        return nc, b
    t_out = [T("o0"), T("o1"), T("o2")]
    ffn_phase(1, [x3d[i] for i in range(3)], t_x3, [yp[0:TS, :], yp[TS:2 * TS, :], ys[:, :]], t_out)
    b.barrier()
    return nc, b


def make_inputs(inp, core, RSEG):
    TS = RSEG * W
    f = np.float32
    xp = np.zeros(((2 * RSEG + 8) * W, D), f)
    xp[256:256 + 2 * TS] = inp["x_prompt"][core]
    xs = np.zeros(((RSEG + 8) * W, D), f)
    Rs = 8 * RSEG
    g0s = core * RSEG
    lo, hi = max(0, g0s - 4), min(Rs, g0s + RSEG + 4)
    xs[(lo - (g0s - 4)) * W:(hi - (g0s - 4)) * W] = inp["x_sample"][0][lo * W:hi * W]
    cvec = np.stack([inp["c_prompt"][core], inp["c_sample"][0]]).astype(f)
    p = np.arange(128)
    kr2, kc = p // 64, p % 64
    mm = np.arange(14)
    qc = np.arange(64)
    dr = kr2[:, None, None] + 13 - mm[None, :, None] + 0 * qc[None, None, :]
    dc = np.clip(kc[:, None, None] - qc[None, None, :], -15, 15) + 15 + 0 * mm[None, :, None]
    rpbY = inp["attn_rpb"][0][:, dr, dc].reshape(NH, 128, 14 * 64).astype(f)
    cs = np.clip(qc - 8, 0, 48)
    cvm = ((kc[:, None] >= cs[None, :]) & (kc[:, None] < cs[None, :] + 16)).astype(f)
    mmi = np.arange(14)
    bandm = ((mmi[None, :] >= kr2[:, None] + 3) & (mmi[None, :] <= kr2[:, None] + 10)).astype(f)
    NQB = RSEG // 4
    xsa = np.zeros(((8 * RSEG + 8) * W, D), f)
    xsa[256:256 + 8 * TS] = inp["x_sample"][0]
    segs = [(0, 2 * RSEG), (RSEG, 2 * RSEG), (g0s, Rs)] + [(j * RSEG, Rs) for j in range(8)]
    rvx = np.zeros((11, 128, NQB, 6, 4), f)
    for seg, (g0, R) in enumerate(segs):
        for qb in range(NQB):
            for kk in range(6):
                k = 5 - kk
                for qr in range(4):
                    qg = g0 + 4 * qb + qr
                    ws = min(max(qg - 4, 0), R - 8)
                    for k2 in range(2):
                        kg = g0 - 4 + 4 * qb + 2 * k + k2
                        ok = (0 <= kg < R) and (ws <= kg < ws + 8)
                        rvx[seg, k2 * 64:(k2 + 1) * 64, qb, kk, qr] = 1.0 if ok else 0.0
    sidx = np.arange(128) // 16
    mfb = np.stack([(sidx[:, None] <= sidx[None, :]), (sidx[:, None] >= sidx[None, :])]).astype(f)
    sel = np.zeros((128, 8), f)
    sel[:, core] = 1.0
    return dict(xp=xp, xs=xs, xsa=xsa, cvec=cvec, ada_w=inp["ada_w"], ada_b=inp["ada_b"], norm_gain=inp["norm_gain"],
                w_qkv=inp["attn_w_qkv"][0], w_o=inp["attn_w_o"][0], rpbY=rpbY, cvm=cvm, bandm=bandm,
                rvx=rvx.reshape(11, 128, NQB * 24), ident=np.eye(128, dtype=f),
                ffn_w1=inp["ffn_w1"], ffn_w2=inp["ffn_w2"],
                s5_a_re=inp["s5_a_re"][0], s5_a_im=inp["s5_a_im"][0], s5_log_dt=inp["s5_log_dt"][0],
                s5_b_re=inp["s5_b_re"][0], s5_b_im=inp["s5_b_im"][0], s5_c_re=inp["s5_c_re"][0], s5_c_im=inp["s5_c_im"][0],
                s5_d=inp["s5_d"][0], s5_w_glu=inp["s5_w_glu"][0], mfb=mfb, sel=sel)


_CACHE = {}


def kernel(**inputs):
    RSEG = 32
    inp = {k: np.asarray(v) for k, v in inputs.items()}
    if "nc" not in _CACHE:
        nc, b = build(RSEG)
        b.emit()
        _CACHE["nc"] = nc
    nc = _CACHE["nc"]
    maps = [make_inputs(inp, core, RSEG) for core in range(8)]
    res = run_bass_kernel_spmd(nc, maps, core_ids=list(range(8)))
    TS = RSEG * W
    y_prompt = np.stack([res.results[c]["yp"] for c in range(8)]).astype(np.float32)
    y_sample = np.concatenate([res.results[c]["ys"] for c in range(8)], axis=0)[None].astype(np.float32)
    return (y_prompt, y_sample)
```

```python
import numpy as np
from contextlib import ExitStack
import concourse.bass as bass
import concourse.mybir as mybir
from concourse.bass_utils import run_bass_kernel_spmd

F32, BF = mybir.dt.float32, mybir.dt.bfloat16
AF = mybir.ActivationFunctionType
ALU = mybir.AluOpType
D = 1024
KC = 8
W = 64
NH = 16
DFF = 4096
EPS = 1e-6
ENG = ("sp", "act", "dve", "pool", "pe")
NDS = 48
SB_BASE = 16640
SB_LIMIT = 229376


class T:
    def __init__(s, name):
        s.name = name
        s.w = None
        s.r = []


class Builder:
    def __init__(s, nc):
        s.nc = nc
        s.prog = {e: [] for e in ENG}
        s.cnt = {}
        s.waited = {e: {} for e in ENG}
        s.ndma = 0
        s.sb = SB_BASE
        s.nalloc = 0

    def alloc(s, shape, dt, name=None):
        nb = int(np.prod(shape[1:])) * (4 if dt == F32 else 2)
        nb = (nb + 63) // 64 * 64
        assert s.sb + nb <= SB_LIMIT, ("SBUF overflow", name, s.sb, nb)
        s.nalloc += 1
        t = s.nc.alloc_sbuf_tensor_at("%s_%d" % (name or "t", s.nalloc), list(shape), dt, offset=s.sb)
        s.sb += nb
        return t

    def _deps(s, eng, reads, writes):
        deps = []
        for t in reads:
            if t.w:
                deps.append((t.w, True))
        for t in writes:
            if t.w:
                deps.append((t.w, True))
            deps.extend((r, False) for r in t.r)
        out = {}
        for (k, v), isw in deps:
            if k == eng and (eng == "pe" or not isw):
                continue
            if s.waited[eng].get(k, 0) >= v:
                continue
            out[k] = max(out.get(k, 0), v)
        for k, v in out.items():
            s.waited[eng][k] = v
        return list(out.items())

    def _finish(s, tok, reads, writes):
        for t in reads:
            t.r.append(tok)
        for t in writes:
            t.w = tok
            t.r = []

    def op(s, eng, fn, reads=(), writes=()):
        waits = s._deps(eng, reads, writes)
        s.cnt[eng] = s.cnt.get(eng, 0) + 1
        tok = (eng, s.cnt[eng])
        s.prog[eng].append((waits, fn, (eng, 1)))
        s._finish(tok, reads, writes)
        return tok

    def dma(s, out, in_, reads=(), writes=(), q="sp", slow=False):
        k = "d%d" % (s.ndma % NDS)
        s.ndma += 1
        waits = s._deps(q, reads, writes)
        prev = s.cnt.get(k, 0)
        if prev and s.waited[q].get(k, 0) < prev:
            waits.append((k, prev))
            s.waited[q][k] = prev
        s.cnt[k] = prev + 16
        tok = (k, s.cnt[k])
        if slow:
            fn = lambda e: e.dma_start(out=out, in_=in_, allow_slow_non_contiguous=True)
        else:
            fn = lambda e: e.dma_start(out=out, in_=in_)
        s.prog[q].append((waits, fn, (k, 16)))
        s._finish(tok, reads, writes)
        return tok

    def barrier(s):
        for e in ENG:
            waits = []
            for k, v in s.cnt.items():
                if k == e and e == "pe":
                    continue
                if s.waited[e].get(k, 0) < v:
                    waits.append((k, v))
                    s.waited[e][k] = v
            if waits:
                s.prog[e].append((waits, None, None))

    def emit(s):
        nc = s.nc
        with ExitStack() as es:
            sems = {}
            for k in list(ENG) + ["d%d" % i for i in range(NDS)]:
                sems[k] = es.enter_context(nc.semaphore("s_" + k))
            block = es.enter_context(nc.Block())
            decos = {"sp": block.sync, "act": block.scalar, "dve": block.vector,
                     "pool": block.gpsimd, "pe": block.tensor}
            for eng in ENG:
                def body(e, eng=eng):
                    for waits, fn, inc in s.prog[eng]:
                        for k, v in waits:
                            e.wait_ge(sems[k], v)
                        if fn is not None:
                            ins = fn(e)
                            ins.then_inc(sems[inc[0]], inc[1])
                decos[eng](body)


def build(RSEG, debug=False, NSEG=11, upto="all"):
    nc = bass.Bass("TRN2", target_bir_lowering=False)
    b = Builder(nc)
    TS = RSEG * W
    TK = (RSEG + 8) * W
    NQB = RSEG // 4
    NTK = TK // 128
    NTS = TS // 128

    def din(name, shape, dt=F32):
        return nc.dram_tensor(name, list(shape), dt, kind="ExternalInput").ap()

    xp = din("xp", [(2 * RSEG + 8) * W, D])
    xs = din("xs", [(RSEG + 8) * W, D])
    xsa = din("xsa", [(8 * RSEG + 8) * W, D])
    cvec = din("cvec", [2, D])
    ada_w = din("ada_w", [2, D, 6 * D])
    ada_b = din("ada_b", [2, 6 * D])
    ngain = din("norm_gain", [2, 4, D])
    w_qkv = din("w_qkv", [D, 3 * D])
    w_o = din("w_o", [D, D])
    rpbY = din("rpbY", [NH, 128, 14 * 64])
    cvm = din("cvm", [128, 64])
    bandm = din("bandm", [128, 14])
    rvx = din("rvx", [11, 128, NQB * 6 * 4])
    ident_in = din("ident", [128, 128])
    ffn_w1 = din("ffn_w1", [2, D, DFF])
    ffn_w2 = din("ffn_w2", [2, DFF, D])
    yp = nc.dram_tensor("yp", [2 * TS, D], F32, kind="ExternalOutput").ap()
    ys = nc.dram_tensor("ys", [TS, D], F32, kind="ExternalOutput").ap()
    mv = nc.dram_tensor("mv", [2, 2, 6, D], F32, kind="ExternalOutput" if debug else "Internal").ap()
    aoT = nc.dram_tensor("aoT", [NSEG, KC, 128, TS], BF).ap()
    x1d = nc.dram_tensor("x1d", [NSEG, TS, D], F32).ap()
    x2d = nc.dram_tensor("x2d", [NSEG, TS, D], F32, kind="ExternalOutput" if debug else "Internal").ap()
    t_mv = T("mv")
    t_ao = [T("ao%d" % i) for i in range(NSEG)]
    t_x1 = [T("x1_%d" % i) for i in range(NSEG)]
    t_x2 = [T("x2_%d" % i) for i in range(NSEG)]

    def xsrc(seg):
        if seg == 0:
            return xp[0:TK, :]
        if seg == 1:
            return xp[TS:TS + TK, :]
        if seg == 2:
            return xs[0:TK, :]
        return xsa[(seg - 3) * TS:(seg - 3) * TS + TK, :]

    banks = [nc.alloc_psum_tensor("pb%d" % i, [128, 512], F32) for i in range(7)]
    t_bank = [T("bank%d" % i) for i in range(7)]

    ident_f = b.alloc([128, 128], F32, "identf")
    ident = b.alloc([128, 128], BF, "ident")
    t_ident = T("ident")
    b.dma(ident_f[:], ident_in, writes=[t_ident])
    b.op("dve", lambda e: e.tensor_copy(out=ident[:], in_=ident_f[:]), reads=[t_ident], writes=[t_ident])
    const_mark = b.sb

    ccol = b.alloc([128, 2, KC], F32, "ccol")
    crep = b.alloc([128, 2, KC, 128], F32, "crep")
    t_c = T("c")
    b.dma(ccol[:], cvec.rearrange("s (kc p) -> p s kc", p=128), writes=[t_c], slow=True)
    b.op("act", lambda e: e.activation(out=ccol[:], in_=ccol[:], func=AF.Silu), reads=[t_c], writes=[t_c])
    for q in range(2):
        b.op("dve", lambda e, q=q: e.tensor_copy(
            out=crep[:, q, :, :], in_=ccol[:, q, :].unsqueeze(2).to_broadcast([128, KC, 128])),
            reads=[t_c], writes=[t_c])
    stage = [b.alloc([128, 3072], F32, "adst%d" % i) for i in range(2)]
    t_stage = [T("adst0"), T("adst1")]
    adab = b.alloc([1, 6 * D], F32, "adab")
    gainb = b.alloc([1, 4, D], F32, "gainb")
    modrow = b.alloc([1, 6 * D], F32, "modrow")
    dv = b.alloc([1, 6, D], F32, "dv")
    t_adab, t_mod, t_dv = T("adab"), T("mod"), T("dv")
    si = 0
    for l in range(2):
        b.dma(adab[:], ada_b[l:l + 1, :], writes=[t_adab])
        b.dma(gainb[:], ngain[l:l + 1, :, :], writes=[t_adab])
        for q in range(2):
            for nh in range(2):
                for kc in range(KC):
                    st, tst = stage[si % 2], t_stage[si % 2]
                    si += 1
                    b.dma(st[:], ada_w[l, kc * 128:(kc + 1) * 128, nh * 3072:(nh + 1) * 3072], writes=[tst])

                    def mm(e, st=st, kc=kc, q=q):
                        for j in range(6):
                            ins = e.matmul(banks[j][0:32, :], lhsT=crep[:, q, kc, 0:32], rhs=st[:, j * 512:(j + 1) * 512],
                                           start=(kc == 0), stop=(kc == KC - 1))
                        return ins
                    b.op("pe", mm, reads=[tst, t_c], writes=t_bank[0:6])
                for j in range(6):
                    c0 = nh * 3072 + j * 512
                    b.op("dve", lambda e, j=j, c0=c0: e.tensor_tensor(
                        out=modrow[0:1, c0:c0 + 512], in0=banks[j][0:1, :], in1=adab[0:1, c0:c0 + 512], op=ALU.add),
                        reads=[t_bank[j], t_adab], writes=[t_mod])
            for (o, sc_i, g_i) in ((0, 1, 0), (3, 4, 2)):
                b.op("dve", lambda e, o=o, sc_i=sc_i, g_i=g_i: e.scalar_tensor_tensor(
                    out=dv[0:1, o, :], in0=modrow[0:1, sc_i * D:(sc_i + 1) * D], scalar=1.0, in1=gainb[0:1, g_i, :],
                    op0=ALU.add, op1=ALU.mult), reads=[t_mod, t_adab], writes=[t_dv])
            for (o, sh_i) in ((1, 0), (4, 3)):
                b.op("dve", lambda e, o=o, sh_i=sh_i: e.tensor_copy(
                    out=dv[0:1, o, :], in_=modrow[0:1, sh_i * D:(sh_i + 1) * D]), reads=[t_mod], writes=[t_dv])
            for (o, gt_i, g_i) in ((2, 2, 1), (5, 5, 3)):
                b.op("dve", lambda e, o=o, gt_i=gt_i, g_i=g_i: e.tensor_tensor(
                    out=dv[0:1, o, :], in0=modrow[0:1, gt_i * D:(gt_i + 1) * D], in1=gainb[0:1, g_i, :], op=ALU.mult),
                    reads=[t_mod, t_adab], writes=[t_dv])
            b.dma(mv[q:q + 1, l, :, :], dv[:], reads=[t_dv], writes=[t_mv])
    b.barrier()
    b.sb = const_mark

    def load_cols(q, l, which, name):
        t = b.alloc([128, KC], F32, name)
        tt = T(name)
        b.dma(t[:], mv[q, l, which, :].rearrange("(kc p) -> p kc", p=128), reads=[t_mv], writes=[tt], slow=True)
        return t, tt

    def load_row(q, l, which, name):
        t = b.alloc([128, D], F32, name)
        tt = T(name)
        b.dma(t[:], mv[q, l, which:which + 1, :].partition_broadcast(128), reads=[t_mv], writes=[tt])
        return t, tt

    junk = b.alloc([128, D], BF, "junk")
    t_junk = T("junk")
    tp_ps = nc.alloc_psum_tensor("tp_ps", [128, KC, 128], BF)
    t_tp = T("tp")
    common_mark = b.sb

    def rstd_of(src_aps, reads, ss, t_ss):
        for i, ap in enumerate(src_aps):
            n = ap.shape[-1]
            b.op("act", lambda e, ap=ap, i=i, n=n: e.activation(out=junk[:, 0:n], in_=ap, func=AF.Square,
                                                            accum_out=ss[:, 1 + i:2 + i]),
                 reads=reads, writes=[t_junk, t_ss])
        if len(src_aps) == 2:
            b.op("dve", lambda e: e.tensor_tensor(out=ss[:, 1:2], in0=ss[:, 1:2], in1=ss[:, 2:3], op=ALU.add),
                 reads=[t_ss], writes=[t_ss])
        b.op("dve", lambda e: e.tensor_scalar(out=ss[:, 0:1], in0=ss[:, 1:2], scalar1=1.0 / D, scalar2=EPS,
                                              op0=ALU.mult, op1=ALU.add), reads=[t_ss], writes=[t_ss])
        b.op("act", lambda e: e.activation(out=ss[:, 3:4], in_=ss[:, 0:1], func=AF.Sqrt), reads=[t_ss], writes=[t_ss])
        b.op("dve", lambda e: e.reciprocal(out=ss[:, 0:1], in_=ss[:, 3:4]), reads=[t_ss], writes=[t_ss])

    def norm_stats(x_t, t_x, ss, t_ss, xn, t_xn):
        rstd_of([x_t[:, :]], [t_x], ss, t_ss)
        b.op("act", lambda e: e.activation(out=xn[:], in_=x_t[:, :], func=AF.Copy, scale=ss[:, 0:1]),
             reads=[t_x, t_ss], writes=[t_xn])

    def norm_tr(xn, t_xn):
        def tr(e):
            for kc in range(KC):
                ins = e.transpose(tp_ps[:, kc, :], xn[:, kc * 128:(kc + 1) * 128], ident[:])
            return ins
        b.op("pe", tr, reads=[t_xn, t_ident], writes=[t_tp])

    def norm_evac(acol, bcol, t_ab, dst, t_dst, tok0):
        for kc in range(KC):
            b.op("act", lambda e, kc=kc: e.activation(out=dst[:, kc, tok0:tok0 + 128], in_=tp_ps[:, kc, :],
                                                      func=AF.Identity, scale=acol[:, kc:kc + 1], bias=bcol[:, kc:kc + 1]),
                 reads=[t_tp, t_ab], writes=[t_dst])

    def norm_T(x_t, t_x, ss, t_ss, xn, t_xn, acol, bcol, t_ab, dst, t_dst, tok0):
        norm_stats(x_t, t_x, ss, t_ss, xn, t_xn)
        norm_tr(xn, t_xn)
        norm_evac(acol, bcol, t_ab, dst, t_dst, tok0)

    for seg in range(NSEG):
        b.barrier()
        b.sb = common_mark
        q = 0 if seg < 2 else 1
        acol, t_acol = load_cols(q, 0, 0, "acol")
        bcol, t_bcol = load_cols(q, 0, 1, "bcol")
        t_ab = T("ab")
        b.op("dve", lambda e: e.tensor_copy(out=acol[:, 0:1], in_=acol[:, 0:1]), reads=[t_acol, t_bcol], writes=[t_ab])
        hT = b.alloc([128, KC, TK], BF, "hT")
        t_hT = T("hT")
        xt = [b.alloc([128, D], F32, "xt%d" % i) for i in range(2)]
        t_xt = [T("xt0"), T("xt1")]
        ssA = [b.alloc([128, 4], F32, "ssA%d" % i) for i in range(2)]
        t_ssA = [T("ssA0"), T("ssA1")]
        xnA = [b.alloc([128, D], BF, "xnA%d" % i) for i in range(2)]
        t_xnA = [T("xnA0"), T("xnA1")]
        xsr = xsrc(seg)

        def a1_stats(t):
            b.dma(xt[t % 2][:], xsr[t * 128:(t + 1) * 128, :], writes=[t_xt[t % 2]])
            norm_stats(xt[t % 2], t_xt[t % 2], ssA[t % 2], t_ssA[t % 2], xnA[t % 2], t_xnA[t % 2])
        a1_stats(0)
        for t in range(NTK):
            norm_tr(xnA[t % 2], t_xnA[t % 2])
            if t + 1 < NTK:
                a1_stats(t + 1)
            norm_evac(acol, bcol, t_ab, hT, t_hT, t * 128)
        wst = b.alloc([128, KC, 3, 128], F32, "wst")
        t_wst = T("wst")
        wq = [b.alloc([128, KC, 3, 128], BF, "wq%d" % i) for i in range(2)]
        KT = [b.alloc([128, TK], BF, "KT%d" % i) for i in range(2)]
        QT = [b.alloc([128, TS], BF, "QT%d" % i) for i in range(2)]
        Vaug = [b.alloc([128, NTK, 2, 128], BF, "Vaug%d" % i) for i in range(2)]
        t_wq = [T("wq0"), T("wq1")]
        t_KT = [T("KT0"), T("KT1")]
        t_QT = [T("QT0"), T("QT1")]
        t_V = [T("V0"), T("V1")]
        Yst = b.alloc([128, 14 * 64], F32, "Yst")
        t_Yst = T("Yst")
        Ytf = [[b.alloc([128, 14 * 64], BF, "Ytf") for hh in range(2)] for i in range(2)]
        Yti = [[b.alloc([128, 14 * 64], BF, "Yti") for hh in range(2)] for i in range(2)]
        t_Yt = [[T("Yt") for hh in range(2)] for i in range(2)]
        cv = b.alloc([128, 64], F32, "cv")
        bandt = b.alloc([128, 14], F32, "bandt")
        rv = b.alloc([128, NQB, 6, 4], BF, "rv")
        rvf = b.alloc([128, NQB * 24], F32, "rvf")
        t_cv, t_rv = T("cv"), T("rv")
        b.dma(cv[:], cvm, writes=[t_cv])
        b.dma(bandt[:], bandm, writes=[t_cv])
        b.dma(rvf[:], rvx[seg], writes=[t_rv])
        b.op("dve", lambda e: e.tensor_copy(out=rv[:].rearrange("p a b c -> p (a b c)"), in_=rvf[:]),
             reads=[t_rv], writes=[t_rv])
        expS = [b.alloc([128, 6, 256], F32, "expS%d" % i) for i in range(2)]
        Pm = [b.alloc([128, 6, 256], BF, "Pm%d" % i) for i in range(2)]
        tmpP = b.alloc([128, 6, 256], BF, "tmpP")
        t_expS, t_Pm, t_tmpP = [T("expS0"), T("expS1")], [T("Pm0"), T("Pm1")], T("tmpP")
        rec = [b.alloc([128, 256], F32, "rec%d" % i) for i in range(2)]
        t_rec = [T("rec0"), T("rec1")]
        AO = [b.alloc([128, TS], BF, "AO%d" % i) for i in range(2)]
        t_AO = [T("AO0"), T("AO1")]
        t_Oh = [t_bank[3], t_bank[2]]
        for i in range(2):
            b.op("pool", lambda e, i=i: e.memset(Vaug[i][:, :, 0, 64:128], 1.0), writes=[t_V[i]])
            b.op("pool", lambda e, i=i: e.memset(Vaug[i][:, :, 1, 0:64], 1.0), writes=[t_V[i]])
        wv = w_qkv.rearrange("(kc p) (w n) -> p kc w n", p=128, w=3)

        def proj_items(hp, sl):
            items = []

            def it_w():
                for wi in range(3):
                    b.dma(wst[:, :, wi, :], wv[:, :, wi, hp * 128:(hp + 1) * 128], writes=[t_wst])
                b.op("pool", lambda e: e.tensor_copy(out=wq[sl][:], in_=wst[:]), reads=[t_wst], writes=[t_wq[sl]])
            items.append(it_w)
            for hh in range(2):
                def it_y(hh=hh):
                    b.dma(Yst[:], rpbY[2 * hp + hh], writes=[t_Yst])
                    b.op("act", lambda e: e.activation(out=Yst[:], in_=Yst[:], func=AF.Exp), reads=[t_Yst], writes=[t_Yst])
                    b.op("dve", lambda e: e.tensor_tensor(
                        out=Ytf[sl][hh][:].rearrange("p (m c) -> p m c", c=64), in0=Yst[:].rearrange("p (m c) -> p m c", c=64),
                        in1=cv[:].unsqueeze(1).to_broadcast([128, 14, 64]), op=ALU.mult),
                        reads=[t_Yst, t_cv], writes=[t_Yt[sl][hh]])
                    b.op("dve", lambda e: e.tensor_tensor(
                        out=Yti[sl][hh][:].rearrange("p (m c) -> p m c", c=64), in0=Ytf[sl][hh][:].rearrange("p (m c) -> p m c", c=64),
                        in1=bandt[:].unsqueeze(2).to_broadcast([128, 14, 64]), op=ALU.mult),
                        reads=[t_cv], writes=[t_Yt[sl][hh]])
                items.append(it_y)
            for (dst, t_dst, wi, ntok, off, scl) in ((KT[sl], t_KT[sl], 1, TK, 0, 1.0), (QT[sl], t_QT[sl], 0, TS, 256, 0.125)):
                for c in range(ntok // 512):
                    def it_kq(dst=dst, t_dst=t_dst, wi=wi, c=c, off=off, scl=scl):
                        bk = c % 2

                        def mm(e):
                            for kc in range(KC):
                                ins = e.matmul(banks[bk][:, :], lhsT=wq[sl][:, kc, wi, :],
                                               rhs=hT[:, kc, off + c * 512:off + (c + 1) * 512],
                                               start=(kc == 0), stop=(kc == KC - 1))
                            return ins
                        b.op("pe", mm, reads=[t_wq[sl], t_hT], writes=[t_bank[bk]])
                        b.op("act", lambda e: e.activation(out=dst[:, c * 512:(c + 1) * 512], in_=banks[bk][:, :],
                                                           func=AF.Copy, scale=scl), reads=[t_bank[bk]], writes=[t_dst])
                    items.append(it_kq)
            for t4 in range(NTK // 4):
                def it_v(t4=t4):
                    bk = t4 % 2

                    def mmv(e):
                        for j in range(4):
                            t = t4 * 4 + j
                            for kc in range(KC):
                                ins = e.matmul(banks[bk][:, j * 128:(j + 1) * 128], lhsT=hT[:, kc, t * 128:(t + 1) * 128],
                                               rhs=wq[sl][:, kc, 2, :], start=(kc == 0), stop=(kc == KC - 1))
                        return ins
                    b.op("pe", mmv, reads=[t_wq[sl], t_hT], writes=[t_bank[bk]])
                    pv = banks[bk][:, :].rearrange("p (j n) -> p j n", n=128)
                    b.op("act", lambda e: e.activation(out=Vaug[sl][:, t4 * 4:t4 * 4 + 4, 0, 0:64], in_=pv[:, :, 0:64], func=AF.Copy),
                         reads=[t_bank[bk]], writes=[t_V[sl]])
                    b.op("dve", lambda e: e.tensor_copy(out=Vaug[sl][:, t4 * 4:t4 * 4 + 4, 1, 64:128], in_=pv[:, :, 64:128]),
                         reads=[t_bank[bk]], writes=[t_V[sl]])
                items.append(it_v)
            return items

        for it in proj_items(0, 0):
            it()
        def attn_hp(hp, sl):
            pending = proj_items(hp + 1, 1 - sl) if hp + 1 < KC else []
            iters = [(hh, qb) for hh in range(2) for qb in range(NQB)]
            per = (len(pending) + len(iters) - 1) // len(iters)

            def scores(n):
                hh, qb = iters[n]
                p0 = 64 * hh

                def mms(e):
                    for kk in range(6):
                        k = 5 - kk
                        kt0 = (4 * qb + 2 * k) * 64
                        ins = e.matmul(banks[4 + kk // 2][:, (kk % 2) * 256:(kk % 2) * 256 + 256],
                                       lhsT=KT[sl][p0:p0 + 64, kt0:kt0 + 128],
                                       rhs=QT[sl][p0:p0 + 64, qb * 256:(qb + 1) * 256], start=True, stop=True)
                    return ins
                b.op("pe", mms, reads=[t_KT[sl], t_QT[sl]], writes=[t_bank[4], t_bank[5], t_bank[6]])

            def softmax(n):
                hh, qb = iters[n]
                bf_ = n % 2
                for j in range(3):
                    b.op("act", lambda e, j=j: e.activation(
                        out=expS[bf_][:, 2 * j:2 * j + 2, :], in_=banks[4 + j][:, :].rearrange("p (a n) -> p a n", n=256),
                        func=AF.Exp), reads=[t_bank[4 + j]], writes=[t_expS[bf_]])
                boundary = qb in (0, NQB - 1)
                Y = Ytf[sl][hh] if boundary else Yti[sl][hh]
                ywin = bass.AP(Y, Y[:].offset, [list(Y[:].ap[0]), [128, 6], [1, 256]])
                if boundary:
                    b.op("dve", lambda e: e.tensor_tensor(out=tmpP[:], in0=expS[bf_][:], in1=ywin, op=ALU.mult),
                         reads=[t_expS[bf_], t_Yt[sl][hh]], writes=[t_tmpP])
                    b.op("pool", lambda e: e.tensor_tensor(
                        out=Pm[bf_][:].rearrange("p k (r c) -> p k r c", c=64),
                        in0=tmpP[:].rearrange("p k (r c) -> p k r c", c=64),
                        in1=rv[:, qb, :, :].unsqueeze(3).to_broadcast([128, 6, 4, 64]), op=ALU.mult),
                        reads=[t_tmpP, t_rv], writes=[t_Pm[bf_]])
                else:
                    b.op("dve", lambda e: e.tensor_tensor(out=Pm[bf_][:], in0=expS[bf_][:], in1=ywin, op=ALU.mult),
                         reads=[t_expS[bf_], t_Yt[sl][hh]], writes=[t_Pm[bf_]])

            def pvmm(n):
                hh, qb = iters[n]
                bf_ = n % 2

                def mmo(e):
                    for kk in range(6):
                        k = 5 - kk
                        tt_ = 2 * qb + k
                        ins = e.matmul(banks[3 - bf_][:, 0:256], lhsT=Vaug[sl][:, tt_, hh, :], rhs=Pm[bf_][:, kk, :],
                                       start=(kk == 0), stop=(kk == 5))
                    return ins
                b.op("pe", mmo, reads=[t_V[sl], t_Pm[bf_]], writes=[t_Oh[bf_]])

            def normo(n):
                hh, qb = iters[n]
                bf_ = n % 2
                dn, dd = (64, 0) if hh == 0 else (0, 64)
                ob = banks[3 - bf_][:, 0:256]
                b.op("dve", lambda e: e.reciprocal(out=rec[bf_][dd:dd + 64, :], in_=ob[dn:dn + 64, :]),
                     reads=[t_Oh[bf_]], writes=[t_rec[bf_]])
                b.op("dve", lambda e: e.tensor_tensor(
                    out=AO[sl][dd:dd + 64, qb * 256:(qb + 1) * 256], in0=ob[dd:dd + 64, :],
                    in1=rec[bf_][dd:dd + 64, :], op=ALU.mult), reads=[t_Oh[bf_], t_rec[bf_]], writes=[t_AO[sl]])

            import os
            MODE = os.environ.get("ATT_MODE", "pipe")
            if MODE == "seq":
                for n in range(len(iters)):
                    scores(n)
                    softmax(n)
                    pvmm(n)
                    normo(n)
                while pending:
                    pending.pop(0)()
            elif MODE == "noproj":
                scores(0)
                for n in range(len(iters)):
                    softmax(n)
                    if n + 1 < len(iters):
                        scores(n + 1)
                    pvmm(n)
                    if n > 0:
                        normo(n - 1)
                normo(len(iters) - 1)
                while pending:
                    pending.pop(0)()
            else:
                scores(0)
                for n in range(len(iters)):
                    softmax(n)
                    if n + 1 < len(iters):
                        scores(n + 1)
                    for _ in range(per):
                        if pending:
                            pending.pop(0)()
                    pvmm(n)
                    if n > 0:
                        normo(n - 1)
                normo(len(iters) - 1)
                while pending:
                    pending.pop(0)()
            b.dma(aoT[seg, hp], AO[sl][:], reads=[t_AO[sl]], writes=[t_ao[seg]])

        for hp in range(KC):
            attn_hp(hp, hp % 2)

        b.barrier()
        b.sb = common_mark
        wo_st = b.alloc([128, KC, D], F32, "wo_st")
        wo = b.alloc([128, KC, D], BF, "wo")
        t_wo = T("wo")
        b.dma(wo_st[:], w_o.rearrange("(kc p) n -> p kc n", p=128), writes=[t_wo])
        b.op("pool", lambda e: e.tensor_copy(out=wo[:], in_=wo_st[:]), reads=[t_wo], writes=[t_wo])
        grow, t_grow = load_row(q, 0, 2, "grow")
        aot = [b.alloc([128, KC, 128], BF, "aot%d" % i) for i in range(2)]
        t_aot = [T("aot0"), T("aot1")]
        xt = [b.alloc([128, D], F32, "xt%d" % i) for i in range(2)]
        t_xt = [T("xt0"), T("xt1")]
        tmpW = [b.alloc([128, D], F32, "tmpW%d" % i) for i in range(2)]
        t_tmpW = [T("tmpW0"), T("tmpW1")]
        x1t = [b.alloc([128, D], F32, "x1t%d" % i) for i in range(2)]
        t_x1t = [T("x1t0"), T("x1t1")]
        ssW = [b.alloc([128, 4], F32, "ssW%d" % i) for i in range(2)]
        t_ssW = [T("ssW0"), T("ssW1")]
        for tt in range(NTS):
            a, ta, x_, tx = aot[tt % 2], t_aot[tt % 2], xt[tt % 2], t_xt[tt % 2]
            b0 = 2 * (tt % 2)
            tmp, t_tmp, ss, t_ss = tmpW[tt % 2], t_tmpW[tt % 2], ssW[tt % 2], t_ssW[tt % 2]
            b.dma(a[:], aoT[seg, :, :, tt * 128:(tt + 1) * 128].rearrange("k p t -> p k t"), reads=[t_ao[seg]], writes=[ta])
            b.dma(x_[:], xsr[256 + tt * 128:256 + (tt + 1) * 128, :], writes=[tx])

            def mmw(e, a=a, b0=b0):
                for nh in range(2):
                    for hp in range(KC):
                        ins = e.matmul(banks[b0 + nh][:, :], lhsT=a[:, hp, :], rhs=wo[:, hp, nh * 512:(nh + 1) * 512],
                                       start=(hp == 0), stop=(hp == KC - 1))
                return ins
            b.op("pe", mmw, reads=[ta, t_wo], writes=[t_bank[b0], t_bank[b0 + 1]])
            rstd_of([banks[b0][:, :], banks[b0 + 1][:, :]], [t_bank[b0], t_bank[b0 + 1]], ss, t_ss)
            for nh in range(2):
                b.op("dve", lambda e, nh=nh, b0=b0, tmp=tmp, ss=ss: e.scalar_tensor_tensor(
                    out=tmp[:, nh * 512:(nh + 1) * 512], in0=banks[b0 + nh][:, :], scalar=ss[:, 0:1],
                    in1=grow[:, nh * 512:(nh + 1) * 512], op0=ALU.mult, op1=ALU.mult),
                    reads=[t_bank[b0 + nh], t_ss, t_grow], writes=[t_tmp])
            o, to = x1t[tt % 2], t_x1t[tt % 2]
            b.op("pool", lambda e, o=o, x_=x_, tmp=tmp: e.tensor_tensor(out=o[:], in0=x_[:], in1=tmp[:], op=ALU.add),
                 reads=[tx, t_tmp], writes=[to])
            b.dma(x1d[seg, tt * 128:(tt + 1) * 128, :], o[:], reads=[to], writes=[t_x1[seg]])

    def ffn_phase(l, srcs, t_srcs, dsts, t_dsts):
        b.barrier()
        b.sb = common_mark
        w1b = b.alloc([128, KC, DFF], BF, "w1b")
        w2b = b.alloc([128, 32, D], BF, "w2b")
        t_w1, t_w2 = T("w1"), T("w2")
        mark = b.sb
        stg = [b.alloc([128, DFF], F32, "stg%d" % i) for i in range(2)]
        t_stg = [T("stg0"), T("stg1")]
        w1v = ffn_w1[l].rearrange("(kc p) n -> p kc n", p=128)
        w2v = ffn_w2[l].rearrange("(k p) n -> p k n", p=128)
        for kc in range(KC):
            st, ts_ = stg[kc % 2], t_stg[kc % 2]
            b.dma(st[:], w1v[:, kc, :], writes=[ts_])
            b.op("pool" if kc % 2 else "dve", lambda e, st=st, kc=kc: e.tensor_copy(out=w1b[:, kc, :], in_=st[:]),
                 reads=[ts_], writes=[t_w1])
        for k4 in range(8):
            st, ts_ = stg[k4 % 2], t_stg[k4 % 2]
            b.dma(st[:].rearrange("p (k n) -> p k n", n=D), w2v[:, k4 * 4:(k4 + 1) * 4, :], writes=[ts_])
            b.op("pool" if k4 % 2 else "dve", lambda e, st=st, k4=k4: e.tensor_copy(
                out=w2b[:, k4 * 4:(k4 + 1) * 4, :], in_=st[:].rearrange("p (k n) -> p k n", n=D)),
                reads=[ts_], writes=[t_w2])
        b.barrier()
        b.sb = mark
        x1t = [b.alloc([128, D], F32, "fx%d" % i) for i in range(4)]
        t_x1t = [T("fx%d" % i) for i in range(4)]
        h2T = b.alloc([128, KC, 512], BF, "h2T")
        t_h2T = T("h2T")
        hid = b.alloc([128, 32, 512], BF, "hid")
        t_hid = T("hid")
        rl = [b.alloc([128, 512], BF, "rl%d" % i) for i in range(2)]
        t_rl = [T("rl0"), T("rl1")]
        tmpF = [b.alloc([128, D], F32, "ftmp%d" % i) for i in range(2)]
        t_tmpF = [T("ftmp0"), T("ftmp1")]
        ssF = [b.alloc([128, 4], F32, "fss%d" % i) for i in range(2)]
        t_ssF = [T("fss0"), T("fss1")]
        ss = b.alloc([128, 4], F32, "fss")
        t_ss = T("fss")
        xn = b.alloc([128, D], BF, "fxn")
        t_xn = T("fxn")
        xn2 = b.alloc([128, D], BF, "fxn2")
        ss2 = b.alloc([128, 4], F32, "fss2")
        xnN, t_xnN = [xn, xn2], [t_xn, T("fxn2")]
        ssN, t_ssN = [ss, ss2], [t_ss, T("fss2")]
        mark2 = b.sb
        oi = 0
        for seg in range(len(srcs)):
            q = 0 if seg < 2 else 1
            if seg in (0, 2):
                if seg == 2:
                    b.barrier()
                b.sb = mark2
                acol, t_acol = load_cols(q, l, 3, "facol")
                bcol, t_bcol = load_cols(q, l, 4, "fbcol")
                t_ab = T("fab")
                b.op("dve", lambda e, acol=acol: e.tensor_copy(out=acol[:, 0:1], in_=acol[:, 0:1]),
                     reads=[t_acol, t_bcol], writes=[t_ab])
                grow, t_grow = load_row(q, l, 5, "fgrow")
            for blk in range(TS // 512):
                def f_stats(tt, blk=blk, seg=seg):
                    r0 = blk * 512 + tt * 128
                    b.dma(x1t[tt][:], srcs[seg][r0:r0 + 128, :], reads=[t_srcs[seg]], writes=[t_x1t[tt]])
                    norm_stats(x1t[tt], t_x1t[tt], ssN[tt % 2], t_ssN[tt % 2], xnN[tt % 2], t_xnN[tt % 2])
                f_stats(0)
                for tt in range(4):
                    norm_tr(xnN[tt % 2], t_xnN[tt % 2])
                    if tt + 1 < 4:
                        f_stats(tt + 1)
                    norm_evac(acol, bcol, t_ab, h2T, t_h2T, tt * 128)
                for m in range(32):
                    bk = m % 2

                    def mmu(e, m=m, bk=bk):
                        for kc in range(KC):
                            ins = e.matmul(banks[bk][:, :], lhsT=w1b[:, kc, m * 128:(m + 1) * 128], rhs=h2T[:, kc, :],
                                           start=(kc == 0), stop=(kc == KC - 1))
                        return ins
                    b.op("pe", mmu, reads=[t_w1, t_h2T], writes=[t_bank[bk]])
                    b.op("act", lambda e, bk=bk: e.activation(out=rl[bk][:], in_=banks[bk][:, :], func=AF.Relu),
                         reads=[t_bank[bk]], writes=[t_rl[bk]])
                    b.op("pool", lambda e, m=m, bk=bk: e.tensor_tensor(out=hid[:, m, :], in0=rl[bk][:], in1=rl[bk][:], op=ALU.mult),
                         reads=[t_rl[bk]], writes=[t_hid])
                for tt in range(4):
                    b0 = 2 + 2 * (tt % 2)
                    tmp_, t_tmp_, ss_, t_ss_ = tmpF[tt % 2], t_tmpF[tt % 2], ssF[tt % 2], t_ssF[tt % 2]

                    def mmd(e, tt=tt, b0=b0):
                        for nh in range(2):
                            for k in range(32):
                                ins = e.matmul(banks[b0 + nh][:, :], lhsT=hid[:, k, tt * 128:(tt + 1) * 128],
                                               rhs=w2b[:, k, nh * 512:(nh + 1) * 512], start=(k == 0), stop=(k == 31))
                        return ins
                    b.op("pe", mmd, reads=[t_hid, t_w2], writes=[t_bank[b0], t_bank[b0 + 1]])
                    rstd_of([banks[b0][:, :], banks[b0 + 1][:, :]], [t_bank[b0], t_bank[b0 + 1]], ss_, t_ss_)
                    for nh in range(2):
                        b.op("dve", lambda e, nh=nh, b0=b0, tmp_=tmp_, ss_=ss_: e.scalar_tensor_tensor(
                            out=tmp_[:, nh * 512:(nh + 1) * 512], in0=banks[b0 + nh][:, :], scalar=ss_[:, 0:1],
                            in1=grow[:, nh * 512:(nh + 1) * 512], op0=ALU.mult, op1=ALU.mult),
                            reads=[t_bank[b0 + nh], t_ss_, t_grow], writes=[t_tmp_])
                    b.op("pool", lambda e, tt=tt, tmp_=tmp_: e.tensor_tensor(out=tmp_[:], in0=x1t[tt][:], in1=tmp_[:], op=ALU.add),
                         reads=[t_x1t[tt], t_tmp_], writes=[t_tmp_])
                    r0 = blk * 512 + tt * 128
                    b.dma(dsts[seg][r0:r0 + 128, :], tmp_[:], reads=[t_tmp_], writes=[t_dsts[seg]])

    ffn_phase(0, [x1d[i] for i in range(NSEG)], t_x1, [x2d[i] for i in range(NSEG)], t_x2)
    if upto == "l0":
        b.barrier()
        return nc, b

    import math
    I32 = mybir.dt.int32
    NCH = TS // 8
    NB = NCH // 16
    NCT = NCH // 128
    LOG2NC = int(round(math.log2(NCH)))
    a_re_in = din("s5_a_re", [2, 64, 64])
    a_im_in = din("s5_a_im", [2, 64, 64])
    ldt_in = din("s5_log_dt", [2, 64])
    b_in = [din("s5_b_re", [2, 64, 64, 16]), din("s5_b_im", [2, 64, 64, 16])]
    c_in = [din("s5_c_re", [2, 64, 16, 64]), din("s5_c_im", [2, 64, 16, 64])]
    d_in = din("s5_d", [D])
    wglu_in = din("s5_w_glu", [D, 2 * D])
    mfb_in = din("mfb", [2, 128, 128])
    sel_in = din("sel", [128, 8])
    winD = nc.dram_tensor("winD", [8, 128, 8 * 2 * 2 * 64], BF).ap()
    woutD = nc.dram_tensor("woutD", [8, 128, 2 * 2 * 4 * 128], BF).ap()
    wtoepD = nc.dram_tensor("wtoepD", [8, 128, 8 * 128], BF).ap()
    ytd = nc.dram_tensor("ytd", [3, TS, D], BF).ap()
    x3d = nc.dram_tensor("x3d", [3, TS, D], F32, kind="ExternalOutput" if debug else "Internal").ap()
    t_wD = T("wD")
    t_yt = [T("yt%d" % i) for i in range(3)]
    t_x3 = [T("x3_%d" % i) for i in range(3)]

    def tt(eng, out, in0, in1, op, reads, writes):
        return b.op(eng, lambda e: e.tensor_tensor(out=out, in0=in0, in1=in1, op=op), reads=reads, writes=writes)

    def tsc(eng, out, in0, s1, s2, op0, op1, reads, writes):
        if op1 is None:
            return b.op(eng, lambda e: e.tensor_scalar(out=out, in0=in0, scalar1=s1, scalar2=None, op0=op0), reads=reads, writes=writes)
        return b.op(eng, lambda e: e.tensor_scalar(out=out, in0=in0, scalar1=s1, scalar2=s2, op0=op0, op1=op1), reads=reads, writes=writes)

    def cp(eng, out, in_, reads, writes):
        if eng == "act":
            return b.op(eng, lambda e: e.activation(out=out, in_=in_, func=AF.Copy), reads=reads, writes=writes)
        return b.op(eng, lambda e: e.tensor_copy(out=out, in_=in_), reads=reads, writes=writes)

    def cmul(eng, o_r, o_i, ar, ai, br, bi, t1, t2, tk):
        tt(eng, t1, ar, br, ALU.mult, tk, tk)
        tt(eng, t2, ai, bi, ALU.mult, tk, tk)
        tt(eng, o_r, t1, t2, ALU.subtract, tk, tk)
        tt(eng, t1, ar, bi, ALU.mult, tk, tk)
        tt(eng, t2, ai, br, ALU.mult, tk, tk)
        tt(eng, o_i, t1, t2, ALU.add, tk, tk)

    def cmad(eng, d_r, d_i, cr, ci, s_r, s_i, t1, reads, writes):
        rw = list(reads) + list(writes)
        tt(eng, t1, cr, s_r, ALU.mult, rw, writes)
        tt(eng, d_r, d_r, t1, ALU.add, rw, writes)
        tt(eng, t1, ci, s_i, ALU.mult, rw, writes)
        tt(eng, d_r, d_r, t1, ALU.subtract, rw, writes)
        tt(eng, t1, cr, s_i, ALU.mult, rw, writes)
        tt(eng, d_i, d_i, t1, ALU.add, rw, writes)
        tt(eng, t1, ci, s_r, ALU.mult, rw, writes)
        tt(eng, d_i, d_i, t1, ALU.add, rw, writes)

    b.barrier()
    b.sb = common_mark
    L8 = b.alloc([128, 2, 2, 32], F32, "L8")
    L128 = b.alloc([128, 2, 2, 32], F32, "L128")
    LSEG = b.alloc([128, 2, 2, 32], F32, "LSEG")
    selt = b.alloc([128, 8], F32, "selt")
    CA8 = b.alloc([128, 32, 2, 2, 2], F32, "CA8")
    CA128 = b.alloc([128, 32, 2, 2, 2], F32, "CA128")
    HIN = b.alloc([128, 3, 32, 2, 2], F32, "HIN")
    t_Lt, t_sel, t_HIN = T("Lt"), T("sel"), T("HIN")
    b.dma(selt[:], sel_in, writes=[t_sel])
    s5_mark = b.sb

    tP = T("prep")
    P = [tP]
    AR = b.alloc([128, 2, 32], F32, "AR")
    AI = b.alloc([128, 2, 32], F32, "AI")
    DT = b.alloc([128, 2, 32], F32, "DT")
    for d in range(2):
        for g2 in range(2):
            b.dma(AR[64 * g2:64 * g2 + 64, d, :], a_re_in[d].rearrange("(gq g2) p -> g2 p gq", g2=2)[g2], writes=P, slow=True)
            b.dma(AI[64 * g2:64 * g2 + 64, d, :], a_im_in[d].rearrange("(gq g2) p -> g2 p gq", g2=2)[g2], writes=P, slow=True)
            b.dma(DT[64 * g2:64 * g2 + 64, d:d + 1, :],
                  ldt_in[d:d + 1, :].rearrange("o (gq g2) -> o g2 gq", g2=2)[:, g2, :].partition_broadcast(64), writes=P, slow=True)
    b.op("act", lambda e: e.activation(out=DT[:], in_=DT[:], func=AF.Exp), reads=P, writes=P)
    XR = b.alloc([128, 2, 32], F32, "XR")
    TH = b.alloc([128, 2, 32], F32, "TH")
    tt("dve", XR[:], AR[:], DT[:], ALU.mult, P, P)
    tt("dve", TH[:], AI[:], DT[:], ALU.mult, P, P)
    PW = b.alloc([128, 9, 2, 2, 32], F32, "PW")
    mg = b.alloc([128, 2, 32], F32, "mg")
    arg = b.alloc([128, 2, 32], F32, "arg")
    uu = b.alloc([128, 2, 32], F32, "uu")
    ni = b.alloc([128, 2, 32], I32, "ni")
    nf = b.alloc([128, 2, 32], F32, "nf")
    for k in range(9):
        b.op("act", lambda e, k=k: e.activation(out=mg[:], in_=XR[:], func=AF.Exp, scale=float(k)), reads=P, writes=P)
        for ri_idx, ph in ((1, 0.0), (0, math.pi / 2)):
            tsc("dve", arg[:], TH[:], float(k), ph, ALU.mult, ALU.add, P, P)
            tsc("dve", uu[:], arg[:], 1.0 / (2 * math.pi), None, ALU.mult, None, P, P)
            cp("dve", ni[:], uu[:], P, P)
            cp("dve", nf[:], ni[:], P, P)
            b.op("dve", lambda e: e.scalar_tensor_tensor(out=uu[:], in0=nf[:], scalar=-2 * math.pi, in1=arg[:],
                                                         op0=ALU.mult, op1=ALU.add), reads=P, writes=P)
            b.op("act", lambda e: e.activation(out=nf[:], in_=uu[:], func=AF.Sin), reads=P, writes=P)
            tt("dve", PW[:, k, ri_idx, :, :], mg[:], nf[:], ALU.mult, P, P)
    EIN = b.alloc([128, 2, 2, 32, 8], F32, "EIN")
    EOUT = b.alloc([128, 2, 2, 32, 8], F32, "EOUT")
    for k in range(9):
        if k <= 7:
            cp("pool", EIN[:, :, 0, :, 7 - k], PW[:, k, :, 0, :], P, P)
            cp("pool", EIN[:, :, 1, :, k], PW[:, k, :, 1, :], P, P)
        if k >= 1:
            cp("pool", EOUT[:, :, 0, :, k - 1], PW[:, k, :, 0, :], P, P)
            cp("pool", EOUT[:, :, 1, :, 8 - k], PW[:, k, :, 1, :], P, P)
    zr = b.alloc([128, 2, 32], F32, "zr")
    zi = b.alloc([128, 2, 32], F32, "zi")
    q1 = b.alloc([128, 2, 32], F32, "q1")
    q2 = b.alloc([128, 2, 32], F32, "q2")
    q3 = b.alloc([128, 2, 32], F32, "q3")
    nr = b.alloc([128, 2, 32], F32, "nr")
    tsc("dve", nr[:], PW[:, 1, 0, :, :], -1.0, None, ALU.add, None, P, P)
    nim = PW[:, 1, 1, :, :]
    tt("dve", q1[:], AR[:], AR[:], ALU.mult, P, P)
    tt("dve", q2[:], AI[:], AI[:], ALU.mult, P, P)
    tt("dve", q1[:], q1[:], q2[:], ALU.add, P, P)
    b.op("dve", lambda e: e.reciprocal(out=q3[:], in_=q1[:]), reads=P, writes=P)
    tt("dve", q1[:], nr[:], AR[:], ALU.mult, P, P)
    tt("dve", q2[:], nim, AI[:], ALU.mult, P, P)
    tt("dve", q1[:], q1[:], q2[:], ALU.add, P, P)
    tt("dve", zr[:], q1[:], q3[:], ALU.mult, P, P)
    tt("dve", q1[:], nim, AR[:], ALU.mult, P, P)
    tt("dve", q2[:], nr[:], AI[:], ALU.mult, P, P)
    tt("dve", q1[:], q1[:], q2[:], ALU.subtract, P, P)
    tt("dve", zi[:], q1[:], q3[:], ALU.mult, P, P)
    PL = [tP, t_Lt]
    cp("dve", L8[:], PW[:, 8, :, :, :], PL, PL)
    I8 = b.alloc([128, 2, 2, 32], F32, "I8")
    tt("dve", q1[:], L8[:, 0], L8[:, 0], ALU.mult, PL, P)
    tt("dve", q2[:], L8[:, 1], L8[:, 1], ALU.mult, PL, P)
    tt("dve", q1[:], q1[:], q2[:], ALU.add, P, P)
    b.op("dve", lambda e: e.reciprocal(out=q3[:], in_=q1[:]), reads=P, writes=P)
    tt("dve", I8[:, 0], L8[:, 0], q3[:], ALU.mult, PL, P)
    tt("dve", q1[:], L8[:, 1], q3[:], ALU.mult, PL, P)
    tsc("dve", I8[:, 1], q1[:], -1.0, None, ALU.mult, None, P, P)
    sqa = b.alloc([128, 2, 2, 32], F32, "sqa")
    sqb = b.alloc([128, 2, 2, 32], F32, "sqb")
    cur = L8
    for it in range(LOG2NC):
        dst = L128 if it == 3 else (LSEG if it == LOG2NC - 1 else (sqa if cur is not sqa else sqb))
        tt("dve", q1[:], cur[:, 0], cur[:, 0], ALU.mult, PL, P)
        tt("dve", q2[:], cur[:, 1], cur[:, 1], ALU.mult, PL, P)
        tt("dve", dst[:, 0], q1[:], q2[:], ALU.subtract, PL, PL)
        tt("dve", q1[:], cur[:, 0], cur[:, 1], ALU.mult, PL, P)
        tsc("dve", dst[:, 1], q1[:], 2.0, None, ALU.mult, None, PL, PL)
        cur = dst
    if upto == "prepA":
        b.barrier()
        return nc, b
    Braw = b.alloc([128, 2, 2, 32, 16], F32, "Braw")
    Bb = b.alloc([128, 2, 2, 32, 16], F32, "Bb")
    for ri in range(2):
        for d in range(2):
            for g2 in range(2):
                b.dma(Braw[64 * g2:64 * g2 + 64, ri, d, :, :],
                      b_in[ri][d].rearrange("(gq g2) p j -> g2 p gq j", g2=2)[g2], writes=P)
    bt1 = b.alloc([128, 32, 16], F32, "bt1")
    bt2 = b.alloc([128, 32, 16], F32, "bt2")
    for d in range(2):
        zrb = zr[:, d, :].unsqueeze(2).to_broadcast([128, 32, 16])
        zib = zi[:, d, :].unsqueeze(2).to_broadcast([128, 32, 16])
        cmul("dve", Bb[:, 0, d], Bb[:, 1, d], zrb, zib, Braw[:, 0, d], Braw[:, 1, d], bt1[:], bt2[:], P)
    Cn = b.alloc([128, 8, 2, 2, 64], F32, "Cn")
    Ct = b.alloc([128, 2, 2, 32, 16], F32, "Ct")
    for ri in range(2):
        for d in range(2):
            b.dma(Cn[:, :, ri, d, :], c_in[ri][d].rearrange("(gb gl) i p -> (gl i) gb p", gl=8), writes=P)
    for gb in range(8):
        bk = gb % 2

        def trc(e, gb=gb, bk=bk):
            for ri in range(2):
                for d in range(2):
                    c4 = ri * 2 + d
                    ins = e.transpose(banks[bk][0:64, c4 * 128:(c4 + 1) * 128], Cn[:, gb, ri, d, :], ident_f[:, :])
            return ins
        b.op("pe", trc, reads=P + [t_ident], writes=[t_bank[bk]])
        for g2 in range(2):
            src = banks[bk][0:64, :].rearrange("p (c q g i) -> p c q g i", c=4, q=4, g=2)[:, :, :, g2, :]
            dst = Ct[64 * g2:64 * g2 + 64, :, :, 4 * gb:4 * gb + 4, :].rearrange("p r d q i -> p (r d) q i")
            cp("act", dst, src, [t_bank[bk]], P)
    if upto == "prepB":
        b.barrier()
        return nc, b
    dcol = b.alloc([128, 64], F32, "dcol")
    for s_ in range(8):
        b.dma(dcol[16 * s_:16 * s_ + 16, :], d_in.rearrange("(g j) -> j g", j=16), writes=P, slow=True)
    MF = b.alloc([128, 128], F32, "MF")
    MB = b.alloc([128, 128], F32, "MB")
    b.dma(MF[:], mfb_in[0], writes=P)
    b.dma(MB[:], mfb_in[1], writes=P)
    XW = b.alloc([128, 2, 2, 4, 8, 16], F32, "XW")
    XI = b.alloc([128, 2, 2, 4, 8, 16], F32, "XI")
    WoF = b.alloc([128, 2, 2, 4, 8, 16], F32, "WoF")
    XIz = [b.alloc([128, 2, 2, 4, 8, 16], F32, "XIz%d" % i) for i in range(2)]
    w1_ = b.alloc([128, 4, 8, 16], F32, "w1_")
    w2_ = b.alloc([128, 4, 8, 16], F32, "w2_")
    wob = b.alloc([128, 2 * 2 * 4 * 128], BF, "wob")
    wib = b.alloc([128, 8 * 2 * 2 * 64], BF, "wib")
    wtb = b.alloc([128, 8, 128], BF, "wtb")
    tq1 = b.alloc([128, 128], F32, "tq1")
    tq2 = b.alloc([128, 128], F32, "tq2")
    t_wo_, t_wi_, t_wt_ = T("wob"), T("wib"), T("wtb")
    SH = [128, 4, 8, 16]
    for gb in range(8):
        gqs = slice(4 * gb, 4 * gb + 4)
        for d in range(2):
            er = EIN[:, 0, d, gqs, :].unsqueeze(3).to_broadcast(SH)
            ei = EIN[:, 1, d, gqs, :].unsqueeze(3).to_broadcast(SH)
            br_ = Bb[:, 0, d, gqs, :].unsqueeze(2).to_broadcast(SH)
            bi_ = Bb[:, 1, d, gqs, :].unsqueeze(2).to_broadcast(SH)
            cmul("dve", XW[:, 0, d], XW[:, 1, d], er, ei, br_, bi_, w1_[:], w2_[:], P)
            ir = I8[:, 0, d, gqs].unsqueeze(2).unsqueeze(3).to_broadcast(SH)
            ii = I8[:, 1, d, gqs].unsqueeze(2).unsqueeze(3).to_broadcast(SH)
            cmul("pool", XI[:, 0, d], XI[:, 1, d], XW[:, 0, d], XW[:, 1, d], ir, ii, w1_[:], w2_[:], P)
            cr = Ct[:, 0, d, gqs, :].unsqueeze(2).to_broadcast(SH)
            ci = Ct[:, 1, d, gqs, :].unsqueeze(2).to_broadcast(SH)
            eor = EOUT[:, 0, d, gqs, :].unsqueeze(3).to_broadcast(SH)
            eoi = EOUT[:, 1, d, gqs, :].unsqueeze(3).to_broadcast(SH)
            cmul("dve", WoF[:, 0, d], w2_[:], cr, ci, eor, eoi, w1_[:], WoF[:, 1, d], P)
            tsc("dve", WoF[:, 1, d], w2_[:], -1.0, None, ALU.mult, None, P, P)
        if upto == "prepC":
            break
        cp("act", wob[:], WoF[:].rearrange("p r d q s i -> p (r d q s i)"), P, [t_wo_])
        b.dma(woutD[gb], wob[:], reads=[t_wo_], writes=[t_wD])
        if upto == "prepD":
            continue
        for hb_ in range(4):
            def trw(e, hb_=hb_):
                for ri in range(2):
                    for d in range(2):
                        col = (ri * 2 + d) * 128
                        ins = e.transpose(banks[hb_][:, col:col + 128],
                                          XW[:, ri, d, hb_, :, :].rearrange("p s j -> p (s j)"), ident_f[:, :])
                return ins
            b.op("pe", trw, reads=P + [t_ident], writes=[t_bank[hb_]])
            cp("act", wib[:, hb_ * 512:(hb_ + 1) * 512].rearrange("p (g c n) -> p c g n", g=2, c=4),
               banks[hb_][:, :].rearrange("p (c g n) -> p c g n", c=4, g=2), [t_bank[hb_]], [t_wi_])
        b.dma(winD[gb], wib[:], reads=[t_wi_], writes=[t_wD])
        if upto == "prepE":
            continue
        for g2 in range(2):
            b.op("pool", lambda e, g2=g2: e.memset(XIz[g2][:], 0.0), reads=P, writes=P)
            cp("pool", XIz[g2][64 * g2:64 * g2 + 64], XI[64 * g2:64 * g2 + 64], P, P)
        for gl in range(8):
            g2, glq = gl % 2, gl // 2
            g = 8 * gb + gl
            bk = 4 + gl % 2

            def mmt(e, g2=g2, glq=glq, bk=bk):
                for d in range(2):
                    for ri in range(2):
                        ins = e.matmul(banks[bk][:, d * 128:(d + 1) * 128],
                                       lhsT=XIz[g2][:, ri, d, glq, :, :].rearrange("p s j -> p (s j)"),
                                       rhs=WoF[:, ri, d, glq, :, :].rearrange("p s i -> p (s i)"),
                                       start=(ri == 0), stop=(ri == 1))
                return ins
            b.op("pe", mmt, reads=P, writes=[t_bank[bk]])
            tt("dve", tq1[:], banks[bk][:, 0:128], MF[:], ALU.mult, P + [t_bank[bk]], P)
            tt("dve", tq2[:], banks[bk][:, 128:256], MB[:], ALU.mult, P + [t_bank[bk]], P)
            tt("dve", tq1[:], tq1[:], tq2[:], ALU.add, P, P)
            b.op("dve", lambda e, gl=gl, g=g: e.scalar_tensor_tensor(
                out=wtb[:, gl, :], in0=ident_f[:, :], scalar=dcol[:, g:g + 1], in1=tq1[:], op0=ALU.mult, op1=ALU.add),
                reads=P + [t_ident], writes=[t_wt_])
        b.dma(wtoepD[gb], wtb[:].rearrange("p g n -> p (g n)"), reads=[t_wt_], writes=[t_wD])

    if upto == "prep":
        b.barrier()
        return nc, b
    b.barrier()
    b.sb = s5_mark
    arow1b = b.alloc([128, D], F32, "arow1")
    brow1b = b.alloc([128, D], F32, "brow1")
    t_ab1b = T("ab1")
    arow1 = [arow1b, arow1b]
    brow1 = [brow1b, brow1b]
    t_ab1 = [t_ab1b, t_ab1b]
    curq = [None]

    def set_q(q):
        if curq[0] == q:
            return
        curq[0] = q
        b.dma(arow1b[:], mv[q, 1, 0:1, :].partition_broadcast(128), reads=[t_mv], writes=[t_ab1b])
        b.dma(brow1b[:], mv[q, 1, 1:2, :].partition_broadcast(128), reads=[t_mv], writes=[t_ab1b])
    EA = b.alloc([128, 8, 32, 2, 2], F32, "EA")
    t_EA = T("EA")
    HBs = [b.alloc([128, 32, 2, 2, NB], F32, "HB%d" % i) for i in range(3)]
    t_HB = [T("HB%d" % i) for i in range(3)]
    s5_mark2 = b.sb

    U8 = b.alloc([128, 64, NCH], BF, "U8")
    t_U8 = T("U8")
    S_off = b.sb
    S = b.alloc([128, 16, 2, 2, max(NCH, 192)], F32, "S")
    S_end = b.sb
    SCW = max(NCH, 192)
    t_S = [T("S_f"), T("S_b")]
    tmpA = [b.alloc([128, 16, 2, NB], F32, "tmpA%d" % i) for i in range(2)]
    tmpA2 = [b.alloc([128, 16, 2, NB], F32, "tmpA2%d" % i) for i in range(2)]
    tmpB = [b.alloc([128, 16, NB], F32, "tmpB%d" % i) for i in range(2)]
    t_tm = [T("tm0"), T("tm1")]
    winb = [b.alloc([128, 8, 2, 2, 64], BF, "winb0")] * 2
    t_winb = [T("winb0")] * 2
    HBtmp = b.alloc([128, 32, 2, 2, NB], F32, "HBtmp")
    t_HBtmp = T("HBtmp")
    sm_t = b.alloc([128, 32, 2], F32, "sm_t")
    s5_mark3 = b.sb
    SENG = ("dve", "pool")
    wcount = [0]

    def build_U8(src, t_src, q):
        set_q(q)
        mk = b.sb
        b.sb = S_off
        xc = b.alloc([128, 8, D], F32, "xc")
        hb = b.alloc([128, 64, 8, 16], BF, "hb")
        assert b.sb <= S_end
        b.sb = mk
        ss8 = b.alloc([128, 24], F32, "ss8")
        t_xc, t_hb, t_s8 = t_S[0], t_S[1], T("ss8")
        for ct in range(NCT):
            b.dma(xc[:], src[ct * 1024:(ct + 1) * 1024, :].rearrange("(c s) d -> c s d", s=8), reads=[t_src], writes=[t_xc])
            for s_ in range(8):
                b.op("act", lambda e, s_=s_: e.activation(out=junk[:], in_=xc[:, s_, :], func=AF.Square,
                                                          accum_out=ss8[:, s_:s_ + 1]), reads=[t_xc], writes=[t_junk, t_s8])
            tsc("dve", ss8[:, 8:16], ss8[:, 0:8], 1.0 / D, EPS, ALU.mult, ALU.add, [t_s8], [t_s8])
            b.op("act", lambda e: e.activation(out=ss8[:, 16:24], in_=ss8[:, 8:16], func=AF.Sqrt), reads=[t_s8], writes=[t_s8])
            b.op("dve", lambda e: e.reciprocal(out=ss8[:, 8:16], in_=ss8[:, 16:24]), reads=[t_s8], writes=[t_s8])
            for s_ in range(8):
                b.op("dve", lambda e, s_=s_: e.scalar_tensor_tensor(
                    out=xc[:, s_, :], in0=xc[:, s_, :], scalar=ss8[:, 8 + s_:9 + s_], in1=arow1[q][:],
                    op0=ALU.mult, op1=ALU.mult), reads=[t_xc, t_s8, t_ab1[q]], writes=[t_xc])
            for s_ in range(8):
                b.op("pool", lambda e, s_=s_: e.tensor_tensor(
                    out=hb[:, :, s_, :], in0=xc[:, s_, :].rearrange("c (g j) -> c g j", j=16),
                    in1=brow1[q][:].rearrange("c (g j) -> c g j", j=16), op=ALU.add),
                    reads=[t_xc, t_ab1[q]], writes=[t_hb])
            for g8 in range(8):
                def tr(e, g8=g8):
                    for gi in range(8):
                        ins = e.transpose(tp_ps[:, gi, :], hb[:, g8 * 8 + gi, :, :].rearrange("c s j -> c (s j)"), ident[:])
                    return ins
                b.op("pe", tr, reads=[t_hb, t_ident], writes=[t_tp])
                cp("act", U8[:, g8 * 8:(g8 + 1) * 8, ct * 128:(ct + 1) * 128], tp_ps[:, :, :], [t_tp], [t_U8])
        b.sb = mk

    def state_mm(half):
        for sb_ in range(4):
            gb = half * 4 + sb_
            wb, twb = winb[wcount[0] % 2], t_winb[wcount[0] % 2]
            wcount[0] += 1
            b.dma(wb[:].rearrange("p g r d n -> p (g r d n)"), winD[gb], reads=[t_wD], writes=[twb])
            for glq in range(4):
                gq = sb_ * 4 + glq
                nbk = (4 * NCH + 511) // 512
                bks = [(glq % 2) * nbk + i for i in range(nbk)]

                def mm(e, glq=glq, gb=gb, wb=wb, bks=bks):
                    for g2 in range(2):
                        gl = 2 * glq + g2
                        g = 8 * gb + gl
                        for d in range(2):
                            for ri in range(2):
                                c4 = d * 2 + ri
                                bk = bks[(c4 * NCH) // 512]
                                off = (c4 * NCH) % 512
                                ins = e.matmul(banks[bk][64 * g2:64 * g2 + 64, off:off + NCH], lhsT=wb[:, gl, ri, d, :],
                                               rhs=U8[:, g, :], start=True, stop=True, tile_position=(0, 64 * g2))
                    return ins
                b.op("pe", mm, reads=[twb, t_U8], writes=[t_bank[x] for x in bks])
                for i, bk in enumerate(bks):
                    n = min(512, 4 * NCH - i * 512)
                    ncmb = n // NCH
                    c40 = (i * 512) // NCH
                    for cc in range(ncmb):
                        c4 = c40 + cc
                        cp("act", S[:, gq, c4 // 2, c4 % 2, 0:NCH], banks[bk][:, cc * NCH:(cc + 1) * NCH], [t_bank[bk]], t_S)

    def c0_scan(half):
        S6 = S[:, :, :, :, 0:NCH].rearrange("p g d r (a c) -> p g d r a c", c=16)
        gqs = slice(half * 16, half * 16 + 16)
        for k in range(1, 16):
            for d in range(2):
                eng = SENG[d]
                c0, pv = (k, k - 1) if d == 0 else (15 - k, 16 - k)
                lr = L8[:, 0, d, gqs]
                li = L8[:, 1, d, gqs]
                tk = [t_S[d], t_tm[d]]
                tt(eng, tmpA[d][:], S6[:, :, d, :, :, pv], lr.unsqueeze(2).unsqueeze(3).to_broadcast([128, 16, 2, NB]),
                   ALU.mult, tk + [t_Lt], tk)
                tt(eng, S6[:, :, d, :, :, c0], S6[:, :, d, :, :, c0], tmpA[d][:], ALU.add, tk, tk)
                lib = li.unsqueeze(2).to_broadcast([128, 16, NB])
                tt(eng, tmpB[d][:], S6[:, :, d, 1, :, pv], lib, ALU.mult, tk + [t_Lt], tk)
                tt(eng, S6[:, :, d, 0, :, c0], S6[:, :, d, 0, :, c0], tmpB[d][:], ALU.subtract, tk, tk)
                tt(eng, tmpB[d][:], S6[:, :, d, 0, :, pv], lib, ALU.mult, tk + [t_Lt], tk)
                tt(eng, S6[:, :, d, 1, :, c0], S6[:, :, d, 1, :, c0], tmpB[d][:], ALU.add, tk, tk)

    def pass1(HB, t_hb_):
        S6 = S[:, :, :, :, 0:NCH].rearrange("p g d r (a c) -> p g d r a c", c=16)
        for half in range(2):
            state_mm(half)
            c0_scan(half)
            gqs = slice(half * 16, half * 16 + 16)
            cp("dve", HB[:, gqs, 0, :, :], S6[:, :, 0, :, :, 15], [t_S[0]], [t_hb_])
            cp("pool", HB[:, gqs, 1, :, :], S6[:, :, 1, :, :, 0], [t_S[1]], [t_hb_])

    def c1_scan(HB, t_hb_, hins, dirs=(0, 1)):
        tk = [t_hb_, t_HBtmp]
        for d in dirs:
            lr = L128[:, 0, d, :]
            li = L128[:, 1, d, :]
            order = list(range(NB)) if d == 0 else list(range(NB - 1, -1, -1))
            prev = None
            for c1 in order:
                if prev is None:
                    prev = c1
                    if d not in hins:
                        continue
                    sr, si, tks = hins[d]
                    rd = tk + list(tks) + [t_Lt]
                else:
                    sr, si = HB[:, :, d, 0, prev], HB[:, :, d, 1, prev]
                    rd = tk + [t_Lt]
                    prev = c1
                cmad("dve", HB[:, :, d, 0, c1], HB[:, :, d, 1, c1], lr, li, sr, si, sm_t[:, :, 0], rd, tk)

    def seg_source(seg):
        return x2d[seg], t_x2[seg]

    if NSEG > 3:
        for j in range(8):
            seg = 3 + j
            src, tsrc = seg_source(seg)
            build_U8(src, tsrc, 1)
            pass1(HBtmp, t_HBtmp)
            c1_scan(HBtmp, t_HBtmp, {})
            cp("dve", EA[:, j, :, 0, :], HBtmp[:, :, 0, :, NB - 1], [t_HBtmp], [t_EA])
            cp("dve", EA[:, j, :, 1, :], HBtmp[:, :, 1, :, 0], [t_HBtmp], [t_EA])
        acc = b.alloc([128, 32, 2, 2], F32, "acc")
        acc2 = b.alloc([128, 32, 2, 2], F32, "acc2")
        t_acc = T("acc")
        b.op("dve", lambda e: e.memset(HIN[:, 2], 0.0), writes=[t_HIN])
        for d in range(2):
            b.op("dve", lambda e: e.memset(acc[:], 0.0), writes=[t_acc])
            order = list(range(8)) if d == 0 else list(range(7, -1, -1))
            for j in order:
                b.op("dve", lambda e, j=j, d=d: e.scalar_tensor_tensor(
                    out=HIN[:, 2, :, d, :], in0=acc[:, :, d, :], scalar=selt[:, j:j + 1], in1=HIN[:, 2, :, d, :],
                    op0=ALU.mult, op1=ALU.add), reads=[t_acc, t_sel], writes=[t_HIN])
                cp("dve", acc2[:, :, d, :], EA[:, j, :, d, :], [t_EA, t_acc], [t_acc])
                cmad("dve", acc2[:, :, d, 0], acc2[:, :, d, 1], LSEG[:, 0, d, :], LSEG[:, 1, d, :],
                     acc[:, :, d, 0], acc[:, :, d, 1], sm_t[:, :, 0], [t_Lt], [t_acc])
                cp("dve", acc[:, :, d, :], acc2[:, :, d, :], [t_acc], [t_acc])

    if upto == "ea":
        b.barrier()
        return nc, b
    own = [0, 1, 2] if NSEG > 3 else [0, 1]
    for seg in own:
        src, tsrc = seg_source(seg)
        build_U8(src, tsrc, 0 if seg < 2 else 1)
        pass1(HBs[seg], t_HB[seg])
    hin_of = {seg: {} for seg in own}
    c1_scan(HBs[1], t_HB[1], {}, dirs=(1,))
    hin_of[0][1] = (HBs[1][:, :, 1, 0, 0], HBs[1][:, :, 1, 1, 0], [t_HB[1]])
    c1_scan(HBs[0], t_HB[0], hin_of[0], dirs=(0, 1))
    hin_of[1][0] = (HBs[0][:, :, 0, 0, NB - 1], HBs[0][:, :, 0, 1, NB - 1], [t_HB[0]])
    c1_scan(HBs[1], t_HB[1], hin_of[1], dirs=(0,))
    if 2 in own:
        for d in range(2):
            hin_of[2][d] = (HIN[:, 2, :, d, 0], HIN[:, 2, :, d, 1], [t_HIN])
        c1_scan(HBs[2], t_HB[2], hin_of[2])
    CXb = HBtmp
    t_CXb = t_HBtmp

    def make_CX(seg):
        HB = HBs[seg]
        b.op("pool", lambda e: e.memset(CXb[:], 0.0), writes=[t_CXb])
        cp("pool", CXb[:, :, 0, :, 1:NB], HB[:, :, 0, :, 0:NB - 1], [t_HB[seg]], [t_CXb])
        cp("pool", CXb[:, :, 1, :, 0:NB - 1], HB[:, :, 1, :, 1:NB], [t_HB[seg]], [t_CXb])
        for d, c1 in ((0, 0), (1, NB - 1)):
            if d in hin_of[seg]:
                sr, si, tks = hin_of[seg][d]
                cp("pool", CXb[:, :, d, 0, c1], sr, list(tks), [t_CXb])
                cp("pool", CXb[:, :, d, 1, c1], si, list(tks), [t_CXb])
    CXs = {seg: CXb for seg in own}
    t_CX = {seg: t_CXb for seg in own}

    if upto == "c1":
        b.barrier()
        return nc, b
    Hx = b.alloc([128, 16, 2, 2, NCH], BF, "Hx")
    t_Hx = T("Hx")
    wo2 = [b.alloc([128, 2, 2, 4, 128], BF, "wo2_%d" % i) for i in range(2)]
    wt2 = [b.alloc([128, 8, 128], BF, "wt2_0")] * 2
    t_wo2 = [T("wo2_0"), T("wo2_1")]
    t_wt2 = [T("wt2_0")] * 2
    ytt = [b.alloc([128, 8, 128], BF, "ytt0")] * 2
    t_ytt = [T("ytt0")] * 2
    yc = 0
    oc = 0
    S6 = S[:, :, :, :, 0:NCH].rearrange("p g d r (a c) -> p g d r a c", c=16)
    Hx6 = Hx[:].rearrange("p g d r (a c) -> p g d r a c", c=16)
    for seg in own:
        src, tsrc = seg_source(seg)
        build_U8(src, tsrc, 0 if seg < 2 else 1)
        make_CX(seg)
        CX = CXs[seg]
        for half in range(2):
            gqs = slice(half * 16, half * 16 + 16)
            state_mm(half)
            for d, c0 in ((0, 0), (1, 15)):
                eng = SENG[d]
                lrb = L8[:, 0, d, gqs].unsqueeze(2).to_broadcast([128, 16, NB])
                lib = L8[:, 1, d, gqs].unsqueeze(2).to_broadcast([128, 16, NB])
                cmad(eng, S6[:, :, d, 0, :, c0], S6[:, :, d, 1, :, c0], lrb, lib, CX[:, gqs, d, 0, :], CX[:, gqs, d, 1, :],
                     tmpB[d][:], [t_CX[seg], t_Lt, t_tm[d]], [t_S[d], t_tm[d]])
            c0_scan(half)
            for ri in range(2):
                cp("act", Hx6[:, :, 0, ri, :, 1:16], S6[:, :, 0, ri, :, 0:15], [t_S[0]], [t_Hx])
                cp("act", Hx6[:, :, 1, ri, :, 0:15], S6[:, :, 1, ri, :, 1:16], [t_S[1]], [t_Hx])
            cp("act", Hx6[:, :, 0, :, :, 0], CX[:, gqs, 0, :, :], [t_CX[seg]], [t_Hx])
            cp("act", Hx6[:, :, 1, :, :, 15], CX[:, gqs, 1, :, :], [t_CX[seg]], [t_Hx])
            for sb_ in range(4):
                gb = half * 4 + sb_
                wo_, two_, wt_, twt_ = wo2[oc % 2], t_wo2[oc % 2], wt2[oc % 2], t_wt2[oc % 2]
                oc += 1
                b.dma(wo_[:].rearrange("p r d q n -> p (r d q n)"), woutD[gb], reads=[t_wD], writes=[two_])
                b.dma(wt_[:].rearrange("p g n -> p (g n)"), wtoepD[gb], reads=[t_wD], writes=[twt_])
                for ct in range(NCT):
                    cs = slice(ct * 128, (ct + 1) * 128)
                    yt_, tyt_ = ytt[yc % 2], t_ytt[yc % 2]
                    yc += 1
                    for hb_ in range(2):
                        bk = 4 + hb_

                        def mmy(e, hb_=hb_, bk=bk, gb=gb, sb_=sb_, wo_=wo_, wt_=wt_, cs=cs):
                            for gg in range(4):
                                gl = hb_ * 4 + gg
                                g2, glq = gl % 2, gl // 2
                                g = 8 * gb + gl
                                gq = sb_ * 4 + glq
                                o = banks[bk][:, gg * 128:(gg + 1) * 128]
                                e.matmul(o, lhsT=U8[:, g, cs], rhs=wt_[:, gl, :], start=True, stop=False)
                                for d in range(2):
                                    for ri in range(2):
                                        ins = e.matmul(o, lhsT=Hx[64 * g2:64 * g2 + 64, gq, d, ri, cs],
                                                       rhs=wo_[64 * g2:64 * g2 + 64, ri, d, glq, :],
                                                       start=False, stop=(d == 1 and ri == 1))
                            return ins
                        b.op("pe", mmy, reads=[t_U8, t_Hx, two_, twt_], writes=[t_bank[bk]])
                        b.op("act", lambda e, hb_=hb_, bk=bk, yt_=yt_: e.activation(
                            out=yt_[:, :, hb_ * 64:(hb_ + 1) * 64].rearrange("c s (g i) -> c g s i", i=16),
                            in_=banks[bk][:, :].rearrange("c (g s i) -> c g s i", g=4, s=8), func=AF.Gelu),
                            reads=[t_bank[bk]], writes=[tyt_])
                    b.dma(ytd[seg][ct * 1024:(ct + 1) * 1024, gb * 128:(gb + 1) * 128].rearrange("(c s) n -> c s n", s=8),
                          yt_[:], reads=[tyt_], writes=[t_yt[seg]])

    if upto == "p2":
        b.barrier()
        return nc, b
    b.barrier()
    b.sb = s5_mark
    wg = b.alloc([128, KC, 2 * D], BF, "wg")
    t_wg = T("wg")
    mkg = b.sb
    wgs = [b.alloc([128, 2 * D], F32, "wgs%d" % i) for i in range(2)]
    t_wgs = [T("wgs0"), T("wgs1")]
    wgv = wglu_in.rearrange("(kc p) n -> p kc n", p=128)
    for kc in range(KC):
        b.dma(wgs[kc % 2][:], wgv[:, kc, :], writes=[t_wgs[kc % 2]])
        cp("pool" if kc % 2 else "dve", wg[:, kc, :], wgs[kc % 2][:], [t_wgs[kc % 2]], [t_wg])
    b.barrier()
    b.sb = mkg
    ytc = b.alloc([128, 8, D], BF, "ytc")
    x2c = b.alloc([128, 8, D], F32, "x2c")
    gT = b.alloc([128, KC, 128], BF, "gT")
    sg = b.alloc([128, D], F32, "sg")
    yv = b.alloc([128, D], F32, "yv")
    tmpg = b.alloc([128, D], F32, "tmpg")
    og = [b.alloc([128, D], F32, "og%d" % i) for i in range(2)]
    ssg = b.alloc([128, 4], F32, "ssg")
    t_ytc, t_x2c, t_gT, t_sg, t_yv, t_tmpg, t_ssg = T("ytc"), T("x2c"), T("gT"), T("sg"), T("yv"), T("tmpg"), T("ssg")
    t_og = [T("og0"), T("og1")]
    mkg2 = b.sb
    ogc = 0
    for seg in own:
        q = 0 if seg < 2 else 1
        if seg in (0, 2):
            if seg == 2:
                b.barrier()
            b.sb = mkg2
            growm, t_growm = load_row(q, 1, 2, "growm")
        for ct in range(NCT):
            rows = slice(ct * 1024, (ct + 1) * 1024)
            b.dma(ytc[:], ytd[seg][rows, :].rearrange("(c s) n -> c s n", s=8), reads=[t_yt[seg]], writes=[t_ytc])
            b.dma(x2c[:], x2d[seg][rows, :].rearrange("(c s) n -> c s n", s=8), reads=[t_x2[seg]], writes=[t_x2c])
            for s_ in range(8):
                def trg(e, s_=s_):
                    for kc in range(KC):
                        ins = e.transpose(tp_ps[:, kc, :], ytc[:, s_, kc * 128:(kc + 1) * 128], ident[:])
                    return ins
                b.op("pe", trg, reads=[t_ytc, t_ident], writes=[t_tp])
                cp("act", gT[:], tp_ps[:, :, :], [t_tp], [t_gT])

                def mmg(e):
                    for n4 in range(4):
                        for kc in range(KC):
                            ins = e.matmul(banks[n4][:, :], lhsT=gT[:, kc, :], rhs=wg[:, kc, n4 * 512:(n4 + 1) * 512],
                                           start=(kc == 0), stop=(kc == KC - 1))
                    return ins
                b.op("pe", mmg, reads=[t_gT, t_wg], writes=t_bank[0:4])
                for h2 in range(2):
                    b.op("act", lambda e, h2=h2: e.activation(out=sg[:, h2 * 512:(h2 + 1) * 512], in_=banks[2 + h2][:, :],
                                                              func=AF.Sigmoid), reads=[t_bank[2 + h2]], writes=[t_sg])
                    tt("dve", yv[:, h2 * 512:(h2 + 1) * 512], banks[h2][:, :], sg[:, h2 * 512:(h2 + 1) * 512], ALU.mult,
                       [t_bank[h2], t_sg], [t_yv])
                rstd_of([yv[:, :]], [t_yv], ssg, t_ssg)
                b.op("dve", lambda e: e.scalar_tensor_tensor(out=tmpg[:], in0=yv[:], scalar=ssg[:, 0:1], in1=growm[:],
                                                             op0=ALU.mult, op1=ALU.mult),
                     reads=[t_yv, t_ssg, t_growm], writes=[t_tmpg])
                o, to = og[ogc % 2], t_og[ogc % 2]
                ogc += 1
                tt("pool", o[:], x2c[:, s_, :], tmpg[:], ALU.add, [t_x2c, t_tmpg], [to])
                b.dma(x3d[seg][rows, :].rearrange("(c s) d -> c s d", s=8)[:, s_, :], o[:], reads=[to], writes=[t_x3[seg]])

    if debug:
        b.barrier()
        return nc, b
    t_out = [T("o0"), T("o1"), T("o2")]
    ffn_phase(1, [x3d[i] for i in range(3)], t_x3, [yp[0:TS, :], yp[TS:2 * TS, :], ys[:, :]], t_out)
    b.barrier()
    return nc, b


def make_inputs(inp, core, RSEG):
    TS = RSEG * W
    f = np.float32
    xp = np.zeros(((2 * RSEG + 8) * W, D), f)
    xp[256:256 + 2 * TS] = inp["x_prompt"][core]
    xs = np.zeros(((RSEG + 8) * W, D), f)
    Rs = 8 * RSEG
    g0s = core * RSEG
    lo, hi = max(0, g0s - 4), min(Rs, g0s + RSEG + 4)
    xs[(lo - (g0s - 4)) * W:(hi - (g0s - 4)) * W] = inp["x_sample"][0][lo * W:hi * W]
    cvec = np.stack([inp["c_prompt"][core], inp["c_sample"][0]]).astype(f)
    p = np.arange(128)
    kr2, kc = p // 64, p % 64
    mm = np.arange(14)
    qc = np.arange(64)
    dr = kr2[:, None, None] + 13 - mm[None, :, None] + 0 * qc[None, None, :]
    dc = np.clip(kc[:, None, None] - qc[None, None, :], -15, 15) + 15 + 0 * mm[None, :, None]
    rpbY = inp["attn_rpb"][0][:, dr, dc].reshape(NH, 128, 14 * 64).astype(f)
    cs = np.clip(qc - 8, 0, 48)
    cvm = ((kc[:, None] >= cs[None, :]) & (kc[:, None] < cs[None, :] + 16)).astype(f)
    mmi = np.arange(14)
    bandm = ((mmi[None, :] >= kr2[:, None] + 3) & (mmi[None, :] <= kr2[:, None] + 10)).astype(f)
    NQB = RSEG // 4
    xsa = np.zeros(((8 * RSEG + 8) * W, D), f)
    xsa[256:256 + 8 * TS] = inp["x_sample"][0]
    segs = [(0, 2 * RSEG), (RSEG, 2 * RSEG), (g0s, Rs)] + [(j * RSEG, Rs) for j in range(8)]
    rvx = np.zeros((11, 128, NQB, 6, 4), f)
    for seg, (g0, R) in enumerate(segs):
        for qb in range(NQB):
            for kk in range(6):
                k = 5 - kk
                for qr in range(4):
                    qg = g0 + 4 * qb + qr
                    ws = min(max(qg - 4, 0), R - 8)
                    for k2 in range(2):
                        kg = g0 - 4 + 4 * qb + 2 * k + k2
                        ok = (0 <= kg < R) and (ws <= kg < ws + 8)
                        rvx[seg, k2 * 64:(k2 + 1) * 64, qb, kk, qr] = 1.0 if ok else 0.0
    sidx = np.arange(128) // 16
    mfb = np.stack([(sidx[:, None] <= sidx[None, :]), (sidx[:, None] >= sidx[None, :])]).astype(f)
    sel = np.zeros((128, 8), f)
    sel[:, core] = 1.0
    return dict(xp=xp, xs=xs, xsa=xsa, cvec=cvec, ada_w=inp["ada_w"], ada_b=inp["ada_b"], norm_gain=inp["norm_gain"],
                w_qkv=inp["attn_w_qkv"][0], w_o=inp["attn_w_o"][0], rpbY=rpbY, cvm=cvm, bandm=bandm,
                rvx=rvx.reshape(11, 128, NQB * 24), ident=np.eye(128, dtype=f),
                ffn_w1=inp["ffn_w1"], ffn_w2=inp["ffn_w2"],
                s5_a_re=inp["s5_a_re"][0], s5_a_im=inp["s5_a_im"][0], s5_log_dt=inp["s5_log_dt"][0],
                s5_b_re=inp["s5_b_re"][0], s5_b_im=inp["s5_b_im"][0], s5_c_re=inp["s5_c_re"][0], s5_c_im=inp["s5_c_im"][0],
                s5_d=inp["s5_d"][0], s5_w_glu=inp["s5_w_glu"][0], mfb=mfb, sel=sel)


_CACHE = {}


def kernel(**inputs):
    RSEG = 32
    inp = {k: np.asarray(v) for k, v in inputs.items()}
    if "nc" not in _CACHE:
        nc, b = build(RSEG)
        b.emit()
        _CACHE["nc"] = nc
    nc = _CACHE["nc"]
    maps = [make_inputs(inp, core, RSEG) for core in range(8)]
    res = run_bass_kernel_spmd(nc, maps, core_ids=list(range(8)))
    TS = RSEG * W
    y_prompt = np.stack([res.results[c]["yp"] for c in range(8)]).astype(np.float32)
    y_sample = np.concatenate([res.results[c]["ys"] for c in range(8)], axis=0)[None].astype(np.float32)
    return (y_prompt, y_sample)
```
